# Optimizing a Trainium2 kernel written in Bass

```python
import jax, jax.numpy as jnp
from jax import lax
import numpy as np

D_MODEL = 1024
BATCH = 8
SEQ = 2048
DEPTH = 1

ATTN_HEADS = 8
HEAD_DIM = 64
ATTN_WIDTH = ATTN_HEADS * HEAD_DIM
MOBA_BLOCK = 256
MOBA_TOPK = 3
ROPE_THETA = 500000.0
ROPE_DIM = HEAD_DIM // 4
QUERY_CHUNK = 16
POOL_WINDOWS = (2, 4, 8, 16)
POOL_GROUPS = len(POOL_WINDOWS)
POOL_GROUP_WIDTH = 128
POOL_WIDTH = POOL_GROUPS * POOL_GROUP_WIDTH
D_FF = 4 * D_MODEL
IN_WIDTH = 3 * ATTN_WIDTH + POOL_WIDTH + 2 * D_MODEL
RMS_EPS = 1e-6

kernel_name = "hybrid_moba_multiscale_pool_gated_block"


def rms_norm(x, g):
    xf = x.astype(jnp.float32)
    y = xf * lax.rsqrt(jnp.mean(xf * xf, axis=-1, keepdims=True) + RMS_EPS)
    return (y * g.astype(jnp.float32)).astype(x.dtype)


def partial_rope(x, pos):
    half = ROPE_DIM // 2
    inv_freq = ROPE_THETA ** (-jnp.arange(half, dtype=jnp.float32) / half)
    ang = pos.astype(jnp.float32)[:, None] * inv_freq[None, :]
    cos = jnp.cos(ang)[None, :, None, :]
    sin = jnp.sin(ang)[None, :, None, :]
    xr = x[..., :ROPE_DIM].astype(jnp.float32)
    x1, x2 = xr[..., :half], xr[..., half:]
    rot = jnp.concatenate([x1 * cos - x2 * sin, x2 * cos + x1 * sin], axis=-1)
    return jnp.concatenate([rot.astype(x.dtype), x[..., ROPE_DIM:]], axis=-1)


def moba_attention(q, k, v):
    B, H, S, Dh = q.shape
    nb = -(-S // MOBA_BLOCK)
    pad = nb * MOBA_BLOCK - S
    kb = jnp.pad(k, ((0, 0), (0, 0), (0, pad), (0, 0))).reshape(B, H, nb, MOBA_BLOCK, Dh)
    vb = jnp.pad(v, ((0, 0), (0, 0), (0, pad), (0, 0))).reshape(B, H, nb, MOBA_BLOCK, Dh)
    counts = jnp.clip(S - jnp.arange(nb) * MOBA_BLOCK, 1, MOBA_BLOCK).astype(jnp.float32)
    k_mean = jnp.sum(kb.astype(jnp.float32), axis=3) / counts[:, None]
    gate = jnp.einsum('bhsd,bhnd->bhsn', q.astype(jnp.float32), k_mean)
    q_blk = jnp.arange(S) // MOBA_BLOCK
    fully_past = jnp.arange(nb)[None, :] < q_blk[:, None]
    gate = jnp.where(fully_past, gate, -jnp.inf)
    n_sel = min(MOBA_TOPK, nb)
    _, top_idx = lax.top_k(gate, n_sel)
    top_ok = top_idx < q_blk[:, None]

    scale = Dh ** -0.5
    n_chunks = S // QUERY_CHUNK
    bi = jnp.arange(B)[:, None, None, None]
    hi = jnp.arange(H)[None, :, None, None]

    def chunk(c):
        s0 = c * QUERY_CHUNK
        qc = lax.dynamic_slice_in_dim(q, s0, QUERY_CHUNK, axis=2)
        idx = lax.dynamic_slice_in_dim(top_idx, s0, QUERY_CHUNK, axis=2)
        ok = lax.dynamic_slice_in_dim(top_ok, s0, QUERY_CHUNK, axis=2)
        blk = s0 // MOBA_BLOCK
        k_own = lax.dynamic_index_in_dim(kb, blk, axis=2, keepdims=False)
        v_own = lax.dynamic_index_in_dim(vb, blk, axis=2, keepdims=False)
        k_sel = kb[bi, hi, idx]
        v_sel = vb[bi, hi, idx]
        s_own = jnp.einsum('bhqd,bhkd->bhqk', qc, k_own,
                           preferred_element_type=jnp.float32) * scale
        qpos = s0 + jnp.arange(QUERY_CHUNK)
        kpos = blk * MOBA_BLOCK + jnp.arange(MOBA_BLOCK)
        s_own = jnp.where(kpos[None, :] <= qpos[:, None], s_own, -jnp.inf)
        s_sel = jnp.einsum('bhqd,bhqnkd->bhqnk', qc, k_sel,
                           preferred_element_type=jnp.float32) * scale
        s_sel = jnp.where(ok[..., None], s_sel, -jnp.inf)
        s_sel = s_sel.reshape(B, H, QUERY_CHUNK, n_sel * MOBA_BLOCK)
        p = jax.nn.softmax(jnp.concatenate([s_own, s_sel], axis=-1), axis=-1).astype(v.dtype)
        p_own = p[..., :MOBA_BLOCK]
        p_sel = p[..., MOBA_BLOCK:].reshape(B, H, QUERY_CHUNK, n_sel, MOBA_BLOCK)
        o = (jnp.einsum('bhqk,bhkd->bhqd', p_own, v_own, preferred_element_type=jnp.float32)
             + jnp.einsum('bhqnk,bhqnkd->bhqd', p_sel, v_sel, preferred_element_type=jnp.float32))
        return o.astype(q.dtype)

    out = lax.map(chunk, jnp.arange(n_chunks))
    return out.transpose(1, 2, 0, 3, 4).reshape(B, H, S, Dh)


def multiscale_pool(u, w_grp, chan_scale):
    B, S, _ = u.shape
    uf = u.astype(jnp.float32).reshape(B, S, POOL_GROUPS, POOL_GROUP_WIDTH)
    csum = jnp.cumsum(uf, axis=1)
    t = jnp.arange(S)
    pooled = []
    for g, w in enumerate(POOL_WINDOWS):
        c = csum[:, :, g]
        c_lag = jnp.pad(c[:, :S - w], ((0, 0), (w, 0), (0, 0)))
        cnt = jnp.minimum(t + 1, w).astype(jnp.float32)
        pooled.append((c - c_lag) / cnt[None, :, None])
    pooled = jnp.stack(pooled, axis=2) - uf
    y = jnp.einsum('bsgc,gcd->bsgd', pooled, w_grp.astype(jnp.float32))
    y = y.reshape(B, S, POOL_WIDTH) * chan_scale.astype(jnp.float32)
    return y.astype(u.dtype)


def setup_inputs(seed: int = 0) -> dict:
    key = jax.random.key(seed)
    ks = jax.random.split(key, 14)
    f32 = jnp.float32

    def nrm(k, shape, fan_in):
        return jax.random.normal(k, shape, f32) * (fan_in ** -0.5)

    x = jax.random.normal(ks[0], (BATCH, SEQ, D_MODEL), f32)
    norm_mix = 1.0 + 0.02 * jax.random.normal(ks[1], (DEPTH, D_MODEL), f32)
    w_in = nrm(ks[2], (DEPTH, D_MODEL, IN_WIDTH), D_MODEL)
    q_norm = 1.0 + 0.02 * jax.random.normal(ks[3], (DEPTH, HEAD_DIM), f32)
    k_norm = 1.0 + 0.02 * jax.random.normal(ks[4], (DEPTH, HEAD_DIM), f32)
    w_pool_grp = nrm(ks[5], (DEPTH, POOL_GROUPS, POOL_GROUP_WIDTH, POOL_GROUP_WIDTH), POOL_GROUP_WIDTH)
    pool_scale = 1.0 + 0.02 * jax.random.normal(ks[6], (DEPTH, POOL_WIDTH), f32)
    w_up_attn = nrm(ks[7], (DEPTH, ATTN_WIDTH, D_MODEL), ATTN_WIDTH)
    w_up_pool = nrm(ks[8], (DEPTH, POOL_WIDTH, D_MODEL), POOL_WIDTH)
    w_out = nrm(ks[9], (DEPTH, D_MODEL, D_MODEL), D_MODEL)
    norm_mlp = 1.0 + 0.02 * jax.random.normal(ks[10], (DEPTH, D_MODEL), f32)
    w_ff1 = nrm(ks[11], (DEPTH, D_MODEL, D_FF), D_MODEL)
    w_ff2 = nrm(ks[12], (DEPTH, D_FF, D_MODEL), D_FF)
    return {"x": x, "norm_mix": norm_mix, "w_in": w_in, "q_norm": q_norm, "k_norm": k_norm,
            "w_pool_grp": w_pool_grp, "pool_scale": pool_scale, "w_up_attn": w_up_attn,
            "w_up_pool": w_up_pool, "w_out": w_out, "norm_mlp": norm_mlp,
            "w_ff1": w_ff1, "w_ff2": w_ff2}


def reference(x, norm_mix, w_in, q_norm, k_norm, w_pool_grp, pool_scale, w_up_attn,
              w_up_pool, w_out, norm_mlp, w_ff1, w_ff2):
    B, S, _ = x.shape
    pos = jnp.arange(S)
    splits = [ATTN_WIDTH, 2 * ATTN_WIDTH, 3 * ATTN_WIDTH,
              3 * ATTN_WIDTH + POOL_WIDTH, 3 * ATTN_WIDTH + POOL_WIDTH + D_MODEL]
    for l in range(DEPTH):
        h = rms_norm(x, norm_mix[l])
        proj = h @ w_in[l]
        q, k, v, u, g_a, g_p = jnp.split(proj, splits, axis=-1)
        q = partial_rope(rms_norm(q.reshape(B, S, ATTN_HEADS, HEAD_DIM), q_norm[l]), pos)
        k = partial_rope(rms_norm(k.reshape(B, S, ATTN_HEADS, HEAD_DIM), k_norm[l]), pos)
        v = v.reshape(B, S, ATTN_HEADS, HEAD_DIM)
        o_attn = moba_attention(q.transpose(0, 2, 1, 3), k.transpose(0, 2, 1, 3),
                                v.transpose(0, 2, 1, 3))
        o_attn = o_attn.transpose(0, 2, 1, 3).reshape(B, S, ATTN_WIDTH)
        o_pool = multiscale_pool(u, w_pool_grp[l], pool_scale[l])
        gate_a = jax.nn.sigmoid(g_a.astype(jnp.float32))
        gate_p = jax.nn.sigmoid(g_p.astype(jnp.float32))
        merged = (gate_a * (o_attn @ w_up_attn[l]).astype(jnp.float32)
                  + gate_p * (o_pool @ w_up_pool[l]).astype(jnp.float32)).astype(x.dtype)
        x = x + merged @ w_out[l]
        h2 = rms_norm(x, norm_mlp[l])
        x = x + jnp.square(jax.nn.relu(h2 @ w_ff1[l])) @ w_ff2[l]
    return x
```

```python
import numpy as np
from contextlib import ExitStack

import concourse.bass as bass
import concourse.mybir as mybir
from concourse.bass_utils import run_bass_kernel_spmd

F32 = mybir.dt.float32
BF16 = mybir.dt.bfloat16
AF = mybir.ActivationFunctionType
ALU = mybir.AluOpType
AX = mybir.AxisListType

S = 2048
D = 1024
NT = 16
H = 8
DH = 64
NB = 8
NEG = -240000.0
EPS = 1e-6
N_CORES = 8


class Buf:
    __slots__ = ("name", "w", "r", "fence", "excl")

    def __init__(self, name, fence=None, excl=False):
        self.name = name
        self.excl = excl
        self.w = None
        self.r = {}
        self.fence = dict(fence) if fence else None


class Trk:
    def __init__(self, nc, es, n_ring=12):
        self.nc = nc
        self.E = {"pe": nc.tensor, "act": nc.scalar, "dve": nc.vector,
                  "pool": nc.gpsimd, "sp": nc.sync}
        self.sems = {}
        for k in self.E:
            self.sems[k] = es.enter_context(nc.semaphore("s_" + k))
        self.cnt = {k: 0 for k in self.E}
        self.seen = {k: {} for k in self.E}
        self.rings = {}
        for q in ("sp", "pool"):
            slots = []
            for j in range(n_ring):
                key = "d_%s%d" % (q, j)
                self.sems[key] = es.enter_context(nc.semaphore(key))
                slots.append([key, 0])
            self.rings[q] = {"i": 0, "slots": slots}

    def snapshot(self):
        snap = {k: v for k, v in self.cnt.items() if v > 0}
        for q in self.rings.values():
            for key, tot in q["slots"]:
                if tot > 0:
                    snap[key] = tot
        return snap

    @staticmethod
    def _add(need, k, v):
        if need.get(k, 0) < v:
            need[k] = v

    def _need(self, reads, writes, e=None):
        need = {}
        for b in reads:
            if b.w:
                self._add(need, *b.w)
            if b.excl:
                for k, v in b.r.items():
                    if k != e:
                        self._add(need, k, v)
            if b.fence:
                for k, v in b.fence.items():
                    self._add(need, k, v)
        for b in writes:
            if b.w:
                self._add(need, *b.w)
            for k, v in b.r.items():
                self._add(need, k, v)
            if b.fence:
                for k, v in b.fence.items():
                    self._add(need, k, v)
        return need

    def _wait(self, e, need):
        eng = self.E[e]
        seen = self.seen[e]
        for k, v in need.items():
            if e == "pe" and k == "pe":
                continue
            if seen.get(k, 0) < v:
                eng.wait_ge(self.sems[k], v)
                seen[k] = v

    def _mark(self, key, val, reads, writes):
        for b in reads:
            if b.r.get(key, 0) < val:
                b.r[key] = val
        for b in writes:
            b.w = (key, val)
            b.r = {}

    def op(self, e, fn, reads=(), writes=()):
        self._wait(e, self._need(reads, writes, e))
        ins = fn(self.E[e])
        self.cnt[e] += 1
        ins.then_inc(self.sems[e], 1)
        self._mark(e, self.cnt[e], reads, writes)

    def group(self, e, fns, reads=(), writes=()):
        self._wait(e, self._need(reads, writes, e))
        ins = None
        for fn in fns:
            ins = fn(self.E[e])
        self.cnt[e] += 1
        ins.then_inc(self.sems[e], 1)
        self._mark(e, self.cnt[e], reads, writes)

    def dma(self, q, out, in_, reads=(), writes=()):
        need = self._need(reads, writes)
        ring = self.rings[q]
        slot = ring["slots"][ring["i"] % len(ring["slots"])]
        ring["i"] += 1
        if slot[1] > 0:
            self._add(need, slot[0], slot[1])
        self._wait(q, need)
        ins = self.E[q].dma_start(out=out, in_=in_)
        slot[1] += 16
        ins.then_inc(self.sems[slot[0]], 16)
        self._mark(slot[0], slot[1], reads, writes)

    def wait_all(self, e):
        self._wait(e, self.snapshot())

    def wait_written(self, e, buf):
        if buf.w:
            self._wait(e, {buf.w[0]: buf.w[1]})


def build_nc(debug=(), stop=None):
    nc = bass.Bass("TRN2", target_bir_lowering=False)

    def din(name, shape):
        return nc.dram_tensor(name, list(shape), F32, kind="ExternalInput").ap()

    x_d = din("x", [S, D])
    w_in_d = din("w_in", [D, 4096])
    w_grp_d = din("w_pool_grp", [4, 128, 128])
    w_ua_d = din("w_up_attn", [512, D])
    w_up_d = din("w_up_pool", [512, D])
    w_out_d = din("w_out", [D, D])
    w_ff1_d = din("w_ff1", [D, 4096])
    w_ff2_d = din("w_ff2", [4096, D])
    c_small_d = din("c_small", [128, 24])
    gmlp_vec_d = din("gmlp_bc", [128, D])
    gmix_vec_d = din("gmix_bc", [128, D])
    c_gv_d = din("c_gv", [64])
    c_ident_d = din("c_ident", [128, 128])
    c_rope_d = din("c_rope", [128, NT * 32])
    c_band_d = din("c_band", [128, 12, 128])
    c_zero_d = din("c_zero", [64, 1024])
    c_ind_d = din("c_ind", [8, S])
    out_d = nc.dram_tensor("out", [S, D], F32, kind="ExternalOutput").ap()
    dbg_d = {}

    es = ExitStack()
    with es:
        RAWB = 103 * 2048 + 256
        raw = nc.alloc_sbuf_tensor("raw", [128, RAWB // 2], BF16)

        def view(off, shape, dt, p0=0):
            esz = 2 if dt == BF16 else 4
            n = 1
            for s_ in shape[1:]:
                n *= s_
            assert off % 4 == 0 and off + n * esz <= RAWB, (off, shape)
            v = raw[p0:p0 + shape[0], off // 2: off // 2 + n * esz // 2]
            if dt != BF16:
                v = v.bitcast(dt)
            if len(shape) == 3:
                v = v.rearrange("p (a b) -> p a b", a=shape[1])
            elif len(shape) == 4:
                v = v.rearrange("p (a b c) -> p a b c", a=shape[1], b=shape[2])
            return v

        KB = 1024
        A0 = 0
        B0 = 7 * KB
        C0 = B0 + 32 * KB
        D0 = C0 + 64 * KB + 256
        E0 = D0 + 32 * KB
        F0 = E0 + 32 * KB
        G0 = F0 + 32 * KB

        ident_f = view(A0 + 0, [128, 128], F32)
        ident_b = view(A0 + 512, [128, 128], BF16)
        tri01 = view(A0 + 768, [128, 128], BF16)
        rope_t = view(A0 + 1024, [128, NT, 32], F32)
        small_c = view(A0 + 3072, [128, 24], F32)
        gmix_c = small_c[:, 0:8]
        gmlp_c = small_c[:, 8:16]
        psc_c = small_c[:, 16:20]
        gcolT_q = small_c[:, 20:21]
        gcolT_k = small_c[:, 21:22]
        gv_qk = view(A0 + 3200, [128, 64], F32)
        gv_q = gv_qk[:, 0:32]
        gv_k = gv_qk[:, 32:64]
        ones_b = view(A0 + 3664, [128, 2], BF16)
        band = view(A0 + 3680, [128, 12, 128], BF16)
        wgrp = view(G0 + 0, [128, 4, 128], BF16)
        ss1 = view(G0 + 1024, [128, NT], F32)
        rstd1 = view(G0 + 1088, [128, NT], F32)
        ss2 = view(G0 + 1152, [128, NT], F32)
        r2sq = view(G0 + 1216, [128, NT], F32)
        kmeanT = view(G0 + 1280, [128, 4, 8], BF16)
        ksum_sb = view(G0 + 1344, [128, 4, 8], F32)
        stat_a = [view(G0 + 5632 + i * 64, [128, 8], F32) for i in range(2)]
        stat_b = [view(G0 + 5760 + i * 64, [128, 8], F32) for i in range(2)]
        junk_h = view(G0 + 5888, [128, 512], BF16)
        ss2h = view(G0 + 6912, [128, 2 * NT], F32)

        hT = view(B0, [128, 8, S], BF16)
        qTz = view(C0, [128, H, S], BF16)
        kT = view(C0 + 32 * KB, [128, 4, S], BF16)
        v_aug = view(C0 + 48 * KB, [128, NT, H, 65], BF16)
        assert C0 + 48 * KB + 16640 <= D0
        x1 = view(C0, [128, NT, D], F32)
        _sb = [F0, C0]
        wga = [view(_sb[s], [128, 8, 128], BF16) for s in range(2)]
        wgp = [view(_sb[s] + 2 * KB, [128, 8, 128], BF16) for s in range(2)]
        wua = [view(_sb[s] + 4 * KB, [128, 4, 128], BF16) for s in range(2)]
        wup = [view(_sb[s] + 5 * KB, [128, 4, 128], BF16) for s in range(2)]
        ga_sb = [view(C0 + 6 * KB + s * 4 * KB, [128, 512], F32) for s in range(2)]
        gp_sb = [view(C0 + 6 * KB + s * 4 * KB + 2 * KB, [128, 512], F32) for s in range(2)]
        xslot = [view(D0 + i * 4 * KB, [128, D], F32) for i in range(8)]
        u_sb = view(D0, [128, NT, 512], BF16)
        pooledT = view(D0 + 16 * KB, [128, 4, S], BF16)
        mT = view(D0, [128, 8, S], BF16)
        a1T = [view(D0 + s * 8 * KB, [128, 8, 512], BF16) for s in range(2)]
        sqz = [view(D0 + 16 * KB + s * 2 * KB, [128, 512], F32) for s in range(2)]
        o_attnT = view(E0, [128, 4, S], BF16)
        junk = view(E0, [128, D], BF16)
        gmix_bc = view(E0 + 2 * KB, [128, D], F32)
        hb0 = [view(E0 + 6 * KB + i * 2 * KB, [128, D], BF16) for i in range(2)]
        E1 = E0 + 16 * KB
        sq_sb = [view(E1 + i * 2 * KB, [128, 512], F32) for i in range(2)]
        qtok = [view(E1 + 4 * KB + i * KB, [128, 512], BF16) for i in range(2)]
        x16 = [view(E1 + 6 * KB + i * 512, [128, 8, 16], F32) for i in range(2)]
        rt1 = [view(E1 + 7 * KB + i * 512, [128, 8, 16], F32) for i in range(2)]
        rt2 = [view(E1 + 8 * KB + i * 512, [128, 8, 16], F32) for i in range(2)]
        rope_q = view(E1 + 9 * KB, [128, NT, 32], F32)
        rope_k = view(E1 + 11 * KB, [128, NT, 32], F32)
        yT = view(E0 + 16 * KB, [128, 4, S], BF16)
        wqkvu = [view(F0 + i * 8 * KB, [128, 8, 512], BF16) for i in range(4)]
        wout = view(F0 + 8 * KB, [128, 8, D], BF16)
        kT2 = view(F0 + 8 * KB, [128, 4, S], BF16)
        xt5 = [view(F0 + 24 * KB + i * 4 * KB, [128, D], F32) for i in range(2)]
        gmlp_bc = view(F0, [128, D], F32)
        hb16 = [view(F0 + 4 * KB + i * 2 * KB, [128, D], BF16) for i in range(2)]
        f2 = F0 + 24 * KB
        PT = [view(f2 + i * 1024, [128, 512], BF16) for i in range(5)]
        o_tok = [view(f2 + 5120 + i * 1024, [128, 512], BF16) for i in range(2)]
        rden = [view(G0 + 7040 + i * 16, [128, 4], F32) for i in range(4)]
        assert f2 + 7168 <= F0 + 31 * KB
        gate_sb = view(F0 + 31 * KB, [128, 8, 8], F32)
        m8_sb = view(F0 + 31 * KB + 256, [128, 8, 8], F32)
        sel_sb = view(F0 + 31 * KB + 512, [128, 8, 8], F32)
        bias8 = [view(F0 + 6 * KB, [128, H, 128], BF16), view(D0 + 16 * KB, [128, H, 128], BF16)]
        rank_sb = view(D0 + 18 * KB, [128, H, 8, 8], F32)
        ffw1 = [view(E0, [128, 8, 1024], BF16), view(F0, [128, 8, 1024], BF16)]
        ffw2 = [view(E0 + 16 * KB, [128, 8, 1024], BF16), view(F0 + 16 * KB, [128, 8, 1024], BF16)]

        banks = [es.enter_context(nc.psum_tensor("bank%d" % i, [128, 512], F32)) for i in range(8)]
        hb = []
        for i in range(8):
            pb_ = Buf("psbank%d" % i, excl=True)
            hb.append([pb_, pb_])

        def bank_f(i):
            return banks[i][:, :]

        def bank_bf(i):
            return banks[i][:, :].bitcast(BF16)

        t = Trk(nc, es)

        def dbg_dump(name, ap, shape, bufs):
            if name not in debug:
                return
            d = nc.dram_tensor("dbg_" + name, list(shape), ap.dtype, kind="ExternalOutput").ap()
            dbg_d[name] = d
            t.dma("sp", d, ap, reads=bufs)

        b_wqkvu = [Buf("wqkvu%d" % i) for i in range(4)]
        w_in_v = w_in_d.rearrange("(c p) n -> p c n", p=128)
        CG_K, CG_Q, CG_V, CG_U = 1, 0, 2, 3
        b_c = {n: Buf(n) for n in ("ident_f", "ident_b", "rope", "gvq", "gvk", "gmix", "gmlp",
                                    "psc", "ones", "band", "wgrp", "gcq", "gck", "ropeq", "ropek",
                                    "qz")}
        t.dma("sp", ident_f, c_ident_d, writes=[b_c["ident_f"]])
        t.dma("sp", small_c, c_small_d, writes=[b_c["gmix"], b_c["gmlp"], b_c["psc"], b_c["gcq"], b_c["gck"]])
        t.dma("sp", gv_qk, c_gv_d.partition_broadcast(128), writes=[b_c["gvq"], b_c["gvk"]])
        t.dma("sp", rope_t, c_rope_d.rearrange("p (t n) -> p t n", t=NT), writes=[b_c["rope"]])
        t.dma("pool", wqkvu[CG_K], w_in_v[:, :, CG_K * 512:(CG_K + 1) * 512], writes=[b_wqkvu[CG_K]])
        t.dma("pool", ident_b, c_ident_d, writes=[b_c["ident_b"]])

        def setup_gains(gv, gc, rp, nm_gv, nm_gc, nm_rp):
            t.op("dve", lambda e: e.memset(gc[0:16, :], 1.0), writes=[b_c[nm_gc]])
            t.op("dve", lambda e: e.memset(gc[64:80, :], 1.0), writes=[b_c[nm_gc]])
            t.op("dve", lambda e: e.tensor_tensor(
                out=rp, in0=rope_t, in1=gv.unsqueeze(1).to_broadcast([128, NT, 32]), op=ALU.mult),
                reads=[b_c["rope"], b_c[nm_gv]], writes=[b_c[nm_rp]])

        b_tri = Buf("tri01")
        b_eps = Buf("eps")
        eps_c = small_c[:, 22:23]

        def small_setup_k():
            t.op("dve", lambda e: e.memset(eps_c, EPS), reads=[b_c["gmix"]], writes=[b_eps])
            t.op("pool", lambda e: e.memset(tri01, 1.0), writes=[b_tri])
            t.op("pool", lambda e: e.affine_select(
                out=tri01, in_=tri01, pattern=[[1, 128]], compare_op=ALU.is_ge, fill=0.0, base=0,
                channel_multiplier=-1), reads=[b_tri], writes=[b_tri])
            t.op("dve", lambda e: e.memset(ones_b, 1.0), writes=[b_c["ones"]])
            setup_gains(gv_k, gcolT_k, rope_k, "gvk", "gck", "ropek")

        def small_setup_rest():
            t.wait_written("pool", b_xslot[7])
            t.dma("pool", wqkvu[CG_Q], w_in_v[:, :, CG_Q * 512:(CG_Q + 1) * 512], writes=[b_wqkvu[CG_Q]])
            setup_gains(gv_q, gcolT_q, rope_q, "gvq", "gcq", "ropeq")
            t.dma("pool", wqkvu[CG_U], w_in_v[:, :, CG_U * 512:(CG_U + 1) * 512], writes=[b_wqkvu[CG_U]])
            t.dma("pool", band, c_band_d, writes=[b_c["band"]])
            t.dma("pool", wgrp, w_grp_d.rearrange("g c d -> c g d"), writes=[b_c["wgrp"]])

        def zero_fill():
            for h in range(H):
                base = 64 if h % 2 == 0 else 0
                t.dma("sp", qTz[base:base + 64, h, :].bitcast(F32), c_zero_d, writes=[b_c["qz"]])

        b_xslot = [Buf("xslot%d" % i) for i in range(8)]
        b_ss1 = Buf("ss1")
        b_rstd1 = Buf("rstd1")
        b_junk = Buf("junk")
        b_hT = [Buf("hT_%d" % i) for i in range(NT)]
        ev_cnt = [0]
        tp_cnt = [0]

        b_gmix_bc = Buf("gmix_bc")
        b_hb0 = [Buf("hb0_%d" % i) for i in range(2)]
        p0_bank = {}

        def p0_load1(tt):
            sl = tt % 8
            t.dma("sp", xslot[sl], x_d[tt * 128:(tt + 1) * 128, :], writes=[b_xslot[sl]])

        def p0_sq(tt):
            sl = tt % 8
            t.op("act", lambda e: e.activation(
                out=junk, in_=xslot[sl], func=AF.Square, accum_out=ss1[:, tt:tt + 1]),
                reads=[b_xslot[sl]], writes=[b_junk, b_ss1])

        def p0_stats(tt):
            c0, c1 = tt, tt + 1
            t.op("act", lambda e: e.activation(out=rstd1[:, c0:c1], in_=ss1[:, c0:c1], func=AF.Sqrt,
                                               scale=1.0 / D, bias=eps_c[:, 0:1]),
                 reads=[b_ss1, b_eps], writes=[b_rstd1])
            t.op("dve", lambda e: e.reciprocal(out=rstd1[:, c0:c1], in_=rstd1[:, c0:c1]),
                 reads=[b_rstd1], writes=[b_rstd1])

        def p0_mul(tt):
            sl = tt % 8
            s2 = tt % 2
            t.op("dve", lambda e: e.scalar_tensor_tensor(
                out=hb0[s2], in0=xslot[sl], scalar=rstd1[:, tt:tt + 1], in1=gmix_bc,
                op0=ALU.mult, op1=ALU.mult),
                reads=[b_xslot[sl], b_rstd1, b_gmix_bc], writes=[b_hb0[s2]])

        def p0_tr1(tt):
            s2 = tt % 2
            bk = tp_cnt[0] % 3
            tp_cnt[0] += 1
            tps = bank_bf(bk)
            t.group("pe", [lambda e, c=c: e.transpose(
                out=tps[:, c * 128:(c + 1) * 128], in_=hb0[s2][:, c * 128:(c + 1) * 128],
                identity=ident_b) for c in range(8)],
                reads=[b_hb0[s2], b_c["ident_b"]], writes=hb[bk])
            t.op("act", lambda e: e.activation(
                out=hT[:, :, tt * 128:(tt + 1) * 128], in_=tps.rearrange("p (c t) -> p c t", c=8),
                func=AF.Copy),
                reads=hb[bk], writes=[b_hT[tt]])

        b_sq = [Buf("sq%d" % i) for i in range(2)]
        b_x16 = [Buf("x16%d" % i) for i in range(2)]
        b_qtok = [Buf("qtok%d" % i) for i in range(2)]
        b_rt1 = [Buf("rt1%d" % i) for i in range(2)]
        b_rt2 = [Buf("rt2%d" % i) for i in range(2)]
        b_sta = [Buf("sta%d" % i) for i in range(2)]
        b_stb = [Buf("stb%d" % i) for i in range(2)]
        b_qT = [Buf("qT%d" % i) for i in range(NT)]
        b_kT = [Buf("kT%d" % i) for i in range(NT)]
        b_v = [Buf("v%d" % i) for i in range(NT)]
        b_vones = Buf("vones")
        b_kmean = Buf("kmeanT")
        b_gate = Buf("gate")
        b_m8 = Buf("m8")
        b_sel = Buf("sel")
        b_rank = Buf("rank")
        b_btok2 = [Buf("bias8_%d" % i) for i in range(2)]
        b_ksum_ps = hb[3][0]
        b_gate_ps = hb[3][0]
        b_bias_ps = hb[3][0]
        ksum_ps = bank_f(3)[:, 0:128].rearrange("p (a b c) -> p a b c", a=4, b=NT)
        gate_ps = bank_f(3)[:, 128:192].rearrange("p (h n) -> p h n", h=H)
        bias_ps = bank_bf(3)[:, 512:640]
        pj_banks = [4, 5, 6, 7]

        jobs = []
        for cga, cgb in ((CG_K, CG_V), (CG_Q, CG_U)):
            for tt in range(NT):
                jobs.append((cga, tt))
                jobs.append((cgb, tt))
        qk_slot = {}
        for ji, (cg, tt) in enumerate(jobs):
            if cg in (CG_K, CG_Q):
                qk_slot[ji] = len(qk_slot) % 2

        def proj_mm(ji):
            cg, tt = jobs[ji]
            bk = pj_banks[ji % 4]
            fns = []
            for c in range(8):
                fns.append(lambda e, c=c: e.matmul(
                    bank_f(bk), lhsT=hT[:, c, tt * 128:(tt + 1) * 128], rhs=wqkvu[cg][:, c, :],
                    start=(c == 0), stop=(c == 7)))
            t.group("pe", fns, reads=[b_hT[tt], b_wqkvu[cg]], writes=hb[bk])

        def post_a2(ji):
            cg, tt = jobs[ji]
            if cg in (CG_K, CG_Q):
                i2 = qk_slot[ji]
                sq3 = sq_sb[i2].rearrange("p (h d) -> p h d", h=H)
                t.op("dve", lambda e: e.tensor_reduce(out=stat_a[i2], in_=sq3, axis=AX.X, op=ALU.add),
                     reads=[b_sq[i2]], writes=[b_sta[i2]])
                t.op("act", lambda e: e.activation(out=stat_b[i2], in_=stat_a[i2], func=AF.Sqrt,
                                                   scale=1.0 / DH, bias=eps_c[:, 0:1]),
                     reads=[b_sta[i2], b_eps], writes=[b_stb[i2]])

        def post_a1(ji):
            cg, tt = jobs[ji]
            bk = pj_banks[ji % 4]
            if cg in (CG_K, CG_Q):
                i2 = qk_slot[ji]
                t.op("act", lambda e: e.activation(out=sq_sb[i2], in_=bank_f(bk), func=AF.Square),
                     reads=hb[bk], writes=[b_sq[i2]])
            elif cg == CG_V:
                t.op("dve", lambda e: e.tensor_copy(
                    out=v_aug[:, tt, :, 0:64], in_=bank_f(bk).rearrange("p (h d) -> p h d", h=H)),
                    reads=hb[bk] + [b_vones], writes=[b_v[tt]])
            else:
                t.op("act", lambda e: e.activation(out=u_sb[:, tt, :], in_=bank_f(bk), func=AF.Copy),
                     reads=hb[bk], writes=[b_u[tt]])

        def post_b(ji):
            cg, tt = jobs[ji]
            if cg not in (CG_K, CG_Q):
                return
            is_k = cg == CG_K
            bk = pj_banks[ji % 4]
            i2 = qk_slot[ji]
            rp = rope_k if is_k else rope_q
            rb = b_c["ropek"] if is_k else b_c["ropeq"]
            ps3 = bank_f(bk).rearrange("p (h d) -> p h d", h=H)
            qt3 = qtok[i2].rearrange("p (h d) -> p h d", h=H)
            t.op("dve", lambda e: e.reciprocal(out=stat_b[i2], in_=stat_b[i2]),
                 reads=[b_stb[i2]], writes=[b_stb[i2]])
            t.op("dve", lambda e: e.tensor_tensor(
                out=qt3, in0=ps3, in1=stat_b[i2].unsqueeze(2).to_broadcast([128, H, DH]), op=ALU.mult),
                reads=hb[bk] + [b_stb[i2]], writes=[b_qtok[i2]])
            t.op("dve", lambda e: e.tensor_tensor(
                out=x16[i2], in0=ps3[:, :, 0:16], in1=stat_b[i2].unsqueeze(2).to_broadcast([128, H, 16]),
                op=ALU.mult),
                reads=hb[bk] + [b_stb[i2]], writes=[b_x16[i2]])
            cs_b = rp[:, tt, 0:16].unsqueeze(1).to_broadcast([128, H, 16])
            sn_lo = rp[:, tt, 16:24].unsqueeze(1).to_broadcast([128, H, 8])
            sn_hi = rp[:, tt, 24:32].unsqueeze(1).to_broadcast([128, H, 8])
            reng = "pool" if is_k else "dve"
            t.op(reng, lambda e: e.tensor_tensor(out=rt1[i2], in0=x16[i2], in1=cs_b, op=ALU.mult),
                 reads=[b_x16[i2], rb], writes=[b_rt1[i2]])
            t.op(reng, lambda e: e.tensor_tensor(out=rt2[i2][:, :, 0:8], in0=x16[i2][:, :, 8:16], in1=sn_lo,
                                                  op=ALU.mult),
                 reads=[b_x16[i2], rb], writes=[b_rt2[i2]])
            t.op(reng, lambda e: e.tensor_tensor(out=rt2[i2][:, :, 8:16], in0=x16[i2][:, :, 0:8], in1=sn_hi,
                                                  op=ALU.mult),
                 reads=[b_x16[i2], rb, b_rt2[i2]], writes=[b_rt2[i2]])
            t.op(reng, lambda e: e.tensor_tensor(out=qt3[:, :, 0:16], in0=rt1[i2], in1=rt2[i2], op=ALU.add),
                 reads=[b_rt1[i2], b_rt2[i2], b_qtok[i2]], writes=[b_qtok[i2]])

        def post_b_pe(ji):
            cg, tt = jobs[ji]
            if cg not in (CG_K, CG_Q):
                return
            is_k = cg == CG_K
            i2 = qk_slot[ji]
            tbk = tp_cnt[0] % 3
            tp_cnt[0] += 1
            tps = bank_bf(tbk)[:, 0:512]
            fns = []
            for pr in range(4):
                fns.append(lambda e, pr=pr: e.transpose(
                    out=tps[:, pr * 128:(pr + 1) * 128], in_=qtok[i2][:, pr * 128:(pr + 1) * 128],
                    identity=ident_b))
            wr = [hb[tbk][0]]
            if is_k:
                for pr in range(4):
                    fns.append(lambda e, pr=pr: e.matmul(
                        ksum_ps[:, pr, tt, :], lhsT=qtok[i2][:, pr * 128:(pr + 1) * 128], rhs=ones_b,
                        start=True, stop=True))
                wr.append(b_ksum_ps)
            t.group("pe", fns, reads=[b_qtok[i2], b_c["ident_b"], b_c["ones"]], writes=wr)
            tps3 = tps.rearrange("p (a b) -> p a b", a=4)
            if is_k:
                t.op("act", lambda e: e.activation(out=kT[:, :, tt * 128:(tt + 1) * 128], in_=tps3,
                                                   func=AF.Copy, scale=gcolT_k[:, 0:1]),
                     reads=[hb[tbk][0], b_c["gck"]], writes=[b_kT[tt]])
            else:
                t.op("act", lambda e: e.activation(
                    out=qTz[0:64, 0:H:2, tt * 128:(tt + 1) * 128], in_=tps3[0:64],
                    func=AF.Copy, scale=gcolT_q[0:64, 0:1]),
                    reads=[hb[tbk][0], b_c["gcq"], b_c["qz"]], writes=[b_qT[tt]])
                t.op("act", lambda e: e.activation(
                    out=qTz[64:128, 1:H:2, tt * 128:(tt + 1) * 128], in_=tps3[64:128],
                    func=AF.Copy, scale=gcolT_q[64:128, 0:1]),
                    reads=[hb[tbk][0], b_c["gcq"], b_c["qz"], b_qT[tt]], writes=[b_qT[tt]])

        def gate_mm(tt):
            fns = []
            for h in range(H):
                fns.append(lambda e, h=h: e.matmul(
                    gate_ps[:, h, :], lhsT=qTz[:, h, tt * 128:(tt + 1) * 128], rhs=kmeanT[:, h // 2, :],
                    start=True, stop=True))
            t.group("pe", fns, reads=[b_qT[tt], b_kmean], writes=[b_gate_ps])

        def gate_sel(tt):
            qb = tt // 2
            t.op("dve", lambda e: e.memset(gate_sb, -1.0e30), writes=[b_gate])
            t.op("dve", lambda e: e.tensor_copy(out=gate_sb[:, :, 0:qb], in_=gate_ps[:, :, 0:qb]),
                 reads=[b_gate_ps, b_gate], writes=[b_gate])
            g_m = gate_sb.unsqueeze(2).to_broadcast([128, H, 8, 8])
            g_n = gate_sb.unsqueeze(3).to_broadcast([128, H, 8, 8])
            t.op("dve", lambda e: e.tensor_tensor(out=rank_sb, in0=g_m, in1=g_n, op=ALU.is_gt),
                 reads=[b_gate, b_rank], writes=[b_rank])

        def gate_sel_b(tt):
            qb = tt // 2
            t.op("dve", lambda e: e.tensor_reduce(out=m8_sb, in_=rank_sb, axis=AX.X, op=ALU.add),
                 reads=[b_rank, b_m8], writes=[b_m8])
            t.op("dve", lambda e: e.tensor_scalar(out=sel_sb, in0=m8_sb, scalar1=2.5, scalar2=None,
                                                  op0=ALU.is_lt),
                 reads=[b_m8, b_sel], writes=[b_sel])
            b8, b_b8 = bias8[tt % 2], b_btok2[tt % 2]
            t.op("dve", lambda e: e.tensor_scalar(
                out=b8[:, 0:H:2, 64:64 + qb], in0=sel_sb[:, 0:H:2, 0:qb], scalar1=-1.0, scalar2=-NEG,
                op0=ALU.add, op1=ALU.mult),
                reads=[b_sel, b_b8], writes=[b_b8])
            t.op("dve", lambda e: e.tensor_scalar(
                out=b8[:, 1:H:2, 0:qb], in0=sel_sb[:, 1:H:2, 0:qb], scalar1=-1.0, scalar2=-NEG,
                op0=ALU.add, op1=ALU.mult),
                reads=[b_sel, b_b8], writes=[b_b8])

        def gate_tr(tt):
            b8, b_b8 = bias8[tt % 2], b_btok2[tt % 2]
            tbk = 3
            tps = bank_bf(tbk)
            t.group("pe", [lambda e, h=h: e.transpose(out=tps[:, h * 128:(h + 1) * 128], in_=b8[:, h, :],
                                                      identity=ident_b) for h in range(H)],
                    reads=[b_b8, b_c["ident_b"]], writes=[hb[tbk][0]])
            tp3 = tps.rearrange("p (h t) -> p h t", h=H)
            t.op("dve", lambda e: e.tensor_copy(out=qTz[64:72, 0:H:2, tt * 128:(tt + 1) * 128],
                                                in_=tp3[64:72, 0:H:2, :]),
                 reads=[hb[tbk][0], b_c["qz"], b_qT[tt]], writes=[b_qT[tt]])
            t.op("dve", lambda e: e.tensor_copy(out=qTz[0:8, 1:H:2, tt * 128:(tt + 1) * 128],
                                                in_=tp3[0:8, 1:H:2, :]),
                 reads=[hb[tbk][0], b_c["qz"], b_qT[tt]], writes=[b_qT[tt]])

        def kmean_fin():
            ks3 = bank_f(3)[:, 0:128].rearrange("p (a n c) -> p a n c", a=4, n=NB)
            ks3 = ks3[:, :, :, 0:4:2]
            t.op("dve", lambda e: e.tensor_reduce(out=ksum_sb, in_=ks3, axis=AX.X, op=ALU.add),
                 reads=[b_ksum_ps], writes=[b_kmean])
            t.op("dve", lambda e: e.tensor_scalar(out=kmeanT, in0=ksum_sb, scalar1=gcolT_k[:, 0:1],
                                                  scalar2=1.0 / 256.0, op0=ALU.mult, op1=ALU.mult),
                 reads=[b_kmean, b_c["gck"]], writes=[b_kmean])

        b_kT2 = Buf("kT2")

        def make_kT2():
            b_kT2.fence = t.snapshot()
            t.dma("sp", kT2, kT, reads=b_kT, writes=[b_kT2])
            for pr in range(4):
                t.dma("pool", kT[64:72, pr, :], c_ind_d, writes=b_kT)
                t.dma("pool", kT2[0:8, pr, :], c_ind_d, writes=[b_kT2])

        later = []

        def run_later(it):
            k = 0
            while k < len(later):
                if later[k][0] <= it:
                    later.pop(k)[1]()
                else:
                    k += 1

        p0_sched = {}

        def p0_at(it_, fn, *a):
            p0_sched.setdefault(max(it_, -1), []).append((fn, a))

        for tt_ in range(2, NT):
            p0_at(2 * tt_ - 6, p0_sq, tt_)
            p0_at(2 * tt_ - 5, p0_stats, tt_)
            p0_at(2 * tt_ - 4, p0_mul, tt_)
            p0_at(2 * tt_ - 2, p0_tr1, tt_)
        for tt_ in range(8, NT):
            p0_at(max(0, 2 * (tt_ - 8) - 1), p0_load1, tt_)

        p0_load1(0)
        t.dma("sp", gmix_bc, gmix_vec_d, writes=[b_gmix_bc])
        for tt_ in range(1, 4):
            p0_load1(tt_)
        t.wait_written("pool", b_xslot[3])
        t.dma("pool", wqkvu[CG_V], w_in_v[:, :, CG_V * 512:(CG_V + 1) * 512], writes=[b_wqkvu[CG_V]])
        t.op("dve", lambda e: e.memset(v_aug[:, :, :, 64:65], 1.0), writes=[b_vones])
        small_setup_k()
        for tt_ in range(4, 8):
            p0_load1(tt_)
        for tt_ in range(2):
            p0_sq(tt_)
            p0_stats(tt_)
            p0_mul(tt_)
            p0_tr1(tt_)
        small_setup_rest()
        for fn_, a_ in p0_sched.get(-1, []):
            fn_(*a_)
        fence_p0 = None
        b_u = None
        NJ = len(jobs)
        DEFER_PE = [NJ - 4, NJ - 2]
        for it in range(NJ + 6):
            if stop and stop.startswith("it") and it >= int(stop[2:]):
                t.wait_all("sp")
                return nc, list(dbg_d.keys())
            if it == 9:
                zero_fill()
            for fn_, a_ in p0_sched.get(it, []):
                fn_(*a_)
            if it < NJ:
                if it == 2 * NT:
                    fence_p0 = t.snapshot()
                    b_u = [Buf("u%d" % i, fence=fence_p0) for i in range(NT)]

                proj_mm(it)
            if 0 <= it - 1 < NJ:
                post_a1(it - 1)
            if 0 <= it - 2 < NJ:
                post_a2(it - 2)
            if 0 <= it - 3 < NJ:
                post_b(it - 3)
            if 0 <= it - 5 < NJ:
                jb = it - 5
                if jb not in DEFER_PE:
                    post_b_pe(jb)
                cg, tt = jobs[jb]
                if cg == CG_K and tt == NT - 1:
                    later.append((it + 2, kmean_fin))
                if cg == CG_V and tt == NT - 1:
                    later.append((it + 2, make_kT2))
            run_later(it)
        run_later(10 ** 9)
        dbg_dump("hT", hT, [128, 8, S], b_hT)
        dbg_dump("qTz", qTz, [128, H, S], b_qT)
        dbg_dump("kT", kT, [128, 4, S], b_kT)
        dbg_dump("v", v_aug, [128, NT, H, 65], b_v)
        dbg_dump("u", u_sb, [128, NT, 512], b_u)
        dbg_dump("kmeanT", kmeanT, [128, 4, 8], [b_kmean])
        fence_p1 = t.snapshot()
        hb[3][0].fence = dict(fence_p1)
        hb[3][1].fence = dict(fence_p1)
        if stop == "p1":
            t.wait_all("sp")
            return nc, list(dbg_d.keys())

        b_wsl = [[Buf("wsl%d_%d" % (s, j), fence=fence_p1) for j in range(4)] for s in range(2)]
        w_ua_v = w_ua_d.rearrange("(c p) n -> p c n", p=128)
        w_up_v = w_up_d.rearrange("(c p) n -> p c n", p=128)

        def load_slices(fc):
            s = fc % 2
            t.dma("pool", wga[s], w_in_v[:, :, 2048 + fc * 128: 2048 + (fc + 1) * 128], writes=[b_wsl[s][0]])
            t.dma("pool", wgp[s], w_in_v[:, :, 3072 + fc * 128: 3072 + (fc + 1) * 128], writes=[b_wsl[s][1]])
            t.dma("pool", wua[s], w_ua_v[:, :, fc * 128:(fc + 1) * 128], writes=[b_wsl[s][2]])
            t.dma("pool", wup[s], w_up_v[:, :, fc * 128:(fc + 1) * 128], writes=[b_wsl[s][3]])

        load_slices(0)

        b_pooled = [Buf("pooled%d" % i, fence=fence_p0) for i in range(4)]
        b_yT = [Buf("yT%d" % i, fence=fence_p1) for i in range(4)]
        def p3_band(tt, bk):
            fns = []
            for g in range(4):
                kind = 2 if tt == 0 else 0
                fns.append(lambda e, g=g, kind=kind: e.matmul(
                    bank_f(bk)[:, g * 128:(g + 1) * 128], lhsT=u_sb[:, tt, g * 128:(g + 1) * 128],
                    rhs=band[:, g * 3 + kind, :], start=True, stop=(tt == 0)))
                if tt > 0:
                    fns.append(lambda e, g=g: e.matmul(
                        bank_f(bk)[:, g * 128:(g + 1) * 128], lhsT=u_sb[:, tt - 1, g * 128:(g + 1) * 128],
                        rhs=band[:, g * 3 + 1, :], start=False, stop=True))
            rd = [b_u[tt], b_c["band"]] + ([b_u[tt - 1]] if tt > 0 else [])
            t.group("pe", fns, reads=rd, writes=hb[bk])
            t.op("dve", lambda e: e.tensor_copy(
                out=pooledT[:, :, tt * 128:(tt + 1) * 128],
                in_=bank_f(bk).rearrange("p (g t) -> p g t", g=4)),
                reads=hb[bk], writes=[b_pooled[tt // 4]])

        def p3_y(tg, g, bk):
            t.group("pe", [lambda e: e.matmul(
                bank_f(bk), lhsT=wgrp[:, g, :], rhs=pooledT[:, g, tg * 512:(tg + 1) * 512],
                start=True, stop=True)],
                reads=[b_pooled[tg], b_c["wgrp"]], writes=hb[bk])
            t.op("dve", lambda e: e.tensor_scalar(
                out=yT[:, g, tg * 512:(tg + 1) * 512], in0=bank_f(bk), scalar1=psc_c[:, g:g + 1],
                scalar2=None, op0=ALU.mult),
                reads=hb[bk] + [b_c["psc"]], writes=[b_yT[tg]])


        NPT = 5
        b_PT = [Buf("PT%d" % i, fence=fence_p1) for i in range(NPT)]
        b_otok = [Buf("otok%d" % i, fence=fence_p1) for i in range(2)]
        b_rden = [Buf("rden%d" % i) for i in range(4)]
        b_oT = [Buf("oT%d" % i, fence=fence_p1) for i in range(NT)]
        st_banks = [0, 1, 2]
        MISC = 3
        accb = [[4, 5], [6, 7]]

        steps = []
        for qb in range(NB):
            for h in range(H):
                for kp in range(qb + 1):
                    steps.append((qb, h, kp))
        LOOK = 3
        pend_post = []

        def emit_qk(si):
            qb, h, kp = steps[si]
            pr = h // 2
            sb_ = st_banks[si % 3]
            q0 = qb * 256
            ksel = kT if h % 2 == 0 else kT2
            fns = []
            for j in range(2):
                kt = 2 * kp + j
                fns.append(lambda e, kt=kt, j=j: e.matmul(
                    bank_f(sb_)[:, j * 256:(j + 1) * 256], lhsT=ksel[:, pr, kt * 128:(kt + 1) * 128],
                    rhs=qTz[:, h, q0:q0 + 256], start=True, stop=True))
            rd = [b_kT[2 * kp], b_kT[2 * kp + 1], b_qT[2 * qb], b_qT[2 * qb + 1]] + ([b_kT2] if h % 2 else [])
            t.group("pe", fns, reads=rd, writes=hb[sb_])
            pt = PT[si % NPT]
            t.op("act", lambda e: e.activation(out=pt, in_=bank_f(sb_), func=AF.Exp, scale=0.125),
                 reads=hb[sb_], writes=[b_PT[si % NPT]])
            if kp == qb and si in gate_busy:
                dg = pt.rearrange("p (a c) -> p a c", c=128)[:, 0:4:3, :]
                t.op("pool", lambda e: e.affine_select(
                    out=dg, in_=dg, pattern=[[0, 2], [1, 128]],
                    compare_op=ALU.is_ge, fill=0.0, base=0, channel_multiplier=-1),
                    reads=[b_PT[si % NPT]], writes=[b_PT[si % NPT]])
            elif kp == qb:
                dg = pt.rearrange("p (a c) -> p a c", c=128)[:, 0:4:3, :]
                t.op("dve", lambda e: e.tensor_tensor(
                    out=dg, in0=dg, in1=tri01.unsqueeze(1).to_broadcast([128, 2, 128]), op=ALU.mult),
                    reads=[b_PT[si % NPT], b_tri], writes=[b_PT[si % NPT]])

        def emit_pv(si):
            qb, h, kp = steps[si]
            half, hh = h // 4, h % 4
            pt = PT[si % NPT]
            for j in range(2):
                kt = 2 * kp + j
                for ql in range(2):
                    if ql == 0 and kt == 2 * qb + 1:
                        continue
                    last = (2 * qb) if ql == 0 else (2 * qb + 1)
                    bk = accb[ql][half]
                    dst = bank_f(bk)[:, hh * 65:(hh + 1) * 65]
                    c0_ = j * 256 + ql * 128
                    t.group("pe", [lambda e: e.matmul(
                        dst, lhsT=pt[:, c0_:c0_ + 128], rhs=v_aug[:, kt, h, :],
                        start=(kt == 0), stop=(kt == last))],
                        reads=[b_PT[si % NPT], b_v[kt]], writes=hb[bk])
            if kp == qb and hh == 3:
                for ql in range(2):
                    bk = accb[ql][half]
                    acc3 = bank_f(bk)[:, 0:260].rearrange("p (h d) -> p h d", h=4)
                    ri = ql * 2 + half
                    t.op("dve", lambda e, acc3=acc3, ri=ri: e.reciprocal(out=rden[ri], in_=acc3[:, :, 64]),
                         reads=hb[bk], writes=[b_rden[ri]])
                    ot3 = o_tok[ql][:, half * 256:(half + 1) * 256].rearrange("p (h d) -> p h d", h=4)
                    t.op("dve", lambda e, acc3=acc3, ri=ri, ot3=ot3: e.tensor_tensor(
                        out=ot3, in0=acc3[:, :, 0:64],
                        in1=rden[ri].unsqueeze(2).to_broadcast([128, 4, 64]), op=ALU.mult),
                        reads=hb[bk] + [b_rden[ri]], writes=[b_otok[ql]])
                if half == 1:
                    def post(ql, qb=qb):
                        if True:
                            tt = 2 * qb + ql
                            tps = bank_bf(MISC)[:, 0:512]
                            t.group("pe", [lambda e, pr=pr_, ql=ql: e.transpose(
                                out=tps[:, pr * 128:(pr + 1) * 128],
                                in_=o_tok[ql][:, pr * 128:(pr + 1) * 128], identity=ident_b)
                                for pr_ in range(4)],
                                reads=[b_otok[ql], b_c["ident_b"]], writes=hb[MISC])
                            t.op("dve", lambda e, tt=tt: e.tensor_copy(
                                out=o_attnT[:, :, tt * 128:(tt + 1) * 128],
                                in_=tps.rearrange("p (a b) -> p a b", a=4)),
                                reads=hb[MISC], writes=[b_oT[tt]])
                    pend_post.append((si + LOOK + 2, lambda post=post: post(0)))
                    pend_post.append((si + LOOK + 5, lambda post=post: post(1)))

        for b_ in (b_btok2[0], b_btok2[1], b_gate, b_m8, b_sel, b_rank):
            b_.fence = dict(fence_p1)
        for i in range(2):
            t.op("dve", lambda e, i=i: e.memset(bias8[i], 0.0), writes=[b_btok2[i]])
        gate_at = {}
        gate_busy = set()
        P3_START = 130
        misc_used = set()
        for i_, (qb_, h_, kp_) in enumerate(steps):
            if h_ == 0 and kp_ == 0 and i_ > 0:
                misc_used.update((i_ + LOOK + 1, i_ + LOOK + 4))

        def misc_free(x_):
            return all(abs(x_ - u_) > 2 for u_ in misc_used)

        s_ = 1
        tr_prev = []
        for k in range(8):
            if k == 4:
                s_ = max(s_, 82)
            while True:
                if misc_free(s_) and (k < 2 or s_ > tr_prev[k - 2]):
                    tr_ = next((s_ + d_ for d_ in range(10, 18) if misc_free(s_ + d_)), None)
                    if tr_ is not None:
                        break
                s_ += 1
            assert tr_ < (80 if k < 4 else P3_START - 2), (k, s_, tr_)
            gate_at.setdefault(s_, []).append(("sel", 8 + k))
            gate_at.setdefault(s_ + 3, []).append(("selb", 8 + k))
            gate_busy.update(range(s_ - 1, s_ + 7))
            gate_at.setdefault(tr_, []).append(("tr", 8 + k))
            misc_used.update((s_, tr_))
            tr_prev.append(tr_)
            s_ += 3
        p3_at = {}
        bounds_ = [len(steps)]
        for i_, (qb_, h_, kp_) in enumerate(steps):
            if h_ == 0 and kp_ == 0:
                bounds_.append(i_)
        slots_ = []
        s_ = P3_START + 2
        while len(slots_) < 32:
            if all(not (b_ + 2 <= s_ <= b_ + 10) for b_ in bounds_):
                slots_.append(s_)
                s_ += 4
            else:
                s_ += 1
        assert slots_[-1] < len(steps) - 4, slots_
        for k in range(NT):
            p3_at[slots_[k]] = ("band", k)
        for k in range(16):
            p3_at[slots_[NT + k]] = ("y", k // 4, k % 4)
        for si in range(len(steps) + LOOK):
            if si == P3_START:
                f80 = t.snapshot()
                for b_ in b_pooled + b_yT:
                    b_.fence = dict(f80)
            if si in (3, 6):
                post_b_pe(DEFER_PE[0] if si == 3 else DEFER_PE[1])
            if si < len(steps):
                emit_qk(si)
            if si in p3_at:
                a_ = p3_at[si]
                if a_[0] == "band":
                    p3_band(a_[1], MISC)
                else:
                    p3_y(a_[1], a_[2], MISC)
            for kind, gt in gate_at.get(si, []):
                if kind == "sel":
                    gate_mm(gt)
                    gate_sel(gt)
                elif kind == "selb":
                    gate_sel_b(gt)
                else:
                    gate_tr(gt)
            if si - LOOK >= 0:
                emit_pv(si - LOOK)
            while pend_post and pend_post[0][0] <= si:
                pend_post.pop(0)[1]()
        while pend_post:
            pend_post.pop(0)[1]()
        dbg_dump("o_attnT", o_attnT, [128, 4, S], b_oT)
        dbg_dump("yT", yT, [128, 4, S], b_yT)
        fence_p2 = t.snapshot()
        if stop == "p2":
            t.wait_all("sp")
            return nc, list(dbg_d.keys())
        b_wout = [Buf("wout%d" % i, fence=fence_p2) for i in range(2)]
        w_out_v = w_out_d.rearrange("(c p) n -> p c n", p=128)
        for hf in range(2):
            t.dma("pool", wout[:, :, hf * 512:(hf + 1) * 512], w_out_v[:, :, hf * 512:(hf + 1) * 512],
                  writes=[b_wout[hf]])

        b_ga = [Buf("ga%d" % s, fence=fence_p2) for s in range(2)]
        b_gp = [Buf("gp%d" % s, fence=fence_p2) for s in range(2)]
        b_mT = [Buf("mT%d" % i, fence=fence_p2) for i in range(4)]
        for j in range(4):
            b_wsl[1][j].fence = dict(fence_p2)
        it = 0
        p4_mid = {}
        for fc in range(8):
            if fc == 3:
                p4_mid = {"dve": t.cnt["dve"]}
            if fc + 1 < 8:
                load_slices(fc + 1)
            s = fc % 2
            for tg in range(4):
                T0 = tg * 512
                pb = (it % 2) * 4
                i2 = it % 2
                it += 1
                t.group("pe", [lambda e, c=c: e.matmul(bank_f(pb + 0), lhsT=wga[s][:, c, :],
                                                        rhs=hT[:, c, T0:T0 + 512], start=(c == 0), stop=(c == 7))
                               for c in range(8)],
                        reads=[b_wsl[s][0]] + b_hT[tg * 4:(tg + 1) * 4], writes=hb[pb + 0])
                t.group("pe", [lambda e, c=c: e.matmul(bank_f(pb + 1), lhsT=wgp[s][:, c, :],
                                                        rhs=hT[:, c, T0:T0 + 512], start=(c == 0), stop=(c == 7))
                               for c in range(8)],
                        reads=[b_wsl[s][1]] + b_hT[tg * 4:(tg + 1) * 4], writes=hb[pb + 1])
                t.group("pe", [lambda e, c=c: e.matmul(bank_f(pb + 2), lhsT=wua[s][:, c, :],
                                                        rhs=o_attnT[:, c, T0:T0 + 512], start=(c == 0), stop=(c == 3))
                               for c in range(4)],
                        reads=[b_wsl[s][2]] + b_oT[tg * 4:(tg + 1) * 4], writes=hb[pb + 2])
                t.group("pe", [lambda e, c=c: e.matmul(bank_f(pb + 3), lhsT=wup[s][:, c, :],
                                                        rhs=yT[:, c, T0:T0 + 512], start=(c == 0), stop=(c == 3))
                               for c in range(4)],
                        reads=[b_wsl[s][3], b_yT[tg]], writes=hb[pb + 3])
                t.op("act", lambda e: e.activation(out=ga_sb[i2], in_=bank_f(pb + 0), func=AF.Sigmoid),
                     reads=hb[pb + 0], writes=[b_ga[i2]])
                t.op("act", lambda e: e.activation(out=gp_sb[i2], in_=bank_f(pb + 1), func=AF.Sigmoid),
                     reads=hb[pb + 1], writes=[b_gp[i2]])
                t.op("dve", lambda e: e.tensor_tensor(out=ga_sb[i2], in0=bank_f(pb + 2), in1=ga_sb[i2],
                                                      op=ALU.mult),
                     reads=hb[pb + 2] + [b_ga[i2]], writes=[b_ga[i2]])
                t.op("dve", lambda e: e.tensor_tensor(out=gp_sb[i2], in0=bank_f(pb + 3), in1=gp_sb[i2],
                                                      op=ALU.mult),
                     reads=hb[pb + 3] + [b_gp[i2]], writes=[b_gp[i2]])
                t.op("dve", lambda e: e.tensor_tensor(out=mT[:, fc, T0:T0 + 512], in0=ga_sb[i2],
                                                      in1=gp_sb[i2], op=ALU.add),
                     reads=[b_ga[i2], b_gp[i2]], writes=[b_mT[tg]])
        dbg_dump("mT", mT, [128, 8, S], b_mT)
        fence_p4 = t.snapshot()

        w_ff1_v = w_ff1_d.rearrange("(c p) n -> p c n", p=128)
        w_ff2_v = w_ff2_d.rearrange("(j p) n -> p j n", p=128)
        b_ffw1 = [[Buf("ffw1_%d_%d" % (s, i)) for i in range(2)] for s in range(2)]
        b_ffw2 = [[Buf("ffw2_%d_%d" % (s, i)) for i in range(2)] for s in range(2)]

        def load_ff(fq):
            s = fq % 2
            for hf in range(2):
                t.dma("pool", ffw1[s][:, :, hf * 512:(hf + 1) * 512],
                      w_ff1_v[:, :, fq * 1024 + hf * 512: fq * 1024 + (hf + 1) * 512],
                      writes=[b_ffw1[s][hf]])
            for hf in range(2):
                t.dma("pool", ffw2[s][:, :, hf * 512:(hf + 1) * 512],
                      w_ff2_v[:, fq * 8:(fq + 1) * 8, hf * 512:(hf + 1) * 512],
                      writes=[b_ffw2[s][hf]])

        for hf in range(2):
            b_ffw1[0][hf].fence = dict(fence_p4)
            b_ffw2[0][hf].fence = dict(fence_p4)
        load_ff(0)

        b_x1 = [Buf("x1_%d" % i, fence=(fence_p4 if i < 4 else fence_p2)) for i in range(NT)]
        b_xt5 = [Buf("xt5_%d" % i, fence=fence_p2) for i in range(2)]
        b_ss2 = Buf("ss2")
        b_r2 = Buf("r2sq")
        b_junk2 = Buf("junk2")
        b_ss2h = Buf("ss2h")
        b_h2T = [Buf("h2T%d" % i, fence=fence_p4) for i in range(4)]
        p5_banks = [0, 1, 2, 3]
        p5_i = 0

        p5_bk = {}

        def p5_mm(tt):
            nonlocal p5_i
            for hf in range(2):
                bk = p5_banks[p5_i % 4]
                p5_i += 1
                p5_bk[(tt, hf)] = bk
                t.group("pe", [lambda e, c=c, bk=bk, hf=hf: e.matmul(
                    bank_f(bk), lhsT=mT[:, c, tt * 128:(tt + 1) * 128],
                    rhs=wout[:, c, hf * 512:(hf + 1) * 512], start=(c == 0), stop=(c == 7))
                    for c in range(8)],
                    reads=[b_mT[tt // 4], b_wout[hf]], writes=hb[bk])

        def p5_add(tt):
            for hf in range(2):
                bk = p5_bk[(tt, hf)]
                dst = x1[:, tt, hf * 512:(hf + 1) * 512]
                t.op("dve", lambda e, bk=bk, dst=dst: e.tensor_tensor(
                    out=dst, in0=bank_f(bk), in1=dst, op=ALU.add),
                    reads=hb[bk] + [b_x1[tt]], writes=[b_x1[tt]])

        p6_banks = [4, 5, 6, 7]
        p6_i = 0

        b_gbc = Buf("gmlp_bc", fence=fence_p4)
        b_hb16 = [Buf("hb16_%d" % i, fence=fence_p4) for i in range(2)]

        p6_slot = {}

        def p6_mul(tt):
            nonlocal p6_i
            s2 = p6_i % 2
            p6_slot[tt] = (s2, p6_banks[p6_i % 4])
            p6_i += 1
            t.op("dve", lambda e: e.tensor_tensor(out=hb16[s2], in0=x1[:, tt, :], in1=gmlp_bc, op=ALU.mult),
                 reads=[b_x1[tt], b_gbc], writes=[b_hb16[s2]])

        def p6_tr(tt):
            s2, bk = p6_slot[tt]
            tps = bank_bf(bk)
            t.group("pe", [lambda e, c=c: e.transpose(
                out=tps[:, c * 128:(c + 1) * 128], in_=hb16[s2][:, c * 128:(c + 1) * 128],
                identity=ident_b) for c in range(8)],
                reads=[b_hb16[s2], b_c["ident_b"]], writes=hb[bk])
            t.op("act", lambda e: e.activation(
                out=hT[:, :, tt * 128:(tt + 1) * 128], in_=tps.rearrange("p (c t) -> p c t", c=8),
                func=AF.Copy),
                reads=hb[bk], writes=[b_h2T[tt // 4]])

        def p6_sq(tg, k):
            tt, hf = tg * 4 + k // 2, k % 2
            t.op("act", lambda e: e.activation(
                out=junk_h, in_=x1[:, tt, hf * 512:(hf + 1) * 512], func=AF.Square,
                accum_out=ss2h[:, hf * NT + tt: hf * NT + tt + 1]),
                reads=[b_x1[tt]], writes=[b_junk2, b_ss2h])

        def p6_stats(tg):
            c0, c1 = tg * 4, tg * 4 + 4
            t.op("dve", lambda e: e.tensor_tensor(out=ss2[:, c0:c1], in0=ss2h[:, c0:c1],
                                                  in1=ss2h[:, NT + c0:NT + c1], op=ALU.add),
                 reads=[b_ss2h], writes=[b_ss2])
            t.op("dve", lambda e: e.tensor_scalar(out=r2sq[:, c0:c1], in0=ss2[:, c0:c1], scalar1=1.0 / D,
                                                  scalar2=EPS, op0=ALU.mult, op1=ALU.add),
                 reads=[b_ss2], writes=[b_r2])
            t.op("dve", lambda e: e.reciprocal(out=r2sq[:, c0:c1], in_=r2sq[:, c0:c1]),
                 reads=[b_r2], writes=[b_r2])

        P5_ORDER = list(range(4, NT)) + list(range(4))
        t._wait("sp", p4_mid)
        for tt in P5_ORDER:
            if tt == 0:
                t.dma("sp", gmlp_bc, gmlp_vec_d, writes=[b_gbc])
            t.dma("sp", x1[:, tt, :], x_d[tt * 128:(tt + 1) * 128, :], writes=[b_x1[tt]])
        for k in range(NT + 1):
            if k < NT:
                p5_mm(P5_ORDER[k])
            if k >= 1:
                p6_mul(P5_ORDER[k - 1])
            if k < NT:
                p5_add(P5_ORDER[k])
            if k >= 1:
                p6_tr(P5_ORDER[k - 1])
        fence_p5 = t.snapshot()
        dbg_dump("x1", x1, [128, NT, D], b_x1)
        fence_p6 = t.snapshot()
        for hf in range(2):
            b_ffw1[1][hf].fence = dict(fence_p6)
            b_ffw2[1][hf].fence = dict(fence_p6)
        load_ff(1)

        b_a1T = [Buf("a1T%d" % s, fence=fence_p5) for s in range(2)]
        b_sqz = [Buf("sqz%d" % s, fence=fence_p5) for s in range(2)]
        f1_banks = [0, 1, 2, 3]
        f2_banks = [4, 5, 6, 7]
        f1_i = 0
        f2_i = 0
        sq_i = 0

        def ff1(fq, tg, ab):
            nonlocal f1_i, sq_i
            s = fq % 2
            for j in range(8):
                bk = f1_banks[f1_i % 4]
                f1_i += 1
                hf = j // 4
                t.group("pe", [lambda e, c=c, j=j, bk=bk: e.matmul(
                    bank_f(bk), lhsT=ffw1[s][:, c, j * 128:(j + 1) * 128],
                    rhs=hT[:, c, tg * 512:(tg + 1) * 512], start=(c == 0), stop=(c == 7))
                    for c in range(8)],
                    reads=[b_ffw1[s][hf], b_h2T[tg]], writes=hb[bk])
                si_ = sq_i % 2
                sq_i += 1
                t.op("act", lambda e, bk=bk, si_=si_: e.activation(out=sqz[si_], in_=bank_f(bk), func=AF.Square),
                     reads=hb[bk], writes=[b_sqz[si_]])
                t.op("dve", lambda e, bk=bk, si_=si_, j=j: e.scalar_tensor_tensor(
                    out=a1T[ab][:, j, :], in0=bank_f(bk), scalar=0.0, in1=sqz[si_],
                    op0=ALU.is_gt, op1=ALU.mult),
                    reads=hb[bk] + [b_sqz[si_]], writes=[b_a1T[ab]])
                if fq == 0:
                    p6_sq(tg, j)

        out_bufs = []

        def ff2(fq, tg, ab):
            nonlocal f2_i
            s = fq % 2
            for i in range(4):
                tt = tg * 4 + i
                for hf in range(2):
                    bk = f2_banks[f2_i % 4]
                    f2_i += 1
                    t.group("pe", [lambda e, j=j, bk=bk, hf=hf, i=i: e.matmul(
                        bank_f(bk), lhsT=a1T[ab][:, j, i * 128:(i + 1) * 128],
                        rhs=ffw2[s][:, j, hf * 512:(hf + 1) * 512], start=(j == 0), stop=(j == 7))
                        for j in range(8)],
                        reads=[b_a1T[ab], b_ffw2[s][hf]], writes=hb[bk])
                    dst = x1[:, tt, hf * 512:(hf + 1) * 512]
                    t.op("dve", lambda e, bk=bk, dst=dst, tt=tt: e.scalar_tensor_tensor(
                        out=dst, in0=bank_f(bk), scalar=r2sq[:, tt:tt + 1], in1=dst,
                        op0=ALU.mult, op1=ALU.add),
                        reads=hb[bk] + [b_r2, b_x1[tt]], writes=[b_x1[tt]])
                if fq == 3:
                    t.dma("sp", out_d[tt * 128:(tt + 1) * 128, :], x1[:, tt, :], reads=[b_x1[tt]])

        ab_i = 0
        pending = None
        for fq in range(4):
            if fq >= 1 and fq + 1 < 4:
                pass
            for tg in (1, 2, 3, 0):
                ab = ab_i % 2
                ab_i += 1
                ff1(fq, tg, ab)
                if fq == 0:
                    p6_stats(tg)
                if pending is not None:
                    ff2(*pending)
                pending = (fq, tg, ab)
            if fq + 2 < 4:
                ff2(*pending)
                pending = None
                load_ff(fq + 2)
        ff2(*pending)

        t.wait_all("sp")
    return nc, list(dbg_d.keys())


def _host_consts():
    ident = np.eye(128, dtype=np.float32)
    half = 8
    inv = (500000.0 ** (-np.arange(half, dtype=np.float32) / half)).astype(np.float32)
    ang = np.arange(S, dtype=np.float32)[:, None] * inv[None, :]
    cos = np.cos(ang).astype(np.float32)
    sin = np.sin(ang).astype(np.float32)
    rope = np.concatenate([cos, cos, -sin, sin], axis=1).astype(np.float32)
    rope = np.ascontiguousarray(rope.reshape(NT, 128, 32).transpose(1, 0, 2).reshape(128, NT * 32))
    band = np.zeros((128, 12, 128), dtype=np.float32)
    tp = np.arange(128)[:, None]
    tq = np.arange(128)[None, :]
    for g, w in enumerate((2, 4, 8, 16)):
        inwin = (tp <= tq) & (tp > tq - w)
        main = np.where(inwin, 1.0 / w, 0.0) - np.eye(128)
        prev = np.where((tp - 128) > (tq - w), 1.0 / w, 0.0)
        cnt = np.minimum(tq + 1, w).astype(np.float64)
        main0 = np.where(inwin, 1.0 / cnt, 0.0) - np.eye(128)
        band[:, g * 3 + 0, :] = main
        band[:, g * 3 + 1, :] = prev
        band[:, g * 3 + 2, :] = main0
    return ident, rope, band


_CACHE = {}


def kernel(x, norm_mix, w_in, q_norm, k_norm, w_pool_grp, pool_scale, w_up_attn, w_up_pool,
           w_out, norm_mlp, w_ff1, w_ff2, _debug=(), _cores=N_CORES, _stop=None):
    f = lambda a: np.ascontiguousarray(np.asarray(a, dtype=np.float32))
    x = f(x)
    ident, rope, band = _host_consts()
    gq, gk = f(q_norm)[0], f(k_norm)[0]
    small = np.zeros((128, 24), dtype=np.float32)
    small[:, 0:8] = f(norm_mix)[0].reshape(8, 128).T
    small[:, 8:16] = f(norm_mlp)[0].reshape(8, 128).T
    small[:, 16:20] = f(pool_scale)[0].reshape(4, 128).T
    small[:, 20] = np.concatenate([gq, gq])
    small[:, 21] = np.concatenate([gk, gk])
    gv = np.concatenate([gq[0:16], gq[8:16], gq[0:8], gk[0:16], gk[8:16], gk[0:8]]).astype(np.float32)
    shared = {
        "w_in": f(w_in)[0], "w_pool_grp": f(w_pool_grp)[0], "w_up_attn": f(w_up_attn)[0],
        "w_up_pool": f(w_up_pool)[0], "w_out": f(w_out)[0], "w_ff1": f(w_ff1)[0], "w_ff2": f(w_ff2)[0],
        "c_small": small, "c_gv": gv, "gmlp_bc": np.ascontiguousarray(np.broadcast_to(f(norm_mlp)[0], (128, D))),
        "gmix_bc": np.ascontiguousarray(np.broadcast_to(f(norm_mix)[0], (128, D))),
        "c_ident": ident, "c_rope": rope, "c_band": band,
        "c_zero": np.zeros((64, 1024), dtype=np.float32),
        "c_ind": (np.arange(S)[None, :] // 256 == np.arange(8)[:, None]).astype(np.float32),
    }
    key = (tuple(_debug), _stop)
    if key not in _CACHE:
        _CACHE[key] = build_nc(debug=_debug, stop=_stop)
    nc, dbg_names = _CACHE[key]
    in_maps = []
    for b in range(_cores):
        m = dict(shared)
        m["x"] = x[b]
        in_maps.append(m)
    res = run_bass_kernel_spmd(nc, in_maps, core_ids=list(range(_cores)))
    out = np.stack([res.results[b]["out"] for b in range(_cores)], axis=0)
    if _debug:
        return out, {n: res.results[0]["dbg_" + n] for n in dbg_names}
    return out
```

```python
import numpy as np
from contextlib import ExitStack

import concourse.bass as bass
import concourse.mybir as mybir
from concourse.bass_utils import run_bass_kernel_spmd

F32 = mybir.dt.float32
BF16 = mybir.dt.bfloat16
AF = mybir.ActivationFunctionType
ALU = mybir.AluOpType
AX = mybir.AxisListType

S = 2048
D = 1024
NT = 16
H = 8
DH = 64
NB = 8
NEG = -240000.0
EPS = 1e-6
N_CORES = 8


class Buf:
    __slots__ = ("name", "w", "r", "fence", "excl")

    def __init__(self, name, fence=None, excl=False):
        self.name = name
        self.excl = excl
        self.w = None
        self.r = {}
        self.fence = dict(fence) if fence else None


class Trk:
    def __init__(self, nc, es, n_ring=12):
        self.nc = nc
        self.E = {"pe": nc.tensor, "act": nc.scalar, "dve": nc.vector,
                  "pool": nc.gpsimd, "sp": nc.sync}
        self.sems = {}
        for k in self.E:
            self.sems[k] = es.enter_context(nc.semaphore("s_" + k))
        self.cnt = {k: 0 for k in self.E}
        self.seen = {k: {} for k in self.E}
        self.rings = {}
        for q in ("sp", "pool"):
            slots = []
            for j in range(n_ring):
                key = "d_%s%d" % (q, j)
                self.sems[key] = es.enter_context(nc.semaphore(key))
                slots.append([key, 0])
            self.rings[q] = {"i": 0, "slots": slots}

    def snapshot(self):
        snap = {k: v for k, v in self.cnt.items() if v > 0}
        for q in self.rings.values():
            for key, tot in q["slots"]:
                if tot > 0:
                    snap[key] = tot
        return snap

    @staticmethod
    def _add(need, k, v):
        if need.get(k, 0) < v:
            need[k] = v

    def _need(self, reads, writes, e=None):
        need = {}
        for b in reads:
            if b.w:
                self._add(need, *b.w)
            if b.excl:
                for k, v in b.r.items():
                    if k != e:
                        self._add(need, k, v)
            if b.fence:
                for k, v in b.fence.items():
                    self._add(need, k, v)
        for b in writes:
            if b.w:
                self._add(need, *b.w)
            for k, v in b.r.items():
                self._add(need, k, v)
            if b.fence:
                for k, v in b.fence.items():
                    self._add(need, k, v)
        return need

    def _wait(self, e, need):
        eng = self.E[e]
        seen = self.seen[e]
        for k, v in need.items():
            if e == "pe" and k == "pe":
                continue
            if seen.get(k, 0) < v:
                eng.wait_ge(self.sems[k], v)
                seen[k] = v

    def _mark(self, key, val, reads, writes):
        for b in reads:
            if b.r.get(key, 0) < val:
                b.r[key] = val
        for b in writes:
            b.w = (key, val)
            b.r = {}

    def op(self, e, fn, reads=(), writes=()):
        self._wait(e, self._need(reads, writes, e))
        ins = fn(self.E[e])
        self.cnt[e] += 1
        ins.then_inc(self.sems[e], 1)
        self._mark(e, self.cnt[e], reads, writes)

    def group(self, e, fns, reads=(), writes=()):
        self._wait(e, self._need(reads, writes, e))
        ins = None
        for fn in fns:
            ins = fn(self.E[e])
        self.cnt[e] += 1
        ins.then_inc(self.sems[e], 1)
        self._mark(e, self.cnt[e], reads, writes)

    def dma(self, q, out, in_, reads=(), writes=()):
        need = self._need(reads, writes)
        ring = self.rings[q]
        slot = ring["slots"][ring["i"] % len(ring["slots"])]
        ring["i"] += 1
        if slot[1] > 0:
            self._add(need, slot[0], slot[1])
        self._wait(q, need)
        ins = self.E[q].dma_start(out=out, in_=in_)
        slot[1] += 16
        ins.then_inc(self.sems[slot[0]], 16)
        self._mark(slot[0], slot[1], reads, writes)

    def wait_all(self, e):
        self._wait(e, self.snapshot())

    def wait_written(self, e, buf):
        if buf.w:
            self._wait(e, {buf.w[0]: buf.w[1]})


def build_nc(debug=(), stop=None):
    nc = bass.Bass("TRN2", target_bir_lowering=False)

    def din(name, shape):
        return nc.dram_tensor(name, list(shape), F32, kind="ExternalInput").ap()

    x_d = din("x", [S, D])
    w_in_d = din("w_in", [D, 4096])
    w_grp_d = din("w_pool_grp", [4, 128, 128])
    w_ua_d = din("w_up_attn", [512, D])
    w_up_d = din("w_up_pool", [512, D])
    w_out_d = din("w_out", [D, D])
    w_ff1_d = din("w_ff1", [D, 4096])
    w_ff2_d = din("w_ff2", [4096, D])
    c_small_d = din("c_small", [128, 24])
    gmlp_vec_d = din("gmlp_bc", [128, D])
    gmix_vec_d = din("gmix_bc", [128, D])
    c_gv_d = din("c_gv", [64])
    c_ident_d = din("c_ident", [128, 128])
    c_rope_d = din("c_rope", [128, NT * 32])
    c_band_d = din("c_band", [128, 12, 128])
    c_zero_d = din("c_zero", [64, 1024])
    c_ind_d = din("c_ind", [8, S])
    out_d = nc.dram_tensor("out", [S, D], F32, kind="ExternalOutput").ap()
    dbg_d = {}

    es = ExitStack()
    with es:
        RAWB = 103 * 2048 + 256
        raw = nc.alloc_sbuf_tensor("raw", [128, RAWB // 2], BF16)

        def view(off, shape, dt, p0=0):
            esz = 2 if dt == BF16 else 4
            n = 1
            for s_ in shape[1:]:
                n *= s_
            assert off % 4 == 0 and off + n * esz <= RAWB, (off, shape)
            v = raw[p0:p0 + shape[0], off // 2: off // 2 + n * esz // 2]
            if dt != BF16:
                v = v.bitcast(dt)
            if len(shape) == 3:
                v = v.rearrange("p (a b) -> p a b", a=shape[1])
            elif len(shape) == 4:
                v = v.rearrange("p (a b c) -> p a b c", a=shape[1], b=shape[2])
            return v

        KB = 1024
        A0 = 0
        B0 = 7 * KB
        C0 = B0 + 32 * KB
        D0 = C0 + 64 * KB + 256
        E0 = D0 + 32 * KB
        F0 = E0 + 32 * KB
        G0 = F0 + 32 * KB

        ident_f = view(A0 + 0, [128, 128], F32)
        ident_b = view(A0 + 512, [128, 128], BF16)
        tri01 = view(A0 + 768, [128, 128], BF16)
        rope_t = view(A0 + 1024, [128, NT, 32], F32)
        small_c = view(A0 + 3072, [128, 24], F32)
        gmix_c = small_c[:, 0:8]
        gmlp_c = small_c[:, 8:16]
        psc_c = small_c[:, 16:20]
        gcolT_q = small_c[:, 20:21]
        gcolT_k = small_c[:, 21:22]
        gv_qk = view(A0 + 3200, [128, 64], F32)
        gv_q = gv_qk[:, 0:32]
        gv_k = gv_qk[:, 32:64]
        ones_b = view(A0 + 3664, [128, 2], BF16)
        band = view(A0 + 3680, [128, 12, 128], BF16)
        wgrp = view(G0 + 0, [128, 4, 128], BF16)
        ss1 = view(G0 + 1024, [128, NT], F32)
        rstd1 = view(G0 + 1088, [128, NT], F32)
        ss2 = view(G0 + 1152, [128, NT], F32)
        r2sq = view(G0 + 1216, [128, NT], F32)
        kmeanT = view(G0 + 1280, [128, 4, 8], BF16)
        ksum_sb = view(G0 + 1344, [128, 4, 8], F32)
        stat_a = [view(G0 + 5632 + i * 64, [128, 8], F32) for i in range(2)]
        stat_b = [view(G0 + 5760 + i * 64, [128, 8], F32) for i in range(2)]
        junk_h = view(G0 + 5888, [128, 512], BF16)
        ss2h = view(G0 + 6912, [128, 2 * NT], F32)

        hT = view(B0, [128, 8, S], BF16)
        qTz = view(C0, [128, H, S], BF16)
        kT = view(C0 + 32 * KB, [128, 4, S], BF16)
        v_aug = view(C0 + 48 * KB, [128, NT, H, 65], BF16)
        assert C0 + 48 * KB + 16640 <= D0
        x1 = view(C0, [128, NT, D], F32)
        _sb = [F0, C0]
        wga = [view(_sb[s], [128, 8, 128], BF16) for s in range(2)]
        wgp = [view(_sb[s] + 2 * KB, [128, 8, 128], BF16) for s in range(2)]
        wua = [view(_sb[s] + 4 * KB, [128, 4, 128], BF16) for s in range(2)]
        wup = [view(_sb[s] + 5 * KB, [128, 4, 128], BF16) for s in range(2)]
        ga_sb = [view(C0 + 6 * KB + s * 4 * KB, [128, 512], F32) for s in range(2)]
        gp_sb = [view(C0 + 6 * KB + s * 4 * KB + 2 * KB, [128, 512], F32) for s in range(2)]
        xslot = [view(D0 + i * 4 * KB, [128, D], F32) for i in range(8)]
        u_sb = view(D0, [128, NT, 512], BF16)
        pooledT = view(D0 + 16 * KB, [128, 4, S], BF16)
        mT = view(D0, [128, 8, S], BF16)
        a1T = [view(D0 + s * 8 * KB, [128, 8, 512], BF16) for s in range(2)]
        sqz = [view(D0 + 16 * KB + s * 2 * KB, [128, 512], F32) for s in range(2)]
        o_attnT = view(E0, [128, 4, S], BF16)
        junk = view(E0, [128, D], BF16)
        gmix_bc = view(E0 + 2 * KB, [128, D], F32)
        hb0 = [view(E0 + 6 * KB + i * 2 * KB, [128, D], BF16) for i in range(2)]
        E1 = E0 + 16 * KB
        sq_sb = [view(E1 + i * 2 * KB, [128, 512], F32) for i in range(2)]
        qtok = [view(E1 + 4 * KB + i * KB, [128, 512], BF16) for i in range(2)]
        x16 = [view(E1 + 6 * KB + i * 512, [128, 8, 16], F32) for i in range(2)]
        rt1 = [view(E1 + 7 * KB + i * 512, [128, 8, 16], F32) for i in range(2)]
        rt2 = [view(E1 + 8 * KB + i * 512, [128, 8, 16], F32) for i in range(2)]
        rope_q = view(E1 + 9 * KB, [128, NT, 32], F32)
        rope_k = view(E1 + 11 * KB, [128, NT, 32], F32)
        yT = view(E0 + 16 * KB, [128, 4, S], BF16)
        wqkvu = [view(F0 + i * 8 * KB, [128, 8, 512], BF16) for i in range(4)]
        wout = view(F0 + 8 * KB, [128, 8, D], BF16)
        kT2 = view(F0 + 8 * KB, [128, 4, S], BF16)
        xt5 = [view(F0 + 24 * KB + i * 4 * KB, [128, D], F32) for i in range(2)]
        gmlp_bc = view(F0, [128, D], F32)
        hb16 = [view(F0 + 4 * KB + i * 2 * KB, [128, D], BF16) for i in range(2)]
        f2 = F0 + 24 * KB
        PT = [view(f2 + i * 1024, [128, 512], BF16) for i in range(5)]
        o_tok = [view(f2 + 5120 + i * 1024, [128, 512], BF16) for i in range(2)]
        rden = [view(G0 + 7040 + i * 16, [128, 4], F32) for i in range(4)]
        assert f2 + 7168 <= F0 + 31 * KB
        gate_sb = view(F0 + 31 * KB, [128, 8, 8], F32)
        m8_sb = view(F0 + 31 * KB + 256, [128, 8, 8], F32)
        sel_sb = view(F0 + 31 * KB + 512, [128, 8, 8], F32)
        bias8 = [view(F0 + 6 * KB, [128, H, 128], BF16), view(D0 + 16 * KB, [128, H, 128], BF16)]
        rank_sb = view(D0 + 18 * KB, [128, H, 8, 8], F32)
        ffw1 = [view(E0, [128, 8, 1024], BF16), view(F0, [128, 8, 1024], BF16)]
        ffw2 = [view(E0 + 16 * KB, [128, 8, 1024], BF16), view(F0 + 16 * KB, [128, 8, 1024], BF16)]

        banks = [es.enter_context(nc.psum_tensor("bank%d" % i, [128, 512], F32)) for i in range(8)]
        hb = []
        for i in range(8):
            pb_ = Buf("psbank%d" % i, excl=True)
            hb.append([pb_, pb_])

        def bank_f(i):
            return banks[i][:, :]

        def bank_bf(i):
            return banks[i][:, :].bitcast(BF16)

        t = Trk(nc, es)

        def dbg_dump(name, ap, shape, bufs):
            if name not in debug:
                return
            d = nc.dram_tensor("dbg_" + name, list(shape), ap.dtype, kind="ExternalOutput").ap()
            dbg_d[name] = d
            t.dma("sp", d, ap, reads=bufs)

        b_wqkvu = [Buf("wqkvu%d" % i) for i in range(4)]
        w_in_v = w_in_d.rearrange("(c p) n -> p c n", p=128)
        CG_K, CG_Q, CG_V, CG_U = 1, 0, 2, 3
        b_c = {n: Buf(n) for n in ("ident_f", "ident_b", "rope", "gvq", "gvk", "gmix", "gmlp",
                                    "psc", "ones", "band", "wgrp", "gcq", "gck", "ropeq", "ropek",
                                    "qz")}
        t.dma("sp", ident_f, c_ident_d, writes=[b_c["ident_f"]])
        t.dma("sp", small_c, c_small_d, writes=[b_c["gmix"], b_c["gmlp"], b_c["psc"], b_c["gcq"], b_c["gck"]])
        t.dma("sp", gv_qk, c_gv_d.partition_broadcast(128), writes=[b_c["gvq"], b_c["gvk"]])
        t.dma("sp", rope_t, c_rope_d.rearrange("p (t n) -> p t n", t=NT), writes=[b_c["rope"]])
        t.dma("pool", wqkvu[CG_K], w_in_v[:, :, CG_K * 512:(CG_K + 1) * 512], writes=[b_wqkvu[CG_K]])
        t.dma("pool", ident_b, c_ident_d, writes=[b_c["ident_b"]])

        def setup_gains(gv, gc, rp, nm_gv, nm_gc, nm_rp):
            t.op("dve", lambda e: e.memset(gc[0:16, :], 1.0), writes=[b_c[nm_gc]])
            t.op("dve", lambda e: e.memset(gc[64:80, :], 1.0), writes=[b_c[nm_gc]])
            t.op("dve", lambda e: e.tensor_tensor(
                out=rp, in0=rope_t, in1=gv.unsqueeze(1).to_broadcast([128, NT, 32]), op=ALU.mult),
                reads=[b_c["rope"], b_c[nm_gv]], writes=[b_c[nm_rp]])

        b_tri = Buf("tri01")
        b_eps = Buf("eps")
        eps_c = small_c[:, 22:23]

        def small_setup_k():
            t.op("dve", lambda e: e.memset(eps_c, EPS), reads=[b_c["gmix"]], writes=[b_eps])
            t.op("pool", lambda e: e.memset(tri01, 1.0), writes=[b_tri])
            t.op("pool", lambda e: e.affine_select(
                out=tri01, in_=tri01, pattern=[[1, 128]], compare_op=ALU.is_ge, fill=0.0, base=0,
                channel_multiplier=-1), reads=[b_tri], writes=[b_tri])
            t.op("dve", lambda e: e.memset(ones_b, 1.0), writes=[b_c["ones"]])
            setup_gains(gv_k, gcolT_k, rope_k, "gvk", "gck", "ropek")

        def small_setup_rest():
            t.wait_written("pool", b_xslot[7])
            t.dma("pool", wqkvu[CG_Q], w_in_v[:, :, CG_Q * 512:(CG_Q + 1) * 512], writes=[b_wqkvu[CG_Q]])
            setup_gains(gv_q, gcolT_q, rope_q, "gvq", "gcq", "ropeq")
            t.dma("pool", wqkvu[CG_U], w_in_v[:, :, CG_U * 512:(CG_U + 1) * 512], writes=[b_wqkvu[CG_U]])
            t.dma("pool", band, c_band_d, writes=[b_c["band"]])
            t.dma("pool", wgrp, w_grp_d.rearrange("g c d -> c g d"), writes=[b_c["wgrp"]])

        def zero_fill():
            for h in range(H):
                base = 64 if h % 2 == 0 else 0
                t.dma("sp", qTz[base:base + 64, h, :].bitcast(F32), c_zero_d, writes=[b_c["qz"]])

        b_xslot = [Buf("xslot%d" % i) for i in range(8)]
        b_ss1 = Buf("ss1")
        b_rstd1 = Buf("rstd1")
        b_junk = Buf("junk")
        b_hT = [Buf("hT_%d" % i) for i in range(NT)]
        ev_cnt = [0]
        tp_cnt = [0]

        b_gmix_bc = Buf("gmix_bc")
        b_hb0 = [Buf("hb0_%d" % i) for i in range(2)]
        p0_bank = {}

        def p0_load1(tt):
            sl = tt % 8
            t.dma("sp", xslot[sl], x_d[tt * 128:(tt + 1) * 128, :], writes=[b_xslot[sl]])

        def p0_sq(tt):
            sl = tt % 8
            t.op("act", lambda e: e.activation(
                out=junk, in_=xslot[sl], func=AF.Square, accum_out=ss1[:, tt:tt + 1]),
                reads=[b_xslot[sl]], writes=[b_junk, b_ss1])

        def p0_stats(tt):
            c0, c1 = tt, tt + 1
            t.op("act", lambda e: e.activation(out=rstd1[:, c0:c1], in_=ss1[:, c0:c1], func=AF.Sqrt,
                                               scale=1.0 / D, bias=eps_c[:, 0:1]),
                 reads=[b_ss1, b_eps], writes=[b_rstd1])
            t.op("dve", lambda e: e.reciprocal(out=rstd1[:, c0:c1], in_=rstd1[:, c0:c1]),
                 reads=[b_rstd1], writes=[b_rstd1])

        def p0_mul(tt):
            sl = tt % 8
            s2 = tt % 2
            t.op("dve", lambda e: e.scalar_tensor_tensor(
                out=hb0[s2], in0=xslot[sl], scalar=rstd1[:, tt:tt + 1], in1=gmix_bc,
                op0=ALU.mult, op1=ALU.mult),
                reads=[b_xslot[sl], b_rstd1, b_gmix_bc], writes=[b_hb0[s2]])

        def p0_tr1(tt):
            s2 = tt % 2
            bk = tp_cnt[0] % 3
            tp_cnt[0] += 1
            tps = bank_bf(bk)
            t.group("pe", [lambda e, c=c: e.transpose(
                out=tps[:, c * 128:(c + 1) * 128], in_=hb0[s2][:, c * 128:(c + 1) * 128],
                identity=ident_b) for c in range(8)],
                reads=[b_hb0[s2], b_c["ident_b"]], writes=hb[bk])
            t.op("act", lambda e: e.activation(
                out=hT[:, :, tt * 128:(tt + 1) * 128], in_=tps.rearrange("p (c t) -> p c t", c=8),
                func=AF.Copy),
                reads=hb[bk], writes=[b_hT[tt]])

        b_sq = [Buf("sq%d" % i) for i in range(2)]
        b_x16 = [Buf("x16%d" % i) for i in range(2)]
        b_qtok = [Buf("qtok%d" % i) for i in range(2)]
        b_rt1 = [Buf("rt1%d" % i) for i in range(2)]
        b_rt2 = [Buf("rt2%d" % i) for i in range(2)]
        b_sta = [Buf("sta%d" % i) for i in range(2)]
        b_stb = [Buf("stb%d" % i) for i in range(2)]
        b_qT = [Buf("qT%d" % i) for i in range(NT)]
        b_kT = [Buf("kT%d" % i) for i in range(NT)]
        b_v = [Buf("v%d" % i) for i in range(NT)]
        b_vones = Buf("vones")
        b_kmean = Buf("kmeanT")
        b_gate = Buf("gate")
        b_m8 = Buf("m8")
        b_sel = Buf("sel")
        b_rank = Buf("rank")
        b_btok2 = [Buf("bias8_%d" % i) for i in range(2)]
        b_ksum_ps = hb[3][0]
        b_gate_ps = hb[3][0]
        b_bias_ps = hb[3][0]
        ksum_ps = bank_f(3)[:, 0:128].rearrange("p (a b c) -> p a b c", a=4, b=NT)
        gate_ps = bank_f(3)[:, 128:192].rearrange("p (h n) -> p h n", h=H)
        bias_ps = bank_bf(3)[:, 512:640]
        pj_banks = [4, 5, 6, 7]

        jobs = []
        for cga, cgb in ((CG_K, CG_V), (CG_Q, CG_U)):
            for tt in range(NT):
                jobs.append((cga, tt))
                jobs.append((cgb, tt))
        qk_slot = {}
        for ji, (cg, tt) in enumerate(jobs):
            if cg in (CG_K, CG_Q):
                qk_slot[ji] = len(qk_slot) % 2

        def proj_mm(ji):
            cg, tt = jobs[ji]
            bk = pj_banks[ji % 4]
            fns = []
            for c in range(8):
                fns.append(lambda e, c=c: e.matmul(
                    bank_f(bk), lhsT=hT[:, c, tt * 128:(tt + 1) * 128], rhs=wqkvu[cg][:, c, :],
                    start=(c == 0), stop=(c == 7)))
            t.group("pe", fns, reads=[b_hT[tt], b_wqkvu[cg]], writes=hb[bk])

        def post_a2(ji):
            cg, tt = jobs[ji]
            if cg in (CG_K, CG_Q):
                i2 = qk_slot[ji]
                sq3 = sq_sb[i2].rearrange("p (h d) -> p h d", h=H)
                t.op("dve", lambda e: e.tensor_reduce(out=stat_a[i2], in_=sq3, axis=AX.X, op=ALU.add),
                     reads=[b_sq[i2]], writes=[b_sta[i2]])
                t.op("act", lambda e: e.activation(out=stat_b[i2], in_=stat_a[i2], func=AF.Sqrt,
                                                   scale=1.0 / DH, bias=eps_c[:, 0:1]),
                     reads=[b_sta[i2], b_eps], writes=[b_stb[i2]])

        def post_a1(ji):
            cg, tt = jobs[ji]
            bk = pj_banks[ji % 4]
            if cg in (CG_K, CG_Q):
                i2 = qk_slot[ji]
                t.op("act", lambda e: e.activation(out=sq_sb[i2], in_=bank_f(bk), func=AF.Square),
                     reads=hb[bk], writes=[b_sq[i2]])
            elif cg == CG_V:
                t.op("dve", lambda e: e.tensor_copy(
                    out=v_aug[:, tt, :, 0:64], in_=bank_f(bk).rearrange("p (h d) -> p h d", h=H)),
                    reads=hb[bk] + [b_vones], writes=[b_v[tt]])
            else:
                t.op("act", lambda e: e.activation(out=u_sb[:, tt, :], in_=bank_f(bk), func=AF.Copy),
                     reads=hb[bk], writes=[b_u[tt]])

        def post_b(ji):
            cg, tt = jobs[ji]
            if cg not in (CG_K, CG_Q):
                return
            is_k = cg == CG_K
            bk = pj_banks[ji % 4]
            i2 = qk_slot[ji]
            rp = rope_k if is_k else rope_q
            rb = b_c["ropek"] if is_k else b_c["ropeq"]
            ps3 = bank_f(bk).rearrange("p (h d) -> p h d", h=H)
            qt3 = qtok[i2].rearrange("p (h d) -> p h d", h=H)
            t.op("dve", lambda e: e.reciprocal(out=stat_b[i2], in_=stat_b[i2]),
                 reads=[b_stb[i2]], writes=[b_stb[i2]])
            t.op("dve", lambda e: e.tensor_tensor(
                out=qt3, in0=ps3, in1=stat_b[i2].unsqueeze(2).to_broadcast([128, H, DH]), op=ALU.mult),
                reads=hb[bk] + [b_stb[i2]], writes=[b_qtok[i2]])
            t.op("dve", lambda e: e.tensor_tensor(
                out=x16[i2], in0=ps3[:, :, 0:16], in1=stat_b[i2].unsqueeze(2).to_broadcast([128, H, 16]),
                op=ALU.mult),
                reads=hb[bk] + [b_stb[i2]], writes=[b_x16[i2]])
            cs_b = rp[:, tt, 0:16].unsqueeze(1).to_broadcast([128, H, 16])
            sn_lo = rp[:, tt, 16:24].unsqueeze(1).to_broadcast([128, H, 8])
            sn_hi = rp[:, tt, 24:32].unsqueeze(1).to_broadcast([128, H, 8])
            reng = "pool" if is_k else "dve"
            t.op(reng, lambda e: e.tensor_tensor(out=rt1[i2], in0=x16[i2], in1=cs_b, op=ALU.mult),
                 reads=[b_x16[i2], rb], writes=[b_rt1[i2]])
            t.op(reng, lambda e: e.tensor_tensor(out=rt2[i2][:, :, 0:8], in0=x16[i2][:, :, 8:16], in1=sn_lo,
                                                  op=ALU.mult),
                 reads=[b_x16[i2], rb], writes=[b_rt2[i2]])
            t.op(reng, lambda e: e.tensor_tensor(out=rt2[i2][:, :, 8:16], in0=x16[i2][:, :, 0:8], in1=sn_hi,
                                                  op=ALU.mult),
                 reads=[b_x16[i2], rb, b_rt2[i2]], writes=[b_rt2[i2]])
            t.op(reng, lambda e: e.tensor_tensor(out=qt3[:, :, 0:16], in0=rt1[i2], in1=rt2[i2], op=ALU.add),
                 reads=[b_rt1[i2], b_rt2[i2], b_qtok[i2]], writes=[b_qtok[i2]])

        def post_b_pe(ji, on_dve=False):
            cg, tt = jobs[ji]
            if cg not in (CG_K, CG_Q):
                return
            is_k = cg == CG_K
            i2 = qk_slot[ji]
            tbk = tp_cnt[0] % 3
            tp_cnt[0] += 1
            tps = bank_bf(tbk)[:, 0:512]
            fns = []
            for pr in range(4):
                fns.append(lambda e, pr=pr: e.transpose(
                    out=tps[:, pr * 128:(pr + 1) * 128], in_=qtok[i2][:, pr * 128:(pr + 1) * 128],
                    identity=ident_b))
            wr = [hb[tbk][0]]
            if is_k:
                for pr in range(4):
                    fns.append(lambda e, pr=pr: e.matmul(
                        ksum_ps[:, pr, tt, :], lhsT=qtok[i2][:, pr * 128:(pr + 1) * 128], rhs=ones_b,
                        start=True, stop=True))
                wr.append(b_ksum_ps)
            t.group("pe", fns, reads=[b_qtok[i2], b_c["ident_b"], b_c["ones"]], writes=wr)
            tps3 = tps.rearrange("p (a b) -> p a b", a=4)
            if is_k:
                t.op("act", lambda e: e.activation(out=kT[:, :, tt * 128:(tt + 1) * 128], in_=tps3,
                                                   func=AF.Copy, scale=gcolT_k[:, 0:1]),
                     reads=[hb[tbk][0], b_c["gck"]], writes=[b_kT[tt]])
            elif on_dve:
                for p0_, hs_ in ((0, 0), (64, 1)):
                    t.op("dve", lambda e, p0_=p0_, hs_=hs_: e.tensor_scalar(
                        out=qTz[p0_:p0_ + 64, hs_:H:2, tt * 128:(tt + 1) * 128], in0=tps3[p0_:p0_ + 64],
                        scalar1=gcolT_q[p0_:p0_ + 64, 0:1], scalar2=None, op0=ALU.mult),
                        reads=[hb[tbk][0], b_c["gcq"], b_c["qz"], b_qT[tt]], writes=[b_qT[tt]])
            else:
                t.op("act", lambda e: e.activation(
                    out=qTz[0:64, 0:H:2, tt * 128:(tt + 1) * 128], in_=tps3[0:64],
                    func=AF.Copy, scale=gcolT_q[0:64, 0:1]),
                    reads=[hb[tbk][0], b_c["gcq"], b_c["qz"]], writes=[b_qT[tt]])
                t.op("act", lambda e: e.activation(
                    out=qTz[64:128, 1:H:2, tt * 128:(tt + 1) * 128], in_=tps3[64:128],
                    func=AF.Copy, scale=gcolT_q[64:128, 0:1]),
                    reads=[hb[tbk][0], b_c["gcq"], b_c["qz"], b_qT[tt]], writes=[b_qT[tt]])

        def gate_mm(tt):
            fns = []
            for h in range(H):
                fns.append(lambda e, h=h: e.matmul(
                    gate_ps[:, h, :], lhsT=qTz[:, h, tt * 128:(tt + 1) * 128], rhs=kmeanT[:, h // 2, :],
                    start=True, stop=True))
            t.group("pe", fns, reads=[b_qT[tt], b_kmean], writes=[b_gate_ps])

        def gate_sel(tt):
            qb = tt // 2
            t.op("dve", lambda e: e.memset(gate_sb, -1.0e30), writes=[b_gate])
            t.op("dve", lambda e: e.tensor_copy(out=gate_sb[:, :, 0:qb], in_=gate_ps[:, :, 0:qb]),
                 reads=[b_gate_ps, b_gate], writes=[b_gate])
            g_m = gate_sb.unsqueeze(2).to_broadcast([128, H, 8, 8])
            g_n = gate_sb.unsqueeze(3).to_broadcast([128, H, 8, 8])
            t.op("dve", lambda e: e.tensor_tensor(out=rank_sb, in0=g_m, in1=g_n, op=ALU.is_gt),
                 reads=[b_gate, b_rank], writes=[b_rank])

        def gate_sel_b(tt):
            qb = tt // 2
            t.op("dve", lambda e: e.tensor_reduce(out=m8_sb, in_=rank_sb, axis=AX.X, op=ALU.add),
                 reads=[b_rank, b_m8], writes=[b_m8])
            t.op("dve", lambda e: e.tensor_scalar(out=sel_sb, in0=m8_sb, scalar1=2.5, scalar2=None,
                                                  op0=ALU.is_lt),
                 reads=[b_m8, b_sel], writes=[b_sel])
            b8, b_b8 = bias8[tt % 2], b_btok2[tt % 2]
            t.op("dve", lambda e: e.tensor_scalar(
                out=b8[:, 0:H:2, 64:64 + qb], in0=sel_sb[:, 0:H:2, 0:qb], scalar1=-1.0, scalar2=-NEG,
                op0=ALU.add, op1=ALU.mult),
                reads=[b_sel, b_b8], writes=[b_b8])
            t.op("dve", lambda e: e.tensor_scalar(
                out=b8[:, 1:H:2, 0:qb], in0=sel_sb[:, 1:H:2, 0:qb], scalar1=-1.0, scalar2=-NEG,
                op0=ALU.add, op1=ALU.mult),
                reads=[b_sel, b_b8], writes=[b_b8])

        def gate_tr(tt):
            b8, b_b8 = bias8[tt % 2], b_btok2[tt % 2]
            tbk = 3
            tps = bank_bf(tbk)
            t.group("pe", [lambda e, h=h: e.transpose(out=tps[:, h * 128:(h + 1) * 128], in_=b8[:, h, :],
                                                      identity=ident_b) for h in range(H)],
                    reads=[b_b8, b_c["ident_b"]], writes=[hb[tbk][0]])
            tp3 = tps.rearrange("p (h t) -> p h t", h=H)
            t.op("dve", lambda e: e.tensor_copy(out=qTz[64:72, 0:H:2, tt * 128:(tt + 1) * 128],
                                                in_=tp3[64:72, 0:H:2, :]),
                 reads=[hb[tbk][0], b_c["qz"], b_qT[tt]], writes=[b_qT[tt]])
            t.op("dve", lambda e: e.tensor_copy(out=qTz[0:8, 1:H:2, tt * 128:(tt + 1) * 128],
                                                in_=tp3[0:8, 1:H:2, :]),
                 reads=[hb[tbk][0], b_c["qz"], b_qT[tt]], writes=[b_qT[tt]])

        def kmean_fin():
            ks3 = bank_f(3)[:, 0:128].rearrange("p (a n c) -> p a n c", a=4, n=NB)
            ks3 = ks3[:, :, :, 0:4:2]
            t.op("dve", lambda e: e.tensor_reduce(out=ksum_sb, in_=ks3, axis=AX.X, op=ALU.add),
                 reads=[b_ksum_ps], writes=[b_kmean])
            t.op("dve", lambda e: e.tensor_scalar(out=kmeanT, in0=ksum_sb, scalar1=gcolT_k[:, 0:1],
                                                  scalar2=1.0 / 256.0, op0=ALU.mult, op1=ALU.mult),
                 reads=[b_kmean, b_c["gck"]], writes=[b_kmean])

        b_kT2 = Buf("kT2")

        def make_kT2():
            b_kT2.fence = t.snapshot()
            t.dma("sp", kT2, kT, reads=b_kT, writes=[b_kT2])
            for pr in range(4):
                t.dma("pool", kT[64:72, pr, :], c_ind_d, writes=b_kT)
                t.dma("pool", kT2[0:8, pr, :], c_ind_d, writes=[b_kT2])

        later = []

        def run_later(it):
            k = 0
            while k < len(later):
                if later[k][0] <= it:
                    later.pop(k)[1]()
                else:
                    k += 1

        p0_sched = {}

        def p0_at(it_, fn, *a):
            p0_sched.setdefault(max(it_, -1), []).append((fn, a))

        for tt_ in range(2, NT):
            p0_at(2 * tt_ - 6, p0_sq, tt_)
            p0_at(2 * tt_ - 5, p0_stats, tt_)
            p0_at(2 * tt_ - 4, p0_mul, tt_)
            p0_at(2 * tt_ - 2, p0_tr1, tt_)
        for tt_ in range(8, NT):
            p0_at(max(0, 2 * (tt_ - 8) - 1), p0_load1, tt_)

        p0_load1(0)
        t.dma("sp", gmix_bc, gmix_vec_d, writes=[b_gmix_bc])
        for tt_ in range(1, 4):
            p0_load1(tt_)
        t.wait_written("pool", b_xslot[3])
        t.dma("pool", wqkvu[CG_V], w_in_v[:, :, CG_V * 512:(CG_V + 1) * 512], writes=[b_wqkvu[CG_V]])
        t.op("dve", lambda e: e.memset(v_aug[:, :, :, 64:65], 1.0), writes=[b_vones])
        small_setup_k()
        for tt_ in range(4, 8):
            p0_load1(tt_)
        for tt_ in range(2):
            p0_sq(tt_)
            p0_stats(tt_)
            p0_mul(tt_)
            p0_tr1(tt_)
        small_setup_rest()
        for fn_, a_ in p0_sched.get(-1, []):
            fn_(*a_)
        fence_p0 = None
        b_u = None
        NJ = len(jobs)
        DEFER_PE = [NJ - 4, NJ - 2]
        for it in range(NJ + 6):
            if stop and stop.startswith("it") and it >= int(stop[2:]):
                t.wait_all("sp")
                return nc, list(dbg_d.keys())
            if it == 9:
                zero_fill()
            for fn_, a_ in p0_sched.get(it, []):
                fn_(*a_)
            if it < NJ:
                if it == 2 * NT:
                    fence_p0 = t.snapshot()
                    b_u = [Buf("u%d" % i, fence=fence_p0) for i in range(NT)]

                proj_mm(it)
            if 0 <= it - 1 < NJ:
                post_a1(it - 1)
            if 0 <= it - 2 < NJ:
                post_a2(it - 2)
            if 0 <= it - 3 < NJ:
                post_b(it - 3)
            if 0 <= it - 5 < NJ:
                jb = it - 5
                if jb not in DEFER_PE:
                    post_b_pe(jb)
                cg, tt = jobs[jb]
                if cg == CG_K and tt == NT - 1:
                    later.append((it + 2, kmean_fin))
                if cg == CG_V and tt == NT - 1:
                    later.append((it + 2, make_kT2))
            run_later(it)
        run_later(10 ** 9)
        dbg_dump("hT", hT, [128, 8, S], b_hT)
        dbg_dump("qTz", qTz, [128, H, S], b_qT)
        dbg_dump("kT", kT, [128, 4, S], b_kT)
        dbg_dump("v", v_aug, [128, NT, H, 65], b_v)
        dbg_dump("u", u_sb, [128, NT, 512], b_u)
        dbg_dump("kmeanT", kmeanT, [128, 4, 8], [b_kmean])
        fence_p1 = t.snapshot()
        hb[3][0].fence = dict(fence_p1)
        hb[3][1].fence = dict(fence_p1)
        if stop == "p1":
            t.wait_all("sp")
            return nc, list(dbg_d.keys())

        b_wsl = [[Buf("wsl%d_%d" % (s, j), fence=fence_p1) for j in range(4)] for s in range(2)]
        w_ua_v = w_ua_d.rearrange("(c p) n -> p c n", p=128)
        w_up_v = w_up_d.rearrange("(c p) n -> p c n", p=128)

        def load_slices(fc):
            s = fc % 2
            t.dma("pool", wga[s], w_in_v[:, :, 2048 + fc * 128: 2048 + (fc + 1) * 128], writes=[b_wsl[s][0]])
            t.dma("pool", wgp[s], w_in_v[:, :, 3072 + fc * 128: 3072 + (fc + 1) * 128], writes=[b_wsl[s][1]])
            t.dma("pool", wua[s], w_ua_v[:, :, fc * 128:(fc + 1) * 128], writes=[b_wsl[s][2]])
            t.dma("pool", wup[s], w_up_v[:, :, fc * 128:(fc + 1) * 128], writes=[b_wsl[s][3]])

        load_slices(0)

        b_pooled = [Buf("pooled%d" % i, fence=fence_p0) for i in range(4)]
        b_yT = [Buf("yT%d" % i, fence=fence_p1) for i in range(4)]
        def p3_band(tt, bk):
            fns = []
            for g in range(4):
                kind = 2 if tt == 0 else 0
                fns.append(lambda e, g=g, kind=kind: e.matmul(
                    bank_f(bk)[:, g * 128:(g + 1) * 128], lhsT=u_sb[:, tt, g * 128:(g + 1) * 128],
                    rhs=band[:, g * 3 + kind, :], start=True, stop=(tt == 0)))
                if tt > 0:
                    fns.append(lambda e, g=g: e.matmul(
                        bank_f(bk)[:, g * 128:(g + 1) * 128], lhsT=u_sb[:, tt - 1, g * 128:(g + 1) * 128],
                        rhs=band[:, g * 3 + 1, :], start=False, stop=True))
            rd = [b_u[tt], b_c["band"]] + ([b_u[tt - 1]] if tt > 0 else [])
            t.group("pe", fns, reads=rd, writes=hb[bk])
            t.op("dve", lambda e: e.tensor_copy(
                out=pooledT[:, :, tt * 128:(tt + 1) * 128],
                in_=bank_f(bk).rearrange("p (g t) -> p g t", g=4)),
                reads=hb[bk], writes=[b_pooled[tt // 4]])

        def p3_y(tg, g, bk):
            t.group("pe", [lambda e: e.matmul(
                bank_f(bk), lhsT=wgrp[:, g, :], rhs=pooledT[:, g, tg * 512:(tg + 1) * 512],
                start=True, stop=True)],
                reads=[b_pooled[tg], b_c["wgrp"]], writes=hb[bk])
            t.op("dve", lambda e: e.tensor_scalar(
                out=yT[:, g, tg * 512:(tg + 1) * 512], in0=bank_f(bk), scalar1=psc_c[:, g:g + 1],
                scalar2=None, op0=ALU.mult),
                reads=hb[bk] + [b_c["psc"]], writes=[b_yT[tg]])


        NPT = 5
        b_PT = [Buf("PT%d" % i, fence=fence_p1) for i in range(NPT)]
        b_otok = [Buf("otok%d" % i, fence=fence_p1) for i in range(2)]
        b_rden = [Buf("rden%d" % i) for i in range(4)]
        b_oT = [Buf("oT%d" % i, fence=fence_p1) for i in range(NT)]
        st_banks = [0, 1, 2]
        MISC = 3
        accb = [[4, 5], [6, 7]]

        steps = []
        for qb in range(NB):
            for h in range(H):
                for kp in range(qb + 1):
                    steps.append((qb, h, kp))
        LOOK = 3
        pend_post = []

        def emit_qk(si):
            qb, h, kp = steps[si]
            pr = h // 2
            sb_ = st_banks[si % 3]
            q0 = qb * 256
            ksel = kT if h % 2 == 0 else kT2
            fns = []
            for j in range(2):
                kt = 2 * kp + j
                fns.append(lambda e, kt=kt, j=j: e.matmul(
                    bank_f(sb_)[:, j * 256:(j + 1) * 256], lhsT=ksel[:, pr, kt * 128:(kt + 1) * 128],
                    rhs=qTz[:, h, q0:q0 + 256], start=True, stop=True))
            rd = [b_kT[2 * kp], b_kT[2 * kp + 1], b_qT[2 * qb], b_qT[2 * qb + 1]] + ([b_kT2] if h % 2 else [])
            t.group("pe", fns, reads=rd, writes=hb[sb_])
            pt = PT[si % NPT]
            t.op("act", lambda e: e.activation(out=pt, in_=bank_f(sb_), func=AF.Exp, scale=0.125),
                 reads=hb[sb_], writes=[b_PT[si % NPT]])
            if kp == qb and (si < 80 or si in gate_busy):
                dg = pt.rearrange("p (a c) -> p a c", c=128)[:, 0:4:3, :]
                t.op("pool", lambda e: e.affine_select(
                    out=dg, in_=dg, pattern=[[0, 2], [1, 128]],
                    compare_op=ALU.is_ge, fill=0.0, base=0, channel_multiplier=-1),
                    reads=[b_PT[si % NPT]], writes=[b_PT[si % NPT]])
            elif kp == qb:
                dg = pt.rearrange("p (a c) -> p a c", c=128)[:, 0:4:3, :]
                t.op("dve", lambda e: e.tensor_tensor(
                    out=dg, in0=dg, in1=tri01.unsqueeze(1).to_broadcast([128, 2, 128]), op=ALU.mult),
                    reads=[b_PT[si % NPT], b_tri], writes=[b_PT[si % NPT]])

        def emit_pv(si):
            qb, h, kp = steps[si]
            half, hh = h // 4, h % 4
            pt = PT[si % NPT]
            for j in range(2):
                kt = 2 * kp + j
                for ql in range(2):
                    if ql == 0 and kt == 2 * qb + 1:
                        continue
                    last = (2 * qb) if ql == 0 else (2 * qb + 1)
                    bk = accb[ql][half]
                    dst = bank_f(bk)[:, hh * 65:(hh + 1) * 65]
                    c0_ = j * 256 + ql * 128
                    t.group("pe", [lambda e: e.matmul(
                        dst, lhsT=pt[:, c0_:c0_ + 128], rhs=v_aug[:, kt, h, :],
                        start=(kt == 0), stop=(kt == last))],
                        reads=[b_PT[si % NPT], b_v[kt]], writes=hb[bk])
            if kp == qb and hh == 3:
                for ql in range(2):
                    bk = accb[ql][half]
                    acc3 = bank_f(bk)[:, 0:260].rearrange("p (h d) -> p h d", h=4)
                    ri = ql * 2 + half
                    t.op("dve", lambda e, acc3=acc3, ri=ri: e.reciprocal(out=rden[ri], in_=acc3[:, :, 64]),
                         reads=hb[bk], writes=[b_rden[ri]])
                    ot3 = o_tok[ql][:, half * 256:(half + 1) * 256].rearrange("p (h d) -> p h d", h=4)
                    t.op("dve", lambda e, acc3=acc3, ri=ri, ot3=ot3: e.tensor_tensor(
                        out=ot3, in0=acc3[:, :, 0:64],
                        in1=rden[ri].unsqueeze(2).to_broadcast([128, 4, 64]), op=ALU.mult),
                        reads=hb[bk] + [b_rden[ri]], writes=[b_otok[ql]])
                if half == 1:
                    def post(ql, qb=qb):
                        if True:
                            tt = 2 * qb + ql
                            tps = bank_bf(MISC)[:, 0:512]
                            t.group("pe", [lambda e, pr=pr_, ql=ql: e.transpose(
                                out=tps[:, pr * 128:(pr + 1) * 128],
                                in_=o_tok[ql][:, pr * 128:(pr + 1) * 128], identity=ident_b)
                                for pr_ in range(4)],
                                reads=[b_otok[ql], b_c["ident_b"]], writes=hb[MISC])
                            t.op("dve", lambda e, tt=tt: e.tensor_copy(
                                out=o_attnT[:, :, tt * 128:(tt + 1) * 128],
                                in_=tps.rearrange("p (a b) -> p a b", a=4)),
                                reads=hb[MISC], writes=[b_oT[tt]])
                    pend_post.append((si + LOOK + 2, lambda post=post: post(0)))
                    pend_post.append((si + LOOK + 5, lambda post=post: post(1)))

        for b_ in (b_btok2[0], b_btok2[1], b_gate, b_m8, b_sel, b_rank):
            b_.fence = dict(fence_p1)
        for i in range(2):
            t.op("dve", lambda e, i=i: e.memset(bias8[i], 0.0), writes=[b_btok2[i]])
        gate_at = {}
        gate_busy = set()
        P3_START = 130
        misc_used = set()
        for i_, (qb_, h_, kp_) in enumerate(steps):
            if h_ == 0 and kp_ == 0 and i_ > 0:
                misc_used.update((i_ + LOOK + 1, i_ + LOOK + 4))

        def misc_free(x_):
            return all(abs(x_ - u_) > 2 for u_ in misc_used)

        s_ = 1
        tr_prev = []
        for k in range(8):
            if k == 4:
                s_ = max(s_, 82)
            while True:
                if misc_free(s_) and (k < 2 or s_ > tr_prev[k - 2]):
                    tr_ = next((s_ + d_ for d_ in range(10, 18) if misc_free(s_ + d_)), None)
                    if tr_ is not None:
                        break
                s_ += 1
            assert tr_ < (80 if k < 4 else P3_START - 2), (k, s_, tr_)
            gate_at.setdefault(s_, []).append(("sel", 8 + k))
            gate_at.setdefault(s_ + 3, []).append(("selb", 8 + k))
            gate_busy.update(range(s_ - 1, s_ + 7))
            gate_at.setdefault(tr_, []).append(("tr", 8 + k))
            misc_used.update((s_, tr_))
            tr_prev.append(tr_)
            s_ += 3
        p3_at = {}
        bounds_ = [len(steps)]
        for i_, (qb_, h_, kp_) in enumerate(steps):
            if h_ == 0 and kp_ == 0:
                bounds_.append(i_)
        slots_ = []
        s_ = P3_START + 2
        while len(slots_) < 32:
            if all(not (b_ + 2 <= s_ <= b_ + 10) for b_ in bounds_):
                slots_.append(s_)
                s_ += 4
            else:
                s_ += 1
        assert slots_[-1] < len(steps) - 4, slots_
        for k in range(NT):
            p3_at[slots_[k]] = ("band", k)
        for k in range(16):
            p3_at[slots_[NT + k]] = ("y", k // 4, k % 4)
        for si in range(len(steps) + LOOK):
            if si == P3_START:
                f80 = t.snapshot()
                for b_ in b_pooled + b_yT:
                    b_.fence = dict(f80)
            if si in (3, 6):
                post_b_pe(DEFER_PE[0] if si == 3 else DEFER_PE[1], on_dve=True)
            if si < len(steps):
                emit_qk(si)
            if si in p3_at:
                a_ = p3_at[si]
                if a_[0] == "band":
                    p3_band(a_[1], MISC)
                else:
                    p3_y(a_[1], a_[2], MISC)
            for kind, gt in gate_at.get(si, []):
                if kind == "sel":
                    gate_mm(gt)
                    gate_sel(gt)
                elif kind == "selb":
                    gate_sel_b(gt)
                else:
                    gate_tr(gt)
            if si - LOOK >= 0:
                emit_pv(si - LOOK)
            while pend_post and pend_post[0][0] <= si:
                pend_post.pop(0)[1]()
        while pend_post:
            pend_post.pop(0)[1]()
        dbg_dump("o_attnT", o_attnT, [128, 4, S], b_oT)
        dbg_dump("yT", yT, [128, 4, S], b_yT)
        fence_p2 = t.snapshot()
        if stop == "p2":
            t.wait_all("sp")
            return nc, list(dbg_d.keys())
        b_wout = [Buf("wout%d" % i, fence=fence_p2) for i in range(2)]
        w_out_v = w_out_d.rearrange("(c p) n -> p c n", p=128)
        for hf in range(2):
            t.dma("pool", wout[:, :, hf * 512:(hf + 1) * 512], w_out_v[:, :, hf * 512:(hf + 1) * 512],
                  writes=[b_wout[hf]])

        b_ga = [Buf("ga%d" % s, fence=fence_p2) for s in range(2)]
        b_gp = [Buf("gp%d" % s, fence=fence_p2) for s in range(2)]
        b_mT = [Buf("mT%d" % i, fence=fence_p2) for i in range(4)]
        for j in range(4):
            b_wsl[1][j].fence = dict(fence_p2)
        it = 0
        p4_mid = {}
        for fc in range(8):
            if fc == 3:
                p4_mid = {"dve": t.cnt["dve"]}
            if fc + 1 < 8:
                load_slices(fc + 1)
            s = fc % 2
            for tg in range(4):
                T0 = tg * 512
                pb = (it % 2) * 4
                i2 = it % 2
                it += 1
                t.group("pe", [lambda e, c=c: e.matmul(bank_f(pb + 0), lhsT=wga[s][:, c, :],
                                                        rhs=hT[:, c, T0:T0 + 512], start=(c == 0), stop=(c == 7))
                               for c in range(8)],
                        reads=[b_wsl[s][0]] + b_hT[tg * 4:(tg + 1) * 4], writes=hb[pb + 0])
                t.group("pe", [lambda e, c=c: e.matmul(bank_f(pb + 1), lhsT=wgp[s][:, c, :],
                                                        rhs=hT[:, c, T0:T0 + 512], start=(c == 0), stop=(c == 7))
                               for c in range(8)],
                        reads=[b_wsl[s][1]] + b_hT[tg * 4:(tg + 1) * 4], writes=hb[pb + 1])
                t.group("pe", [lambda e, c=c: e.matmul(bank_f(pb + 2), lhsT=wua[s][:, c, :],
                                                        rhs=o_attnT[:, c, T0:T0 + 512], start=(c == 0), stop=(c == 3))
                               for c in range(4)],
                        reads=[b_wsl[s][2]] + b_oT[tg * 4:(tg + 1) * 4], writes=hb[pb + 2])
                t.group("pe", [lambda e, c=c: e.matmul(bank_f(pb + 3), lhsT=wup[s][:, c, :],
                                                        rhs=yT[:, c, T0:T0 + 512], start=(c == 0), stop=(c == 3))
                               for c in range(4)],
                        reads=[b_wsl[s][3], b_yT[tg]], writes=hb[pb + 3])
                t.op("act", lambda e: e.activation(out=ga_sb[i2], in_=bank_f(pb + 0), func=AF.Sigmoid),
                     reads=hb[pb + 0], writes=[b_ga[i2]])
                t.op("act", lambda e: e.activation(out=gp_sb[i2], in_=bank_f(pb + 1), func=AF.Sigmoid),
                     reads=hb[pb + 1], writes=[b_gp[i2]])
                t.op("dve", lambda e: e.tensor_tensor(out=ga_sb[i2], in0=bank_f(pb + 2), in1=ga_sb[i2],
                                                      op=ALU.mult),
                     reads=hb[pb + 2] + [b_ga[i2]], writes=[b_ga[i2]])
                t.op("dve", lambda e: e.tensor_tensor(out=gp_sb[i2], in0=bank_f(pb + 3), in1=gp_sb[i2],
                                                      op=ALU.mult),
                     reads=hb[pb + 3] + [b_gp[i2]], writes=[b_gp[i2]])
                t.op("dve", lambda e: e.tensor_tensor(out=mT[:, fc, T0:T0 + 512], in0=ga_sb[i2],
                                                      in1=gp_sb[i2], op=ALU.add),
                     reads=[b_ga[i2], b_gp[i2]], writes=[b_mT[tg]])
        dbg_dump("mT", mT, [128, 8, S], b_mT)
        fence_p4 = t.snapshot()

        w_ff1_v = w_ff1_d.rearrange("(c p) n -> p c n", p=128)
        w_ff2_v = w_ff2_d.rearrange("(j p) n -> p j n", p=128)
        b_ffw1 = [[Buf("ffw1_%d_%d" % (s, i)) for i in range(2)] for s in range(2)]
        b_ffw2 = [[Buf("ffw2_%d_%d" % (s, i)) for i in range(2)] for s in range(2)]

        def load_ff(fq):
            s = fq % 2
            for hf in range(2):
                t.dma("pool", ffw1[s][:, :, hf * 512:(hf + 1) * 512],
                      w_ff1_v[:, :, fq * 1024 + hf * 512: fq * 1024 + (hf + 1) * 512],
                      writes=[b_ffw1[s][hf]])
            for hf in range(2):
                t.dma("pool", ffw2[s][:, :, hf * 512:(hf + 1) * 512],
                      w_ff2_v[:, fq * 8:(fq + 1) * 8, hf * 512:(hf + 1) * 512],
                      writes=[b_ffw2[s][hf]])

        for hf in range(2):
            b_ffw1[0][hf].fence = dict(fence_p4)
            b_ffw2[0][hf].fence = dict(fence_p4)
        load_ff(0)

        b_x1 = [Buf("x1_%d" % i, fence=(fence_p4 if i < 4 else fence_p2)) for i in range(NT)]
        b_xt5 = [Buf("xt5_%d" % i, fence=fence_p2) for i in range(2)]
        b_ss2 = Buf("ss2")
        b_r2 = Buf("r2sq")
        b_junk2 = Buf("junk2")
        b_ss2h = Buf("ss2h")
        b_h2T = [Buf("h2T%d" % i, fence=fence_p4) for i in range(4)]
        p5_banks = [0, 1, 2, 3]
        p5_i = 0

        p5_bk = {}

        def p5_mm(tt):
            nonlocal p5_i
            for hf in range(2):
                bk = p5_banks[p5_i % 4]
                p5_i += 1
                p5_bk[(tt, hf)] = bk
                t.group("pe", [lambda e, c=c, bk=bk, hf=hf: e.matmul(
                    bank_f(bk), lhsT=mT[:, c, tt * 128:(tt + 1) * 128],
                    rhs=wout[:, c, hf * 512:(hf + 1) * 512], start=(c == 0), stop=(c == 7))
                    for c in range(8)],
                    reads=[b_mT[tt // 4], b_wout[hf]], writes=hb[bk])

        def p5_add(tt):
            for hf in range(2):
                bk = p5_bk[(tt, hf)]
                dst = x1[:, tt, hf * 512:(hf + 1) * 512]
                t.op("dve", lambda e, bk=bk, dst=dst: e.tensor_tensor(
                    out=dst, in0=bank_f(bk), in1=dst, op=ALU.add),
                    reads=hb[bk] + [b_x1[tt]], writes=[b_x1[tt]])

        p6_banks = [4, 5, 6, 7]
        p6_i = 0

        b_gbc = Buf("gmlp_bc", fence=fence_p4)
        b_hb16 = [Buf("hb16_%d" % i, fence=fence_p4) for i in range(2)]

        p6_slot = {}

        def p6_mul(tt):
            nonlocal p6_i
            s2 = p6_i % 2
            p6_slot[tt] = (s2, p6_banks[p6_i % 4])
            p6_i += 1
            t.op("dve", lambda e: e.tensor_tensor(out=hb16[s2], in0=x1[:, tt, :], in1=gmlp_bc, op=ALU.mult),
                 reads=[b_x1[tt], b_gbc], writes=[b_hb16[s2]])

        def p6_tr(tt):
            s2, bk = p6_slot[tt]
            tps = bank_bf(bk)
            t.group("pe", [lambda e, c=c: e.transpose(
                out=tps[:, c * 128:(c + 1) * 128], in_=hb16[s2][:, c * 128:(c + 1) * 128],
                identity=ident_b) for c in range(8)],
                reads=[b_hb16[s2], b_c["ident_b"]], writes=hb[bk])
            t.op("act", lambda e: e.activation(
                out=hT[:, :, tt * 128:(tt + 1) * 128], in_=tps.rearrange("p (c t) -> p c t", c=8),
                func=AF.Copy),
                reads=hb[bk], writes=[b_h2T[tt // 4]])

        def p6_sq(tg, k):
            tt, hf = tg * 4 + k // 2, k % 2
            t.op("act", lambda e: e.activation(
                out=junk_h, in_=x1[:, tt, hf * 512:(hf + 1) * 512], func=AF.Square,
                accum_out=ss2h[:, hf * NT + tt: hf * NT + tt + 1]),
                reads=[b_x1[tt]], writes=[b_junk2, b_ss2h])

        def p6_stats(tg):
            c0, c1 = tg * 4, tg * 4 + 4
            t.op("dve", lambda e: e.tensor_tensor(out=ss2[:, c0:c1], in0=ss2h[:, c0:c1],
                                                  in1=ss2h[:, NT + c0:NT + c1], op=ALU.add),
                 reads=[b_ss2h], writes=[b_ss2])
            t.op("dve", lambda e: e.tensor_scalar(out=r2sq[:, c0:c1], in0=ss2[:, c0:c1], scalar1=1.0 / D,
                                                  scalar2=EPS, op0=ALU.mult, op1=ALU.add),
                 reads=[b_ss2], writes=[b_r2])
            t.op("dve", lambda e: e.reciprocal(out=r2sq[:, c0:c1], in_=r2sq[:, c0:c1]),
                 reads=[b_r2], writes=[b_r2])

        P5_ORDER = list(range(4, NT)) + list(range(4))
        t._wait("sp", p4_mid)
        for tt in P5_ORDER:
            if tt == 0:
                t.dma("sp", gmlp_bc, gmlp_vec_d, writes=[b_gbc])
            t.dma("sp", x1[:, tt, :], x_d[tt * 128:(tt + 1) * 128, :], writes=[b_x1[tt]])
        for k in range(NT + 1):
            if k < NT:
                p5_mm(P5_ORDER[k])
            if k >= 1:
                p6_mul(P5_ORDER[k - 1])
            if k < NT:
                p5_add(P5_ORDER[k])
            if k >= 1:
                p6_tr(P5_ORDER[k - 1])
        fence_p5 = t.snapshot()
        dbg_dump("x1", x1, [128, NT, D], b_x1)
        fence_p6 = t.snapshot()
        for hf in range(2):
            b_ffw1[1][hf].fence = dict(fence_p6)
            b_ffw2[1][hf].fence = dict(fence_p6)
        load_ff(1)

        b_a1T = [Buf("a1T%d" % s, fence=fence_p5) for s in range(2)]
        b_sqz = [Buf("sqz%d" % s, fence=fence_p5) for s in range(2)]
        f1_banks = [0, 1, 2, 3]
        f2_banks = [4, 5, 6, 7]
        f1_i = 0
        f2_i = 0
        sq_i = 0

        def ff1(fq, tg, ab):
            nonlocal f1_i, sq_i
            s = fq % 2
            for j in range(8):
                bk = f1_banks[f1_i % 4]
                f1_i += 1
                hf = j // 4
                t.group("pe", [lambda e, c=c, j=j, bk=bk: e.matmul(
                    bank_f(bk), lhsT=ffw1[s][:, c, j * 128:(j + 1) * 128],
                    rhs=hT[:, c, tg * 512:(tg + 1) * 512], start=(c == 0), stop=(c == 7))
                    for c in range(8)],
                    reads=[b_ffw1[s][hf], b_h2T[tg]], writes=hb[bk])
                si_ = sq_i % 2
                sq_i += 1
                t.op("act", lambda e, bk=bk, si_=si_: e.activation(out=sqz[si_], in_=bank_f(bk), func=AF.Square),
                     reads=hb[bk], writes=[b_sqz[si_]])
                t.op("dve", lambda e, bk=bk, si_=si_, j=j: e.scalar_tensor_tensor(
                    out=a1T[ab][:, j, :], in0=bank_f(bk), scalar=0.0, in1=sqz[si_],
                    op0=ALU.is_gt, op1=ALU.mult),
                    reads=hb[bk] + [b_sqz[si_]], writes=[b_a1T[ab]])
                if fq == 0:
                    p6_sq(tg, j)

        out_bufs = []

        def ff2(fq, tg, ab):
            nonlocal f2_i
            s = fq % 2
            for i in range(4):
                tt = tg * 4 + i
                for hf in range(2):
                    bk = f2_banks[f2_i % 4]
                    f2_i += 1
                    t.group("pe", [lambda e, j=j, bk=bk, hf=hf, i=i: e.matmul(
                        bank_f(bk), lhsT=a1T[ab][:, j, i * 128:(i + 1) * 128],
                        rhs=ffw2[s][:, j, hf * 512:(hf + 1) * 512], start=(j == 0), stop=(j == 7))
                        for j in range(8)],
                        reads=[b_a1T[ab], b_ffw2[s][hf]], writes=hb[bk])
                    dst = x1[:, tt, hf * 512:(hf + 1) * 512]
                    t.op("dve", lambda e, bk=bk, dst=dst, tt=tt: e.scalar_tensor_tensor(
                        out=dst, in0=bank_f(bk), scalar=r2sq[:, tt:tt + 1], in1=dst,
                        op0=ALU.mult, op1=ALU.add),
                        reads=hb[bk] + [b_r2, b_x1[tt]], writes=[b_x1[tt]])
                if fq == 3:
                    t.dma("sp", out_d[tt * 128:(tt + 1) * 128, :], x1[:, tt, :], reads=[b_x1[tt]])

        ab_i = 0
        pending = None
        for fq in range(4):
            if fq >= 1 and fq + 1 < 4:
                pass
            for tg in (1, 2, 3, 0):
                ab = ab_i % 2
                ab_i += 1
                ff1(fq, tg, ab)
                if fq == 0:
                    p6_stats(tg)
                if pending is not None:
                    ff2(*pending)
                pending = (fq, tg, ab)
            if fq + 2 < 4:
                ff2(*pending)
                pending = None
                load_ff(fq + 2)
        ff2(*pending)

        t.wait_all("sp")
    return nc, list(dbg_d.keys())


def _host_consts():
    ident = np.eye(128, dtype=np.float32)
    half = 8
    inv = (500000.0 ** (-np.arange(half, dtype=np.float32) / half)).astype(np.float32)
    ang = np.arange(S, dtype=np.float32)[:, None] * inv[None, :]
    cos = np.cos(ang).astype(np.float32)
    sin = np.sin(ang).astype(np.float32)
    rope = np.concatenate([cos, cos, -sin, sin], axis=1).astype(np.float32)
    rope = np.ascontiguousarray(rope.reshape(NT, 128, 32).transpose(1, 0, 2).reshape(128, NT * 32))
    band = np.zeros((128, 12, 128), dtype=np.float32)
    tp = np.arange(128)[:, None]
    tq = np.arange(128)[None, :]
    for g, w in enumerate((2, 4, 8, 16)):
        inwin = (tp <= tq) & (tp > tq - w)
        main = np.where(inwin, 1.0 / w, 0.0) - np.eye(128)
        prev = np.where((tp - 128) > (tq - w), 1.0 / w, 0.0)
        cnt = np.minimum(tq + 1, w).astype(np.float64)
        main0 = np.where(inwin, 1.0 / cnt, 0.0) - np.eye(128)
        band[:, g * 3 + 0, :] = main
        band[:, g * 3 + 1, :] = prev
        band[:, g * 3 + 2, :] = main0
    return ident, rope, band


_CACHE = {}


def kernel(x, norm_mix, w_in, q_norm, k_norm, w_pool_grp, pool_scale, w_up_attn, w_up_pool,
           w_out, norm_mlp, w_ff1, w_ff2, _debug=(), _cores=N_CORES, _stop=None):
    f = lambda a: np.ascontiguousarray(np.asarray(a, dtype=np.float32))
    x = f(x)
    ident, rope, band = _host_consts()
    gq, gk = f(q_norm)[0], f(k_norm)[0]
    small = np.zeros((128, 24), dtype=np.float32)
    small[:, 0:8] = f(norm_mix)[0].reshape(8, 128).T
    small[:, 8:16] = f(norm_mlp)[0].reshape(8, 128).T
    small[:, 16:20] = f(pool_scale)[0].reshape(4, 128).T
    small[:, 20] = np.concatenate([gq, gq])
    small[:, 21] = np.concatenate([gk, gk])
    gv = np.concatenate([gq[0:16], gq[8:16], gq[0:8], gk[0:16], gk[8:16], gk[0:8]]).astype(np.float32)
    shared = {
        "w_in": f(w_in)[0], "w_pool_grp": f(w_pool_grp)[0], "w_up_attn": f(w_up_attn)[0],
        "w_up_pool": f(w_up_pool)[0], "w_out": f(w_out)[0], "w_ff1": f(w_ff1)[0], "w_ff2": f(w_ff2)[0],
        "c_small": small, "c_gv": gv, "gmlp_bc": np.ascontiguousarray(np.broadcast_to(f(norm_mlp)[0], (128, D))),
        "gmix_bc": np.ascontiguousarray(np.broadcast_to(f(norm_mix)[0], (128, D))),
        "c_ident": ident, "c_rope": rope, "c_band": band,
        "c_zero": np.zeros((64, 1024), dtype=np.float32),
        "c_ind": (np.arange(S)[None, :] // 256 == np.arange(8)[:, None]).astype(np.float32),
    }
    key = (tuple(_debug), _stop)
    if key not in _CACHE:
        _CACHE[key] = build_nc(debug=_debug, stop=_stop)
    nc, dbg_names = _CACHE[key]
    in_maps = []
    for b in range(_cores):
        m = dict(shared)
        m["x"] = x[b]
        in_maps.append(m)
    res = run_bass_kernel_spmd(nc, in_maps, core_ids=list(range(_cores)))
    out = np.stack([res.results[b]["out"] for b in range(_cores)], axis=0)
    if _debug:
        return out, {n: res.results[0]["dbg_" + n] for n in dbg_names}
    return out
```

```python
import numpy as np
from contextlib import ExitStack

import concourse.bass as bass
import concourse.mybir as mybir
from concourse.bass_utils import run_bass_kernel_spmd

F32 = mybir.dt.float32
BF16 = mybir.dt.bfloat16
AF = mybir.ActivationFunctionType
ALU = mybir.AluOpType
AX = mybir.AxisListType

S = 2048
D = 1024
NT = 16
H = 8
DH = 64
NB = 8
NEG = -240000.0
EPS = 1e-6
N_CORES = 8


class Buf:
    __slots__ = ("name", "w", "r", "fence", "excl")

    def __init__(self, name, fence=None, excl=False):
        self.name = name
        self.excl = excl
        self.w = None
        self.r = {}
        self.fence = dict(fence) if fence else None


class Trk:
    def __init__(self, nc, es, n_ring=12):
        self.nc = nc
        self.E = {"pe": nc.tensor, "act": nc.scalar, "dve": nc.vector,
                  "pool": nc.gpsimd, "sp": nc.sync}
        self.sems = {}
        for k in self.E:
            self.sems[k] = es.enter_context(nc.semaphore("s_" + k))
        self.cnt = {k: 0 for k in self.E}
        self.seen = {k: {} for k in self.E}
        self.rings = {}
        for q in ("sp", "pool"):
            slots = []
            for j in range(n_ring):
                key = "d_%s%d" % (q, j)
                self.sems[key] = es.enter_context(nc.semaphore(key))
                slots.append([key, 0])
            self.rings[q] = {"i": 0, "slots": slots}

    def snapshot(self):
        snap = {k: v for k, v in self.cnt.items() if v > 0}
        for q in self.rings.values():
            for key, tot in q["slots"]:
                if tot > 0:
                    snap[key] = tot
        return snap

    @staticmethod
    def _add(need, k, v):
        if need.get(k, 0) < v:
            need[k] = v

    def _need(self, reads, writes, e=None):
        need = {}
        for b in reads:
            if b.w:
                self._add(need, *b.w)
            if b.excl:
                for k, v in b.r.items():
                    if k != e:
                        self._add(need, k, v)
            if b.fence:
                for k, v in b.fence.items():
                    self._add(need, k, v)
        for b in writes:
            if b.w:
                self._add(need, *b.w)
            for k, v in b.r.items():
                self._add(need, k, v)
            if b.fence:
                for k, v in b.fence.items():
                    self._add(need, k, v)
        return need

    def _wait(self, e, need):
        eng = self.E[e]
        seen = self.seen[e]
        for k, v in need.items():
            if e == "pe" and k == "pe":
                continue
            if seen.get(k, 0) < v:
                eng.wait_ge(self.sems[k], v)
                seen[k] = v

    def _mark(self, key, val, reads, writes):
        for b in reads:
            if b.r.get(key, 0) < val:
                b.r[key] = val
        for b in writes:
            b.w = (key, val)
            b.r = {}

    def op(self, e, fn, reads=(), writes=()):
        self._wait(e, self._need(reads, writes, e))
        ins = fn(self.E[e])
        self.cnt[e] += 1
        ins.then_inc(self.sems[e], 1)
        self._mark(e, self.cnt[e], reads, writes)

    def group(self, e, fns, reads=(), writes=()):
        self._wait(e, self._need(reads, writes, e))
        ins = None
        for fn in fns:
            ins = fn(self.E[e])
        self.cnt[e] += 1
        ins.then_inc(self.sems[e], 1)
        self._mark(e, self.cnt[e], reads, writes)

    def dma(self, q, out, in_, reads=(), writes=()):
        need = self._need(reads, writes)
        ring = self.rings[q]
        slot = ring["slots"][ring["i"] % len(ring["slots"])]
        ring["i"] += 1
        if slot[1] > 0:
            self._add(need, slot[0], slot[1])
        self._wait(q, need)
        ins = self.E[q].dma_start(out=out, in_=in_)
        slot[1] += 16
        ins.then_inc(self.sems[slot[0]], 16)
        self._mark(slot[0], slot[1], reads, writes)

    def wait_all(self, e):
        self._wait(e, self.snapshot())

    def wait_written(self, e, buf):
        if buf.w:
            self._wait(e, {buf.w[0]: buf.w[1]})


def build_nc(debug=(), stop=None):
    nc = bass.Bass("TRN2", target_bir_lowering=False)

    def din(name, shape):
        return nc.dram_tensor(name, list(shape), F32, kind="ExternalInput").ap()

    x_d = din("x", [S, D])
    w_in_d = din("w_in", [D, 4096])
    w_grp_d = din("w_pool_grp", [4, 128, 128])
    w_ua_d = din("w_up_attn", [512, D])
    w_up_d = din("w_up_pool", [512, D])
    w_out_d = din("w_out", [D, D])
    w_ff1_d = din("w_ff1", [D, 4096])
    w_ff2_d = din("w_ff2", [4096, D])
    c_small_d = din("c_small", [128, 24])
    gmlp_vec_d = din("gmlp_bc", [128, D])
    gmix_vec_d = din("gmix_bc", [128, D])
    c_gv_d = din("c_gv", [64])
    c_ident_d = din("c_ident", [128, 128])
    c_rope_d = din("c_rope", [128, NT * 32])
    c_band_d = din("c_band", [128, 12, 128])
    c_zero_d = din("c_zero", [64, 1024])
    c_ind_d = din("c_ind", [8, S])
    out_d = nc.dram_tensor("out", [S, D], F32, kind="ExternalOutput").ap()
    dbg_d = {}

    es = ExitStack()
    with es:
        RAWB = 103 * 2048 + 256
        raw = nc.alloc_sbuf_tensor("raw", [128, RAWB // 2], BF16)

        def view(off, shape, dt, p0=0):
            esz = 2 if dt == BF16 else 4
            n = 1
            for s_ in shape[1:]:
                n *= s_
            assert off % 4 == 0 and off + n * esz <= RAWB, (off, shape)
            v = raw[p0:p0 + shape[0], off // 2: off // 2 + n * esz // 2]
            if dt != BF16:
                v = v.bitcast(dt)
            if len(shape) == 3:
                v = v.rearrange("p (a b) -> p a b", a=shape[1])
            elif len(shape) == 4:
                v = v.rearrange("p (a b c) -> p a b c", a=shape[1], b=shape[2])
            return v

        KB = 1024
        A0 = 0
        B0 = 7 * KB
        C0 = B0 + 32 * KB
        D0 = C0 + 64 * KB + 256
        E0 = D0 + 32 * KB
        F0 = E0 + 32 * KB
        G0 = F0 + 32 * KB

        ident_f = view(A0 + 0, [128, 128], F32)
        ident_b = view(A0 + 512, [128, 128], BF16)
        tri01 = view(A0 + 768, [128, 128], BF16)
        rope_t = view(A0 + 1024, [128, NT, 32], F32)
        small_c = view(A0 + 3072, [128, 24], F32)
        gmix_c = small_c[:, 0:8]
        gmlp_c = small_c[:, 8:16]
        psc_c = small_c[:, 16:20]
        gcolT_q = small_c[:, 20:21]
        gcolT_k = small_c[:, 21:22]
        gv_qk = view(A0 + 3200, [128, 64], F32)
        gv_q = gv_qk[:, 0:32]
        gv_k = gv_qk[:, 32:64]
        ones_b = view(A0 + 3664, [128, 2], BF16)
        band = view(A0 + 3680, [128, 12, 128], BF16)
        wgrp = view(G0 + 0, [128, 4, 128], BF16)
        ss1 = view(G0 + 1024, [128, NT], F32)
        rstd1 = view(G0 + 1088, [128, NT], F32)
        ss2 = view(G0 + 1152, [128, NT], F32)
        r2sq = view(G0 + 1216, [128, NT], F32)
        kmeanT = view(G0 + 1280, [128, 4, 8], BF16)
        ksum_sb = view(G0 + 1344, [128, 4, 8], F32)
        stat_a = [view(G0 + 5632 + i * 64, [128, 8], F32) for i in range(2)]
        stat_b = [view(G0 + 5760 + i * 64, [128, 8], F32) for i in range(2)]
        junk_h = view(G0 + 5888, [128, 512], BF16)
        ss2h = view(G0 + 6912, [128, 2 * NT], F32)

        hT = view(B0, [128, 8, S], BF16)
        qTz = view(C0, [128, H, S], BF16)
        kT = view(C0 + 32 * KB, [128, 4, S], BF16)
        v_aug = view(C0 + 48 * KB, [128, NT, H, 65], BF16)
        assert C0 + 48 * KB + 16640 <= D0
        x1 = view(C0, [128, NT, D], F32)
        _sb = [F0, C0]
        wga = [view(_sb[s], [128, 8, 128], BF16) for s in range(2)]
        wgp = [view(_sb[s] + 2 * KB, [128, 8, 128], BF16) for s in range(2)]
        wua = [view(_sb[s] + 4 * KB, [128, 4, 128], BF16) for s in range(2)]
        wup = [view(_sb[s] + 5 * KB, [128, 4, 128], BF16) for s in range(2)]
        ga_sb = [view(C0 + 6 * KB + s * 4 * KB, [128, 512], F32) for s in range(2)]
        gp_sb = [view(C0 + 6 * KB + s * 4 * KB + 2 * KB, [128, 512], F32) for s in range(2)]
        xslot = [view(D0 + i * 4 * KB, [128, D], F32) for i in range(8)]
        u_sb = view(D0, [128, NT, 512], BF16)
        pooledT = view(D0 + 16 * KB, [128, 4, S], BF16)
        mT = view(D0, [128, 8, S], BF16)
        a1T = [view(D0 + s * 8 * KB, [128, 8, 512], BF16) for s in range(2)]
        sqz = [view(D0 + 16 * KB + s * 2 * KB, [128, 512], F32) for s in range(2)]
        o_attnT = view(E0, [128, 4, S], BF16)
        junk = view(E0, [128, D], BF16)
        gmix_bc = view(E0 + 2 * KB, [128, D], F32)
        hb0 = [view(E0 + 6 * KB + i * 2 * KB, [128, D], BF16) for i in range(2)]
        E1 = E0 + 16 * KB
        sq_sb = [view(E1 + i * 2 * KB, [128, 512], F32) for i in range(2)]
        qtok = [view(E1 + 4 * KB + i * KB, [128, 512], BF16) for i in range(2)]
        x16 = [view(E1 + 6 * KB + i * 512, [128, 8, 16], F32) for i in range(2)]
        rt1 = [view(E1 + 7 * KB + i * 512, [128, 8, 16], F32) for i in range(2)]
        rt2 = [view(E1 + 8 * KB + i * 512, [128, 8, 16], F32) for i in range(2)]
        rope_q = view(E1 + 9 * KB, [128, NT, 32], F32)
        rope_k = view(E1 + 11 * KB, [128, NT, 32], F32)
        yT = view(E0 + 16 * KB, [128, 4, S], BF16)
        wqkvu = [view(F0 + i * 8 * KB, [128, 8, 512], BF16) for i in range(4)]
        wout = view(F0 + 8 * KB, [128, 8, D], BF16)
        kT2 = view(F0 + 8 * KB, [128, 4, S], BF16)
        xt5 = [view(F0 + 24 * KB + i * 4 * KB, [128, D], F32) for i in range(2)]
        gmlp_bc = view(F0, [128, D], F32)
        hb16 = [view(F0 + 4 * KB + i * 2 * KB, [128, D], BF16) for i in range(2)]
        f2 = F0 + 24 * KB
        PT = [view(f2 + i * 1024, [128, 512], BF16) for i in range(5)]
        o_tok = [view(f2 + 5120 + i * 1024, [128, 512], BF16) for i in range(2)]
        rden = [view(G0 + 7040 + i * 16, [128, 4], F32) for i in range(4)]
        assert f2 + 7168 <= F0 + 31 * KB
        gate_sb = view(F0 + 31 * KB, [128, 8, 8], F32)
        m8_sb = view(F0 + 31 * KB + 256, [128, 8, 8], F32)
        sel_sb = view(F0 + 31 * KB + 512, [128, 8, 8], F32)
        bias8 = [view(F0 + 6 * KB, [128, H, 128], BF16), view(D0 + 16 * KB, [128, H, 128], BF16)]
        rank_sb = view(D0 + 18 * KB, [128, H, 8, 8], F32)
        ffw1 = [view(E0, [128, 8, 1024], BF16), view(F0, [128, 8, 1024], BF16)]
        ffw2 = [view(E0 + 16 * KB, [128, 8, 1024], BF16), view(F0 + 16 * KB, [128, 8, 1024], BF16)]

        banks = [es.enter_context(nc.psum_tensor("bank%d" % i, [128, 512], F32)) for i in range(8)]
        hb = []
        for i in range(8):
            pb_ = Buf("psbank%d" % i, excl=True)
            hb.append([pb_, pb_])

        def bank_f(i):
            return banks[i][:, :]

        def bank_bf(i):
            return banks[i][:, :].bitcast(BF16)

        t = Trk(nc, es)

        def dbg_dump(name, ap, shape, bufs):
            if name not in debug:
                return
            d = nc.dram_tensor("dbg_" + name, list(shape), ap.dtype, kind="ExternalOutput").ap()
            dbg_d[name] = d
            t.dma("sp", d, ap, reads=bufs)

        b_wqkvu = [Buf("wqkvu%d" % i) for i in range(4)]
        w_in_v = w_in_d.rearrange("(c p) n -> p c n", p=128)
        CG_K, CG_Q, CG_V, CG_U = 1, 0, 2, 3
        b_c = {n: Buf(n) for n in ("ident_f", "ident_b", "rope", "gvq", "gvk", "gmix", "gmlp",
                                    "psc", "ones", "band", "wgrp", "gcq", "gck", "ropeq", "ropek",
                                    "qz")}
        t.dma("sp", ident_f, c_ident_d, writes=[b_c["ident_f"]])
        t.dma("sp", small_c, c_small_d, writes=[b_c["gmix"], b_c["gmlp"], b_c["psc"], b_c["gcq"], b_c["gck"]])
        t.dma("sp", gv_qk, c_gv_d.partition_broadcast(128), writes=[b_c["gvq"], b_c["gvk"]])
        t.dma("sp", rope_t, c_rope_d.rearrange("p (t n) -> p t n", t=NT), writes=[b_c["rope"]])
        t.dma("pool", wqkvu[CG_K], w_in_v[:, :, CG_K * 512:(CG_K + 1) * 512], writes=[b_wqkvu[CG_K]])
        t.dma("pool", ident_b, c_ident_d, writes=[b_c["ident_b"]])

        def setup_gains(gv, gc, rp, nm_gv, nm_gc, nm_rp):
            t.op("dve", lambda e: e.memset(gc[0:16, :], 1.0), writes=[b_c[nm_gc]])
            t.op("dve", lambda e: e.memset(gc[64:80, :], 1.0), writes=[b_c[nm_gc]])
            t.op("dve", lambda e: e.tensor_tensor(
                out=rp, in0=rope_t, in1=gv.unsqueeze(1).to_broadcast([128, NT, 32]), op=ALU.mult),
                reads=[b_c["rope"], b_c[nm_gv]], writes=[b_c[nm_rp]])

        b_tri = Buf("tri01")
        b_eps = Buf("eps")
        eps_c = small_c[:, 22:23]

        def small_setup_k():
            t.op("dve", lambda e: e.memset(eps_c, EPS), reads=[b_c["gmix"]], writes=[b_eps])
            t.op("pool", lambda e: e.memset(tri01, 1.0), writes=[b_tri])
            t.op("pool", lambda e: e.affine_select(
                out=tri01, in_=tri01, pattern=[[1, 128]], compare_op=ALU.is_ge, fill=0.0, base=0,
                channel_multiplier=-1), reads=[b_tri], writes=[b_tri])
            t.op("dve", lambda e: e.memset(ones_b, 1.0), writes=[b_c["ones"]])
            setup_gains(gv_k, gcolT_k, rope_k, "gvk", "gck", "ropek")

        def small_setup_rest():
            t.wait_written("pool", b_xslot[7])
            t.dma("pool", wqkvu[CG_Q], w_in_v[:, :, CG_Q * 512:(CG_Q + 1) * 512], writes=[b_wqkvu[CG_Q]])
            setup_gains(gv_q, gcolT_q, rope_q, "gvq", "gcq", "ropeq")
            t.dma("pool", wqkvu[CG_U], w_in_v[:, :, CG_U * 512:(CG_U + 1) * 512], writes=[b_wqkvu[CG_U]])
            t.dma("pool", band, c_band_d, writes=[b_c["band"]])
            t.dma("pool", wgrp, w_grp_d.rearrange("g c d -> c g d"), writes=[b_c["wgrp"]])

        def zero_fill():
            for h in range(H):
                base = 64 if h % 2 == 0 else 0
                t.dma("sp", qTz[base:base + 64, h, :].bitcast(F32), c_zero_d, writes=[b_c["qz"]])

        b_xslot = [Buf("xslot%d" % i) for i in range(8)]
        b_ss1 = Buf("ss1")
        b_rstd1 = Buf("rstd1")
        b_junk = Buf("junk")
        b_hT = [Buf("hT_%d" % i) for i in range(NT)]
        ev_cnt = [0]
        tp_cnt = [0]

        b_gmix_bc = Buf("gmix_bc")
        b_hb0 = [Buf("hb0_%d" % i) for i in range(2)]
        p0_bank = {}

        def p0_load1(tt):
            sl = tt % 8
            t.dma("sp", xslot[sl], x_d[tt * 128:(tt + 1) * 128, :], writes=[b_xslot[sl]])

        def p0_sq(tt):
            sl = tt % 8
            t.op("act", lambda e: e.activation(
                out=junk, in_=xslot[sl], func=AF.Square, accum_out=ss1[:, tt:tt + 1]),
                reads=[b_xslot[sl]], writes=[b_junk, b_ss1])

        def p0_stats(tt):
            c0, c1 = tt, tt + 1
            t.op("act", lambda e: e.activation(out=rstd1[:, c0:c1], in_=ss1[:, c0:c1], func=AF.Sqrt,
                                               scale=1.0 / D, bias=eps_c[:, 0:1]),
                 reads=[b_ss1, b_eps], writes=[b_rstd1])
            t.op("dve", lambda e: e.reciprocal(out=rstd1[:, c0:c1], in_=rstd1[:, c0:c1]),
                 reads=[b_rstd1], writes=[b_rstd1])

        def p0_mul(tt):
            sl = tt % 8
            s2 = tt % 2
            t.op("dve", lambda e: e.scalar_tensor_tensor(
                out=hb0[s2], in0=xslot[sl], scalar=rstd1[:, tt:tt + 1], in1=gmix_bc,
                op0=ALU.mult, op1=ALU.mult),
                reads=[b_xslot[sl], b_rstd1, b_gmix_bc], writes=[b_hb0[s2]])

        def p0_tr1(tt):
            s2 = tt % 2
            bk = tp_cnt[0] % 3
            tp_cnt[0] += 1
            tps = bank_bf(bk)
            t.group("pe", [lambda e, c=c: e.transpose(
                out=tps[:, c * 128:(c + 1) * 128], in_=hb0[s2][:, c * 128:(c + 1) * 128],
                identity=ident_b) for c in range(8)],
                reads=[b_hb0[s2], b_c["ident_b"]], writes=hb[bk])
            t.op("act", lambda e: e.activation(
                out=hT[:, :, tt * 128:(tt + 1) * 128], in_=tps.rearrange("p (c t) -> p c t", c=8),
                func=AF.Copy),
                reads=hb[bk], writes=[b_hT[tt]])

        b_sq = [Buf("sq%d" % i) for i in range(2)]
        b_x16 = [Buf("x16%d" % i) for i in range(2)]
        b_qtok = [Buf("qtok%d" % i) for i in range(2)]
        b_rt1 = [Buf("rt1%d" % i) for i in range(2)]
        b_rt2 = [Buf("rt2%d" % i) for i in range(2)]
        b_sta = [Buf("sta%d" % i) for i in range(2)]
        b_stb = [Buf("stb%d" % i) for i in range(2)]
        b_qT = [Buf("qT%d" % i) for i in range(NT)]
        b_kT = [Buf("kT%d" % i) for i in range(NT)]
        b_v = [Buf("v%d" % i) for i in range(NT)]
        b_vones = Buf("vones")
        b_kmean = Buf("kmeanT")
        b_gate = Buf("gate")
        b_m8 = Buf("m8")
        b_sel = Buf("sel")
        b_rank = Buf("rank")
        b_btok2 = [Buf("bias8_%d" % i) for i in range(2)]
        b_ksum_ps = hb[3][0]
        b_gate_ps = hb[3][0]
        b_bias_ps = hb[3][0]
        ksum_ps = bank_f(3)[:, 0:128].rearrange("p (a b c) -> p a b c", a=4, b=NT)
        gate_ps = bank_f(3)[:, 128:192].rearrange("p (h n) -> p h n", h=H)
        bias_ps = bank_bf(3)[:, 512:640]
        pj_banks = [4, 5, 6, 7]

        jobs = []
        for cga, cgb in ((CG_K, CG_V), (CG_Q, CG_U)):
            for tt in range(NT):
                jobs.append((cga, tt))
                jobs.append((cgb, tt))
        qk_slot = {}
        for ji, (cg, tt) in enumerate(jobs):
            if cg in (CG_K, CG_Q):
                qk_slot[ji] = len(qk_slot) % 2

        def proj_mm(ji):
            cg, tt = jobs[ji]
            bk = pj_banks[ji % 4]
            fns = []
            for c in range(8):
                fns.append(lambda e, c=c: e.matmul(
                    bank_f(bk), lhsT=hT[:, c, tt * 128:(tt + 1) * 128], rhs=wqkvu[cg][:, c, :],
                    start=(c == 0), stop=(c == 7)))
            t.group("pe", fns, reads=[b_hT[tt], b_wqkvu[cg]], writes=hb[bk])

        def post_a2(ji):
            cg, tt = jobs[ji]
            if cg in (CG_K, CG_Q):
                i2 = qk_slot[ji]
                sq3 = sq_sb[i2].rearrange("p (h d) -> p h d", h=H)
                t.op("dve", lambda e: e.tensor_reduce(out=stat_a[i2], in_=sq3, axis=AX.X, op=ALU.add),
                     reads=[b_sq[i2]], writes=[b_sta[i2]])
                t.op("act", lambda e: e.activation(out=stat_b[i2], in_=stat_a[i2], func=AF.Sqrt,
                                                   scale=1.0 / DH, bias=eps_c[:, 0:1]),
                     reads=[b_sta[i2], b_eps], writes=[b_stb[i2]])

        def post_a1(ji):
            cg, tt = jobs[ji]
            bk = pj_banks[ji % 4]
            if cg in (CG_K, CG_Q):
                i2 = qk_slot[ji]
                t.op("act", lambda e: e.activation(out=sq_sb[i2], in_=bank_f(bk), func=AF.Square),
                     reads=hb[bk], writes=[b_sq[i2]])
            elif cg == CG_V:
                t.op("dve", lambda e: e.tensor_copy(
                    out=v_aug[:, tt, :, 0:64], in_=bank_f(bk).rearrange("p (h d) -> p h d", h=H)),
                    reads=hb[bk] + [b_vones], writes=[b_v[tt]])
            else:
                t.op("act", lambda e: e.activation(out=u_sb[:, tt, :], in_=bank_f(bk), func=AF.Copy),
                     reads=hb[bk], writes=[b_u[tt]])

        def post_b(ji):
            cg, tt = jobs[ji]
            if cg not in (CG_K, CG_Q):
                return
            is_k = cg == CG_K
            bk = pj_banks[ji % 4]
            i2 = qk_slot[ji]
            rp = rope_k if is_k else rope_q
            rb = b_c["ropek"] if is_k else b_c["ropeq"]
            ps3 = bank_f(bk).rearrange("p (h d) -> p h d", h=H)
            qt3 = qtok[i2].rearrange("p (h d) -> p h d", h=H)
            t.op("dve", lambda e: e.reciprocal(out=stat_b[i2], in_=stat_b[i2]),
                 reads=[b_stb[i2]], writes=[b_stb[i2]])
            t.op("dve", lambda e: e.tensor_tensor(
                out=qt3, in0=ps3, in1=stat_b[i2].unsqueeze(2).to_broadcast([128, H, DH]), op=ALU.mult),
                reads=hb[bk] + [b_stb[i2]], writes=[b_qtok[i2]])
            t.op("dve", lambda e: e.tensor_tensor(
                out=x16[i2], in0=ps3[:, :, 0:16], in1=stat_b[i2].unsqueeze(2).to_broadcast([128, H, 16]),
                op=ALU.mult),
                reads=hb[bk] + [b_stb[i2]], writes=[b_x16[i2]])
            cs_b = rp[:, tt, 0:16].unsqueeze(1).to_broadcast([128, H, 16])
            sn_lo = rp[:, tt, 16:24].unsqueeze(1).to_broadcast([128, H, 8])
            sn_hi = rp[:, tt, 24:32].unsqueeze(1).to_broadcast([128, H, 8])
            reng = "pool" if is_k else "dve"
            t.op(reng, lambda e: e.tensor_tensor(out=rt1[i2], in0=x16[i2], in1=cs_b, op=ALU.mult),
                 reads=[b_x16[i2], rb], writes=[b_rt1[i2]])
            t.op(reng, lambda e: e.tensor_tensor(out=rt2[i2][:, :, 0:8], in0=x16[i2][:, :, 8:16], in1=sn_lo,
                                                  op=ALU.mult),
                 reads=[b_x16[i2], rb], writes=[b_rt2[i2]])
            t.op(reng, lambda e: e.tensor_tensor(out=rt2[i2][:, :, 8:16], in0=x16[i2][:, :, 0:8], in1=sn_hi,
                                                  op=ALU.mult),
                 reads=[b_x16[i2], rb, b_rt2[i2]], writes=[b_rt2[i2]])
            t.op(reng, lambda e: e.tensor_tensor(out=qt3[:, :, 0:16], in0=rt1[i2], in1=rt2[i2], op=ALU.add),
                 reads=[b_rt1[i2], b_rt2[i2], b_qtok[i2]], writes=[b_qtok[i2]])

        def post_b_pe(ji):
            cg, tt = jobs[ji]
            if cg not in (CG_K, CG_Q):
                return
            is_k = cg == CG_K
            i2 = qk_slot[ji]
            tbk = tp_cnt[0] % 3
            tp_cnt[0] += 1
            tps = bank_bf(tbk)[:, 0:512]
            fns = []
            for pr in range(4):
                fns.append(lambda e, pr=pr: e.transpose(
                    out=tps[:, pr * 128:(pr + 1) * 128], in_=qtok[i2][:, pr * 128:(pr + 1) * 128],
                    identity=ident_b))
            wr = [hb[tbk][0]]
            if is_k:
                for pr in range(4):
                    fns.append(lambda e, pr=pr: e.matmul(
                        ksum_ps[:, pr, tt, :], lhsT=qtok[i2][:, pr * 128:(pr + 1) * 128], rhs=ones_b,
                        start=True, stop=True))
                wr.append(b_ksum_ps)
            t.group("pe", fns, reads=[b_qtok[i2], b_c["ident_b"], b_c["ones"]], writes=wr)
            tps3 = tps.rearrange("p (a b) -> p a b", a=4)
            if is_k:
                t.op("act", lambda e: e.activation(out=kT[:, :, tt * 128:(tt + 1) * 128], in_=tps3,
                                                   func=AF.Copy, scale=gcolT_k[:, 0:1]),
                     reads=[hb[tbk][0], b_c["gck"]], writes=[b_kT[tt]])
            else:
                t.op("act", lambda e: e.activation(
                    out=qTz[0:64, 0:H:2, tt * 128:(tt + 1) * 128], in_=tps3[0:64],
                    func=AF.Copy, scale=gcolT_q[0:64, 0:1]),
                    reads=[hb[tbk][0], b_c["gcq"], b_c["qz"]], writes=[b_qT[tt]])
                t.op("act", lambda e: e.activation(
                    out=qTz[64:128, 1:H:2, tt * 128:(tt + 1) * 128], in_=tps3[64:128],
                    func=AF.Copy, scale=gcolT_q[64:128, 0:1]),
                    reads=[hb[tbk][0], b_c["gcq"], b_c["qz"], b_qT[tt]], writes=[b_qT[tt]])

        def gate_mm(tt):
            fns = []
            for h in range(H):
                fns.append(lambda e, h=h: e.matmul(
                    gate_ps[:, h, :], lhsT=qTz[:, h, tt * 128:(tt + 1) * 128], rhs=kmeanT[:, h // 2, :],
                    start=True, stop=True))
            t.group("pe", fns, reads=[b_qT[tt], b_kmean], writes=[b_gate_ps])

        def gate_sel(tt):
            qb = tt // 2
            t.op("dve", lambda e: e.memset(gate_sb, -1.0e30), writes=[b_gate])
            t.op("dve", lambda e: e.tensor_copy(out=gate_sb[:, :, 0:qb], in_=gate_ps[:, :, 0:qb]),
                 reads=[b_gate_ps, b_gate], writes=[b_gate])
            g_m = gate_sb.unsqueeze(2).to_broadcast([128, H, 8, 8])
            g_n = gate_sb.unsqueeze(3).to_broadcast([128, H, 8, 8])
            t.op("dve", lambda e: e.tensor_tensor(out=rank_sb, in0=g_m, in1=g_n, op=ALU.is_gt),
                 reads=[b_gate, b_rank], writes=[b_rank])

        def gate_sel_b(tt):
            qb = tt // 2
            t.op("dve", lambda e: e.tensor_reduce(out=m8_sb, in_=rank_sb, axis=AX.X, op=ALU.add),
                 reads=[b_rank, b_m8], writes=[b_m8])
            t.op("dve", lambda e: e.tensor_scalar(out=sel_sb, in0=m8_sb, scalar1=2.5, scalar2=None,
                                                  op0=ALU.is_lt),
                 reads=[b_m8, b_sel], writes=[b_sel])
            b8, b_b8 = bias8[tt % 2], b_btok2[tt % 2]
            t.op("dve", lambda e: e.tensor_scalar(
                out=b8[:, 0:H:2, 64:64 + qb], in0=sel_sb[:, 0:H:2, 0:qb], scalar1=-1.0, scalar2=-NEG,
                op0=ALU.add, op1=ALU.mult),
                reads=[b_sel, b_b8], writes=[b_b8])
            t.op("dve", lambda e: e.tensor_scalar(
                out=b8[:, 1:H:2, 0:qb], in0=sel_sb[:, 1:H:2, 0:qb], scalar1=-1.0, scalar2=-NEG,
                op0=ALU.add, op1=ALU.mult),
                reads=[b_sel, b_b8], writes=[b_b8])

        def gate_tr(tt):
            b8, b_b8 = bias8[tt % 2], b_btok2[tt % 2]
            tbk = 3
            tps = bank_bf(tbk)
            t.group("pe", [lambda e, h=h: e.transpose(out=tps[:, h * 128:(h + 1) * 128], in_=b8[:, h, :],
                                                      identity=ident_b) for h in range(H)],
                    reads=[b_b8, b_c["ident_b"]], writes=[hb[tbk][0]])
            tp3 = tps.rearrange("p (h t) -> p h t", h=H)
            t.op("dve", lambda e: e.tensor_copy(out=qTz[64:72, 0:H:2, tt * 128:(tt + 1) * 128],
                                                in_=tp3[64:72, 0:H:2, :]),
                 reads=[hb[tbk][0], b_c["qz"], b_qT[tt]], writes=[b_qT[tt]])
            t.op("dve", lambda e: e.tensor_copy(out=qTz[0:8, 1:H:2, tt * 128:(tt + 1) * 128],
                                                in_=tp3[0:8, 1:H:2, :]),
                 reads=[hb[tbk][0], b_c["qz"], b_qT[tt]], writes=[b_qT[tt]])

        def kmean_fin():
            ks3 = bank_f(3)[:, 0:128].rearrange("p (a n c) -> p a n c", a=4, n=NB)
            ks3 = ks3[:, :, :, 0:4:2]
            t.op("dve", lambda e: e.tensor_reduce(out=ksum_sb, in_=ks3, axis=AX.X, op=ALU.add),
                 reads=[b_ksum_ps], writes=[b_kmean])
            t.op("dve", lambda e: e.tensor_scalar(out=kmeanT, in0=ksum_sb, scalar1=gcolT_k[:, 0:1],
                                                  scalar2=1.0 / 256.0, op0=ALU.mult, op1=ALU.mult),
                 reads=[b_kmean, b_c["gck"]], writes=[b_kmean])

        b_kT2 = Buf("kT2")

        def make_kT2():
            b_kT2.fence = t.snapshot()
            t.dma("sp", kT2, kT, reads=b_kT, writes=[b_kT2])
            for pr in range(4):
                t.dma("pool", kT[64:72, pr, :], c_ind_d, writes=b_kT)
                t.dma("pool", kT2[0:8, pr, :], c_ind_d, writes=[b_kT2])

        later = []

        def run_later(it):
            k = 0
            while k < len(later):
                if later[k][0] <= it:
                    later.pop(k)[1]()
                else:
                    k += 1

        p0_sched = {}

        def p0_at(it_, fn, *a):
            p0_sched.setdefault(max(it_, -1), []).append((fn, a))

        for tt_ in range(2, NT):
            p0_at(2 * tt_ - 6, p0_sq, tt_)
            p0_at(2 * tt_ - 5, p0_stats, tt_)
            p0_at(2 * tt_ - 4, p0_mul, tt_)
            p0_at(2 * tt_ - 2, p0_tr1, tt_)
        for tt_ in range(8, NT):
            p0_at(max(0, 2 * (tt_ - 8) - 1), p0_load1, tt_)

        p0_load1(0)
        t.dma("sp", gmix_bc, gmix_vec_d, writes=[b_gmix_bc])
        for tt_ in range(1, 4):
            p0_load1(tt_)
        t.wait_written("pool", b_xslot[3])
        t.dma("pool", wqkvu[CG_V], w_in_v[:, :, CG_V * 512:(CG_V + 1) * 512], writes=[b_wqkvu[CG_V]])
        t.op("dve", lambda e: e.memset(v_aug[:, :, :, 64:65], 1.0), writes=[b_vones])
        small_setup_k()
        for tt_ in range(4, 8):
            p0_load1(tt_)
        for tt_ in range(2):
            p0_sq(tt_)
            p0_stats(tt_)
            p0_mul(tt_)
            if tt_ == 0:
                p0_tr1(tt_)
        p0_at(1, p0_tr1, 1)
        small_setup_rest()
        for fn_, a_ in p0_sched.get(-1, []):
            fn_(*a_)
        fence_p0 = None
        b_u = None
        NJ = len(jobs)
        DEFER_PE = [NJ - 4, NJ - 2]
        for it in range(NJ + 6):
            if stop and stop.startswith("it") and it >= int(stop[2:]):
                t.wait_all("sp")
                return nc, list(dbg_d.keys())
            if it == 9:
                zero_fill()
            for fn_, a_ in p0_sched.get(it, []):
                fn_(*a_)
            if it < NJ:
                if it == 2 * NT:
                    fence_p0 = t.snapshot()
                    b_u = [Buf("u%d" % i, fence=fence_p0) for i in range(NT)]

                proj_mm(it)
            if 0 <= it - 1 < NJ:
                post_a1(it - 1)
            if 0 <= it - 2 < NJ:
                post_a2(it - 2)
            if 0 <= it - 3 < NJ:
                post_b(it - 3)
            if 0 <= it - 5 < NJ:
                jb = it - 5
                if jb not in DEFER_PE:
                    post_b_pe(jb)
                cg, tt = jobs[jb]
                if cg == CG_K and tt == NT - 1:
                    later.append((it + 2, kmean_fin))
                if cg == CG_V and tt == NT - 1:
                    later.append((it + 2, make_kT2))
            run_later(it)
        run_later(10 ** 9)
        dbg_dump("hT", hT, [128, 8, S], b_hT)
        dbg_dump("qTz", qTz, [128, H, S], b_qT)
        dbg_dump("kT", kT, [128, 4, S], b_kT)
        dbg_dump("v", v_aug, [128, NT, H, 65], b_v)
        dbg_dump("u", u_sb, [128, NT, 512], b_u)
        dbg_dump("kmeanT", kmeanT, [128, 4, 8], [b_kmean])
        fence_p1 = t.snapshot()
        hb[3][0].fence = dict(fence_p1)
        hb[3][1].fence = dict(fence_p1)
        if stop == "p1":
            t.wait_all("sp")
            return nc, list(dbg_d.keys())

        b_wsl = [[Buf("wsl%d_%d" % (s, j), fence=fence_p1) for j in range(4)] for s in range(2)]
        w_ua_v = w_ua_d.rearrange("(c p) n -> p c n", p=128)
        w_up_v = w_up_d.rearrange("(c p) n -> p c n", p=128)

        def load_slices(fc):
            s = fc % 2
            t.dma("pool", wga[s], w_in_v[:, :, 2048 + fc * 128: 2048 + (fc + 1) * 128], writes=[b_wsl[s][0]])
            t.dma("pool", wgp[s], w_in_v[:, :, 3072 + fc * 128: 3072 + (fc + 1) * 128], writes=[b_wsl[s][1]])
            t.dma("pool", wua[s], w_ua_v[:, :, fc * 128:(fc + 1) * 128], writes=[b_wsl[s][2]])
            t.dma("pool", wup[s], w_up_v[:, :, fc * 128:(fc + 1) * 128], writes=[b_wsl[s][3]])

        load_slices(0)

        b_pooled = [Buf("pooled%d" % i, fence=fence_p0) for i in range(4)]
        b_yT = [Buf("yT%d" % i, fence=fence_p1) for i in range(4)]
        def p3_band(tt, bk):
            fns = []
            for g in range(4):
                kind = 2 if tt == 0 else 0
                fns.append(lambda e, g=g, kind=kind: e.matmul(
                    bank_f(bk)[:, g * 128:(g + 1) * 128], lhsT=u_sb[:, tt, g * 128:(g + 1) * 128],
                    rhs=band[:, g * 3 + kind, :], start=True, stop=(tt == 0)))
                if tt > 0:
                    fns.append(lambda e, g=g: e.matmul(
                        bank_f(bk)[:, g * 128:(g + 1) * 128], lhsT=u_sb[:, tt - 1, g * 128:(g + 1) * 128],
                        rhs=band[:, g * 3 + 1, :], start=False, stop=True))
            rd = [b_u[tt], b_c["band"]] + ([b_u[tt - 1]] if tt > 0 else [])
            t.group("pe", fns, reads=rd, writes=hb[bk])
            t.op("dve", lambda e: e.tensor_copy(
                out=pooledT[:, :, tt * 128:(tt + 1) * 128],
                in_=bank_f(bk).rearrange("p (g t) -> p g t", g=4)),
                reads=hb[bk], writes=[b_pooled[tt // 4]])

        def p3_y(tg, g, bk):
            t.group("pe", [lambda e: e.matmul(
                bank_f(bk), lhsT=wgrp[:, g, :], rhs=pooledT[:, g, tg * 512:(tg + 1) * 512],
                start=True, stop=True)],
                reads=[b_pooled[tg], b_c["wgrp"]], writes=hb[bk])
            t.op("dve", lambda e: e.tensor_scalar(
                out=yT[:, g, tg * 512:(tg + 1) * 512], in0=bank_f(bk), scalar1=psc_c[:, g:g + 1],
                scalar2=None, op0=ALU.mult),
                reads=hb[bk] + [b_c["psc"]], writes=[b_yT[tg]])


        NPT = 5
        b_PT = [Buf("PT%d" % i, fence=fence_p1) for i in range(NPT)]
        b_otok = [Buf("otok%d" % i, fence=fence_p1) for i in range(2)]
        b_rden = [Buf("rden%d" % i) for i in range(4)]
        b_oT = [Buf("oT%d" % i, fence=fence_p1) for i in range(NT)]
        st_banks = [0, 1, 2]
        MISC = 3
        accb = [[4, 5], [6, 7]]

        steps = []
        for qb in range(NB):
            for h in range(H):
                for kp in range(qb + 1):
                    steps.append((qb, h, kp))
        LOOK = 3
        pend_post = []

        def emit_qk(si):
            qb, h, kp = steps[si]
            pr = h // 2
            sb_ = st_banks[si % 3]
            q0 = qb * 256
            ksel = kT if h % 2 == 0 else kT2
            fns = []
            for j in range(2):
                kt = 2 * kp + j
                fns.append(lambda e, kt=kt, j=j: e.matmul(
                    bank_f(sb_)[:, j * 256:(j + 1) * 256], lhsT=ksel[:, pr, kt * 128:(kt + 1) * 128],
                    rhs=qTz[:, h, q0:q0 + 256], start=True, stop=True))
            rd = [b_kT[2 * kp], b_kT[2 * kp + 1], b_qT[2 * qb], b_qT[2 * qb + 1]] + ([b_kT2] if h % 2 else [])
            t.group("pe", fns, reads=rd, writes=hb[sb_])
            pt = PT[si % NPT]
            t.op("act", lambda e: e.activation(out=pt, in_=bank_f(sb_), func=AF.Exp, scale=0.125),
                 reads=hb[sb_], writes=[b_PT[si % NPT]])
            if kp == qb and (si < 80 or si in gate_busy):
                dg = pt.rearrange("p (a c) -> p a c", c=128)[:, 0:4:3, :]
                t.op("pool", lambda e: e.affine_select(
                    out=dg, in_=dg, pattern=[[0, 2], [1, 128]],
                    compare_op=ALU.is_ge, fill=0.0, base=0, channel_multiplier=-1),
                    reads=[b_PT[si % NPT]], writes=[b_PT[si % NPT]])
            elif kp == qb:
                dg = pt.rearrange("p (a c) -> p a c", c=128)[:, 0:4:3, :]
                t.op("dve", lambda e: e.tensor_tensor(
                    out=dg, in0=dg, in1=tri01.unsqueeze(1).to_broadcast([128, 2, 128]), op=ALU.mult),
                    reads=[b_PT[si % NPT], b_tri], writes=[b_PT[si % NPT]])

        def emit_pv(si):
            qb, h, kp = steps[si]
            half, hh = h // 4, h % 4
            pt = PT[si % NPT]
            for j in range(2):
                kt = 2 * kp + j
                for ql in range(2):
                    if ql == 0 and kt == 2 * qb + 1:
                        continue
                    last = (2 * qb) if ql == 0 else (2 * qb + 1)
                    bk = accb[ql][half]
                    dst = bank_f(bk)[:, hh * 65:(hh + 1) * 65]
                    c0_ = j * 256 + ql * 128
                    t.group("pe", [lambda e: e.matmul(
                        dst, lhsT=pt[:, c0_:c0_ + 128], rhs=v_aug[:, kt, h, :],
                        start=(kt == 0), stop=(kt == last))],
                        reads=[b_PT[si % NPT], b_v[kt]], writes=hb[bk])
            if kp == qb and hh == 3:
                for ql in range(2):
                    bk = accb[ql][half]
                    acc3 = bank_f(bk)[:, 0:260].rearrange("p (h d) -> p h d", h=4)
                    ri = ql * 2 + half
                    t.op("dve", lambda e, acc3=acc3, ri=ri: e.reciprocal(out=rden[ri], in_=acc3[:, :, 64]),
                         reads=hb[bk], writes=[b_rden[ri]])
                    ot3 = o_tok[ql][:, half * 256:(half + 1) * 256].rearrange("p (h d) -> p h d", h=4)
                    t.op("dve", lambda e, acc3=acc3, ri=ri, ot3=ot3: e.tensor_tensor(
                        out=ot3, in0=acc3[:, :, 0:64],
                        in1=rden[ri].unsqueeze(2).to_broadcast([128, 4, 64]), op=ALU.mult),
                        reads=hb[bk] + [b_rden[ri]], writes=[b_otok[ql]])
                if half == 1:
                    def post(ql, qb=qb):
                        if True:
                            tt = 2 * qb + ql
                            tps = bank_bf(MISC)[:, 0:512]
                            t.group("pe", [lambda e, pr=pr_, ql=ql: e.transpose(
                                out=tps[:, pr * 128:(pr + 1) * 128],
                                in_=o_tok[ql][:, pr * 128:(pr + 1) * 128], identity=ident_b)
                                for pr_ in range(4)],
                                reads=[b_otok[ql], b_c["ident_b"]], writes=hb[MISC])
                            t.op("dve", lambda e, tt=tt: e.tensor_copy(
                                out=o_attnT[:, :, tt * 128:(tt + 1) * 128],
                                in_=tps.rearrange("p (a b) -> p a b", a=4)),
                                reads=hb[MISC], writes=[b_oT[tt]])
                    pend_post.append((si + LOOK + 2, lambda post=post: post(0)))
                    pend_post.append((si + LOOK + 5, lambda post=post: post(1)))

        for b_ in (b_btok2[0], b_btok2[1], b_gate, b_m8, b_sel, b_rank):
            b_.fence = dict(fence_p1)
        for i in range(2):
            t.op("dve", lambda e, i=i: e.memset(bias8[i], 0.0), writes=[b_btok2[i]])
        gate_at = {}
        gate_busy = set()
        P3_START = 130
        misc_used = set()
        for i_, (qb_, h_, kp_) in enumerate(steps):
            if h_ == 0 and kp_ == 0 and i_ > 0:
                misc_used.update((i_ + LOOK + 1, i_ + LOOK + 4))

        def misc_free(x_):
            return all(abs(x_ - u_) > 2 for u_ in misc_used)

        s_ = 1
        tr_prev = []
        for k in range(8):
            if k == 4:
                s_ = max(s_, 82)
            while True:
                if misc_free(s_) and (k < 2 or s_ > tr_prev[k - 2]):
                    tr_ = next((s_ + d_ for d_ in range(10, 18) if misc_free(s_ + d_)), None)
                    if tr_ is not None:
                        break
                s_ += 1
            assert tr_ < (80 if k < 4 else P3_START - 2), (k, s_, tr_)
            gate_at.setdefault(s_, []).append(("sel", 8 + k))
            gate_at.setdefault(s_ + 3, []).append(("selb", 8 + k))
            gate_busy.update(range(s_ - 1, s_ + 7))
            gate_at.setdefault(tr_, []).append(("tr", 8 + k))
            misc_used.update((s_, tr_))
            tr_prev.append(tr_)
            s_ += 3
        p3_at = {}
        bounds_ = [len(steps)]
        for i_, (qb_, h_, kp_) in enumerate(steps):
            if h_ == 0 and kp_ == 0:
                bounds_.append(i_)
        slots_ = []
        s_ = P3_START + 2
        while len(slots_) < 32:
            if all(not (b_ + 2 <= s_ <= b_ + 10) for b_ in bounds_):
                slots_.append(s_)
                s_ += 4
            else:
                s_ += 1
        assert slots_[-1] < len(steps) - 4, slots_
        for k in range(NT):
            p3_at[slots_[k]] = ("band", k)
        for k in range(16):
            p3_at[slots_[NT + k]] = ("y", k // 4, k % 4)
        for si in range(len(steps) + LOOK):
            if si == P3_START:
                f80 = t.snapshot()
                for b_ in b_pooled + b_yT:
                    b_.fence = dict(f80)
            if si in (3, 6):
                post_b_pe(DEFER_PE[0] if si == 3 else DEFER_PE[1])
            if si < len(steps):
                emit_qk(si)
            if si in p3_at:
                a_ = p3_at[si]
                if a_[0] == "band":
                    p3_band(a_[1], MISC)
                else:
                    p3_y(a_[1], a_[2], MISC)
            for kind, gt in gate_at.get(si, []):
                if kind == "sel":
                    gate_mm(gt)
                    gate_sel(gt)
                elif kind == "selb":
                    gate_sel_b(gt)
                else:
                    gate_tr(gt)
            if si - LOOK >= 0:
                emit_pv(si - LOOK)
            while pend_post and pend_post[0][0] <= si:
                pend_post.pop(0)[1]()
        while pend_post:
            pend_post.pop(0)[1]()
        dbg_dump("o_attnT", o_attnT, [128, 4, S], b_oT)
        dbg_dump("yT", yT, [128, 4, S], b_yT)
        fence_p2 = t.snapshot()
        if stop == "p2":
            t.wait_all("sp")
            return nc, list(dbg_d.keys())
        b_wout = [Buf("wout%d" % i, fence=fence_p2) for i in range(2)]
        w_out_v = w_out_d.rearrange("(c p) n -> p c n", p=128)
        for hf in range(2):
            t.dma("pool", wout[:, :, hf * 512:(hf + 1) * 512], w_out_v[:, :, hf * 512:(hf + 1) * 512],
                  writes=[b_wout[hf]])

        b_ga = [Buf("ga%d" % s, fence=fence_p2) for s in range(2)]
        b_gp = [Buf("gp%d" % s, fence=fence_p2) for s in range(2)]
        b_mT = [Buf("mT%d" % i, fence=fence_p2) for i in range(4)]
        for j in range(4):
            b_wsl[1][j].fence = dict(fence_p2)
        it = 0
        p4_mid = {}
        for fc in range(8):
            if fc == 3:
                p4_mid = {"dve": t.cnt["dve"]}
            if fc + 1 < 8:
                load_slices(fc + 1)
            s = fc % 2
            for tg in range(4):
                T0 = tg * 512
                pb = (it % 2) * 4
                i2 = it % 2
                it += 1
                t.group("pe", [lambda e, c=c: e.matmul(bank_f(pb + 0), lhsT=wga[s][:, c, :],
                                                        rhs=hT[:, c, T0:T0 + 512], start=(c == 0), stop=(c == 7))
                               for c in range(8)],
                        reads=[b_wsl[s][0]] + b_hT[tg * 4:(tg + 1) * 4], writes=hb[pb + 0])
                t.group("pe", [lambda e, c=c: e.matmul(bank_f(pb + 1), lhsT=wgp[s][:, c, :],
                                                        rhs=hT[:, c, T0:T0 + 512], start=(c == 0), stop=(c == 7))
                               for c in range(8)],
                        reads=[b_wsl[s][1]] + b_hT[tg * 4:(tg + 1) * 4], writes=hb[pb + 1])
                t.group("pe", [lambda e, c=c: e.matmul(bank_f(pb + 2), lhsT=wua[s][:, c, :],
                                                        rhs=o_attnT[:, c, T0:T0 + 512], start=(c == 0), stop=(c == 3))
                               for c in range(4)],
                        reads=[b_wsl[s][2]] + b_oT[tg * 4:(tg + 1) * 4], writes=hb[pb + 2])
                t.group("pe", [lambda e, c=c: e.matmul(bank_f(pb + 3), lhsT=wup[s][:, c, :],
                                                        rhs=yT[:, c, T0:T0 + 512], start=(c == 0), stop=(c == 3))
                               for c in range(4)],
                        reads=[b_wsl[s][3], b_yT[tg]], writes=hb[pb + 3])
                t.op("act", lambda e: e.activation(out=ga_sb[i2], in_=bank_f(pb + 0), func=AF.Sigmoid),
                     reads=hb[pb + 0], writes=[b_ga[i2]])
                t.op("act", lambda e: e.activation(out=gp_sb[i2], in_=bank_f(pb + 1), func=AF.Sigmoid),
                     reads=hb[pb + 1], writes=[b_gp[i2]])
                t.op("dve", lambda e: e.tensor_tensor(out=ga_sb[i2], in0=bank_f(pb + 2), in1=ga_sb[i2],
                                                      op=ALU.mult),
                     reads=hb[pb + 2] + [b_ga[i2]], writes=[b_ga[i2]])
                t.op("dve", lambda e: e.tensor_tensor(out=gp_sb[i2], in0=bank_f(pb + 3), in1=gp_sb[i2],
                                                      op=ALU.mult),
                     reads=hb[pb + 3] + [b_gp[i2]], writes=[b_gp[i2]])
                t.op("dve", lambda e: e.tensor_tensor(out=mT[:, fc, T0:T0 + 512], in0=ga_sb[i2],
                                                      in1=gp_sb[i2], op=ALU.add),
                     reads=[b_ga[i2], b_gp[i2]], writes=[b_mT[tg]])
        dbg_dump("mT", mT, [128, 8, S], b_mT)
        fence_p4 = t.snapshot()

        w_ff1_v = w_ff1_d.rearrange("(c p) n -> p c n", p=128)
        w_ff2_v = w_ff2_d.rearrange("(j p) n -> p j n", p=128)
        b_ffw1 = [[Buf("ffw1_%d_%d" % (s, i)) for i in range(2)] for s in range(2)]
        b_ffw2 = [[Buf("ffw2_%d_%d" % (s, i)) for i in range(2)] for s in range(2)]

        def load_ff(fq):
            s = fq % 2
            for hf in range(2):
                t.dma("pool", ffw1[s][:, :, hf * 512:(hf + 1) * 512],
                      w_ff1_v[:, :, fq * 1024 + hf * 512: fq * 1024 + (hf + 1) * 512],
                      writes=[b_ffw1[s][hf]])
            for hf in range(2):
                t.dma("pool", ffw2[s][:, :, hf * 512:(hf + 1) * 512],
                      w_ff2_v[:, fq * 8:(fq + 1) * 8, hf * 512:(hf + 1) * 512],
                      writes=[b_ffw2[s][hf]])

        for hf in range(2):
            b_ffw1[0][hf].fence = dict(fence_p4)
            b_ffw2[0][hf].fence = dict(fence_p4)
        load_ff(0)

        b_x1 = [Buf("x1_%d" % i, fence=(fence_p4 if i < 4 else fence_p2)) for i in range(NT)]
        b_xt5 = [Buf("xt5_%d" % i, fence=fence_p2) for i in range(2)]
        b_ss2 = Buf("ss2")
        b_r2 = Buf("r2sq")
        b_junk2 = Buf("junk2")
        b_ss2h = Buf("ss2h")
        b_h2T = [Buf("h2T%d" % i, fence=fence_p4) for i in range(4)]
        p5_banks = [0, 1, 2, 3]
        p5_i = 0

        p5_bk = {}

        def p5_mm(tt):
            nonlocal p5_i
            for hf in range(2):
                bk = p5_banks[p5_i % 4]
                p5_i += 1
                p5_bk[(tt, hf)] = bk
                t.group("pe", [lambda e, c=c, bk=bk, hf=hf: e.matmul(
                    bank_f(bk), lhsT=mT[:, c, tt * 128:(tt + 1) * 128],
                    rhs=wout[:, c, hf * 512:(hf + 1) * 512], start=(c == 0), stop=(c == 7))
                    for c in range(8)],
                    reads=[b_mT[tt // 4], b_wout[hf]], writes=hb[bk])

        def p5_add(tt):
            for hf in range(2):
                bk = p5_bk[(tt, hf)]
                dst = x1[:, tt, hf * 512:(hf + 1) * 512]
                t.op("dve", lambda e, bk=bk, dst=dst: e.tensor_tensor(
                    out=dst, in0=bank_f(bk), in1=dst, op=ALU.add),
                    reads=hb[bk] + [b_x1[tt]], writes=[b_x1[tt]])

        p6_banks = [4, 5, 6, 7]
        p6_i = 0

        b_gbc = Buf("gmlp_bc", fence=fence_p4)
        b_hb16 = [Buf("hb16_%d" % i, fence=fence_p4) for i in range(2)]

        p6_slot = {}

        def p6_mul(tt):
            nonlocal p6_i
            s2 = p6_i % 2
            p6_slot[tt] = (s2, p6_banks[p6_i % 4])
            p6_i += 1
            t.op("dve", lambda e: e.tensor_tensor(out=hb16[s2], in0=x1[:, tt, :], in1=gmlp_bc, op=ALU.mult),
                 reads=[b_x1[tt], b_gbc], writes=[b_hb16[s2]])

        def p6_tr(tt):
            s2, bk = p6_slot[tt]
            tps = bank_bf(bk)
            t.group("pe", [lambda e, c=c: e.transpose(
                out=tps[:, c * 128:(c + 1) * 128], in_=hb16[s2][:, c * 128:(c + 1) * 128],
                identity=ident_b) for c in range(8)],
                reads=[b_hb16[s2], b_c["ident_b"]], writes=hb[bk])
            t.op("act", lambda e: e.activation(
                out=hT[:, :, tt * 128:(tt + 1) * 128], in_=tps.rearrange("p (c t) -> p c t", c=8),
                func=AF.Copy),
                reads=hb[bk], writes=[b_h2T[tt // 4]])

        def p6_sq(tg, k):
            tt, hf = tg * 4 + k // 2, k % 2
            t.op("act", lambda e: e.activation(
                out=junk_h, in_=x1[:, tt, hf * 512:(hf + 1) * 512], func=AF.Square,
                accum_out=ss2h[:, hf * NT + tt: hf * NT + tt + 1]),
                reads=[b_x1[tt]], writes=[b_junk2, b_ss2h])

        def p6_stats(tg):
            c0, c1 = tg * 4, tg * 4 + 4
            t.op("dve", lambda e: e.tensor_tensor(out=ss2[:, c0:c1], in0=ss2h[:, c0:c1],
                                                  in1=ss2h[:, NT + c0:NT + c1], op=ALU.add),
                 reads=[b_ss2h], writes=[b_ss2])
            t.op("dve", lambda e: e.tensor_scalar(out=r2sq[:, c0:c1], in0=ss2[:, c0:c1], scalar1=1.0 / D,
                                                  scalar2=EPS, op0=ALU.mult, op1=ALU.add),
                 reads=[b_ss2], writes=[b_r2])
            t.op("dve", lambda e: e.reciprocal(out=r2sq[:, c0:c1], in_=r2sq[:, c0:c1]),
                 reads=[b_r2], writes=[b_r2])

        P5_ORDER = list(range(4, NT)) + list(range(4))
        t._wait("sp", p4_mid)
        for tt in P5_ORDER:
            if tt == 0:
                t.dma("sp", gmlp_bc, gmlp_vec_d, writes=[b_gbc])
            t.dma("sp", x1[:, tt, :], x_d[tt * 128:(tt + 1) * 128, :], writes=[b_x1[tt]])
        for k in range(NT + 1):
            if k < NT:
                p5_mm(P5_ORDER[k])
            if k >= 1:
                p6_mul(P5_ORDER[k - 1])
            if k < NT:
                p5_add(P5_ORDER[k])
            if k >= 1:
                p6_tr(P5_ORDER[k - 1])
        fence_p5 = t.snapshot()
        dbg_dump("x1", x1, [128, NT, D], b_x1)
        fence_p6 = t.snapshot()
        for hf in range(2):
            b_ffw1[1][hf].fence = dict(fence_p6)
            b_ffw2[1][hf].fence = dict(fence_p6)
        load_ff(1)

        b_a1T = [Buf("a1T%d" % s, fence=fence_p5) for s in range(2)]
        b_sqz = [Buf("sqz%d" % s, fence=fence_p5) for s in range(2)]
        f1_banks = [0, 1, 2, 3]
        f2_banks = [4, 5, 6, 7]
        f1_i = 0
        f2_i = 0
        sq_i = 0

        def ff1(fq, tg, ab):
            nonlocal f1_i, sq_i
            s = fq % 2
            for j in range(8):
                bk = f1_banks[f1_i % 4]
                f1_i += 1
                hf = j // 4
                t.group("pe", [lambda e, c=c, j=j, bk=bk: e.matmul(
                    bank_f(bk), lhsT=ffw1[s][:, c, j * 128:(j + 1) * 128],
                    rhs=hT[:, c, tg * 512:(tg + 1) * 512], start=(c == 0), stop=(c == 7))
                    for c in range(8)],
                    reads=[b_ffw1[s][hf], b_h2T[tg]], writes=hb[bk])
                si_ = sq_i % 2
                sq_i += 1
                t.op("act", lambda e, bk=bk, si_=si_: e.activation(out=sqz[si_], in_=bank_f(bk), func=AF.Square),
                     reads=hb[bk], writes=[b_sqz[si_]])
                t.op("dve", lambda e, bk=bk, si_=si_, j=j: e.scalar_tensor_tensor(
                    out=a1T[ab][:, j, :], in0=bank_f(bk), scalar=0.0, in1=sqz[si_],
                    op0=ALU.is_gt, op1=ALU.mult),
                    reads=hb[bk] + [b_sqz[si_]], writes=[b_a1T[ab]])
                if fq == 0:
                    p6_sq(tg, j)

        out_bufs = []

        def ff2(fq, tg, ab):
            nonlocal f2_i
            s = fq % 2
            for i in range(4):
                tt = tg * 4 + i
                for hf in range(2):
                    bk = f2_banks[f2_i % 4]
                    f2_i += 1
                    t.group("pe", [lambda e, j=j, bk=bk, hf=hf, i=i: e.matmul(
                        bank_f(bk), lhsT=a1T[ab][:, j, i * 128:(i + 1) * 128],
                        rhs=ffw2[s][:, j, hf * 512:(hf + 1) * 512], start=(j == 0), stop=(j == 7))
                        for j in range(8)],
                        reads=[b_a1T[ab], b_ffw2[s][hf]], writes=hb[bk])
                    dst = x1[:, tt, hf * 512:(hf + 1) * 512]
                    t.op("dve", lambda e, bk=bk, dst=dst, tt=tt: e.scalar_tensor_tensor(
                        out=dst, in0=bank_f(bk), scalar=r2sq[:, tt:tt + 1], in1=dst,
                        op0=ALU.mult, op1=ALU.add),
                        reads=hb[bk] + [b_r2, b_x1[tt]], writes=[b_x1[tt]])
                if fq == 3:
                    t.dma("sp", out_d[tt * 128:(tt + 1) * 128, :], x1[:, tt, :], reads=[b_x1[tt]])

        ab_i = 0
        pending = None
        for fq in range(4):
            if fq >= 1 and fq + 1 < 4:
                pass
            for tg in (1, 2, 3, 0):
                ab = ab_i % 2
                ab_i += 1
                ff1(fq, tg, ab)
                if fq == 0:
                    p6_stats(tg)
                if pending is not None:
                    ff2(*pending)
                pending = (fq, tg, ab)
            if fq + 2 < 4:
                ff2(*pending)
                pending = None
                load_ff(fq + 2)
        ff2(*pending)

        t.wait_all("sp")
    return nc, list(dbg_d.keys())


def _host_consts():
    ident = np.eye(128, dtype=np.float32)
    half = 8
    inv = (500000.0 ** (-np.arange(half, dtype=np.float32) / half)).astype(np.float32)
    ang = np.arange(S, dtype=np.float32)[:, None] * inv[None, :]
    cos = np.cos(ang).astype(np.float32)
    sin = np.sin(ang).astype(np.float32)
    rope = np.concatenate([cos, cos, -sin, sin], axis=1).astype(np.float32)
    rope = np.ascontiguousarray(rope.reshape(NT, 128, 32).transpose(1, 0, 2).reshape(128, NT * 32))
    band = np.zeros((128, 12, 128), dtype=np.float32)
    tp = np.arange(128)[:, None]
    tq = np.arange(128)[None, :]
    for g, w in enumerate((2, 4, 8, 16)):
        inwin = (tp <= tq) & (tp > tq - w)
        main = np.where(inwin, 1.0 / w, 0.0) - np.eye(128)
        prev = np.where((tp - 128) > (tq - w), 1.0 / w, 0.0)
        cnt = np.minimum(tq + 1, w).astype(np.float64)
        main0 = np.where(inwin, 1.0 / cnt, 0.0) - np.eye(128)
        band[:, g * 3 + 0, :] = main
        band[:, g * 3 + 1, :] = prev
        band[:, g * 3 + 2, :] = main0
    return ident, rope, band


_CACHE = {}


def kernel(x, norm_mix, w_in, q_norm, k_norm, w_pool_grp, pool_scale, w_up_attn, w_up_pool,
           w_out, norm_mlp, w_ff1, w_ff2, _debug=(), _cores=N_CORES, _stop=None):
    f = lambda a: np.ascontiguousarray(np.asarray(a, dtype=np.float32))
    x = f(x)
    ident, rope, band = _host_consts()
    gq, gk = f(q_norm)[0], f(k_norm)[0]
    small = np.zeros((128, 24), dtype=np.float32)
    small[:, 0:8] = f(norm_mix)[0].reshape(8, 128).T
    small[:, 8:16] = f(norm_mlp)[0].reshape(8, 128).T
    small[:, 16:20] = f(pool_scale)[0].reshape(4, 128).T
    small[:, 20] = np.concatenate([gq, gq])
    small[:, 21] = np.concatenate([gk, gk])
    gv = np.concatenate([gq[0:16], gq[8:16], gq[0:8], gk[0:16], gk[8:16], gk[0:8]]).astype(np.float32)
    shared = {
        "w_in": f(w_in)[0], "w_pool_grp": f(w_pool_grp)[0], "w_up_attn": f(w_up_attn)[0],
        "w_up_pool": f(w_up_pool)[0], "w_out": f(w_out)[0], "w_ff1": f(w_ff1)[0], "w_ff2": f(w_ff2)[0],
        "c_small": small, "c_gv": gv, "gmlp_bc": np.ascontiguousarray(np.broadcast_to(f(norm_mlp)[0], (128, D))),
        "gmix_bc": np.ascontiguousarray(np.broadcast_to(f(norm_mix)[0], (128, D))),
        "c_ident": ident, "c_rope": rope, "c_band": band,
        "c_zero": np.zeros((64, 1024), dtype=np.float32),
        "c_ind": (np.arange(S)[None, :] // 256 == np.arange(8)[:, None]).astype(np.float32),
    }
    key = (tuple(_debug), _stop)
    if key not in _CACHE:
        _CACHE[key] = build_nc(debug=_debug, stop=_stop)
    nc, dbg_names = _CACHE[key]
    in_maps = []
    for b in range(_cores):
        m = dict(shared)
        m["x"] = x[b]
        in_maps.append(m)
    res = run_bass_kernel_spmd(nc, in_maps, core_ids=list(range(_cores)))
    out = np.stack([res.results[b]["out"] for b in range(_cores)], axis=0)
    if _debug:
        return out, {n: res.results[0]["dbg_" + n] for n in dbg_names}
    return out
```

```python
import numpy as np
from contextlib import ExitStack

import concourse.bass as bass
import concourse.mybir as mybir
from concourse.bass_utils import run_bass_kernel_spmd

F32 = mybir.dt.float32
BF16 = mybir.dt.bfloat16
AF = mybir.ActivationFunctionType
ALU = mybir.AluOpType
AX = mybir.AxisListType

S = 2048
D = 1024
NT = 16
H = 8
DH = 64
NB = 8
NEG = -240000.0
EPS = 1e-6
N_CORES = 8


class Buf:
    __slots__ = ("name", "w", "r", "fence", "excl")

    def __init__(self, name, fence=None, excl=False):
        self.name = name
        self.excl = excl
        self.w = None
        self.r = {}
        self.fence = dict(fence) if fence else None


class Trk:
    def __init__(self, nc, es, n_ring=12):
        self.nc = nc
        self.E = {"pe": nc.tensor, "act": nc.scalar, "dve": nc.vector,
                  "pool": nc.gpsimd, "sp": nc.sync}
        self.sems = {}
        for k in self.E:
            self.sems[k] = es.enter_context(nc.semaphore("s_" + k))
        self.cnt = {k: 0 for k in self.E}
        self.seen = {k: {} for k in self.E}
        self.rings = {}
        for q in ("sp", "pool"):
            slots = []
            for j in range(n_ring):
                key = "d_%s%d" % (q, j)
                self.sems[key] = es.enter_context(nc.semaphore(key))
                slots.append([key, 0])
            self.rings[q] = {"i": 0, "slots": slots}

    def snapshot(self):
        snap = {k: v for k, v in self.cnt.items() if v > 0}
        for q in self.rings.values():
            for key, tot in q["slots"]:
                if tot > 0:
                    snap[key] = tot
        return snap

    @staticmethod
    def _add(need, k, v):
        if need.get(k, 0) < v:
            need[k] = v

    def _need(self, reads, writes, e=None):
        need = {}
        for b in reads:
            if b.w:
                self._add(need, *b.w)
            if b.excl:
                for k, v in b.r.items():
                    if k != e:
                        self._add(need, k, v)
            if b.fence:
                for k, v in b.fence.items():
                    self._add(need, k, v)
        for b in writes:
            if b.w:
                self._add(need, *b.w)
            for k, v in b.r.items():
                self._add(need, k, v)
            if b.fence:
                for k, v in b.fence.items():
                    self._add(need, k, v)
        return need

    def _wait(self, e, need):
        eng = self.E[e]
        seen = self.seen[e]
        for k, v in need.items():
            if e == "pe" and k == "pe":
                continue
            if seen.get(k, 0) < v:
                eng.wait_ge(self.sems[k], v)
                seen[k] = v

    def _mark(self, key, val, reads, writes):
        for b in reads:
            if b.r.get(key, 0) < val:
                b.r[key] = val
        for b in writes:
            b.w = (key, val)
            b.r = {}

    def op(self, e, fn, reads=(), writes=()):
        self._wait(e, self._need(reads, writes, e))
        ins = fn(self.E[e])
        self.cnt[e] += 1
        ins.then_inc(self.sems[e], 1)
        self._mark(e, self.cnt[e], reads, writes)

    def group(self, e, fns, reads=(), writes=()):
        self._wait(e, self._need(reads, writes, e))
        ins = None
        for fn in fns:
            ins = fn(self.E[e])
        self.cnt[e] += 1
        ins.then_inc(self.sems[e], 1)
        self._mark(e, self.cnt[e], reads, writes)

    def dma(self, q, out, in_, reads=(), writes=()):
        need = self._need(reads, writes)
        ring = self.rings[q]
        slot = ring["slots"][ring["i"] % len(ring["slots"])]
        ring["i"] += 1
        if slot[1] > 0:
            self._add(need, slot[0], slot[1])
        self._wait(q, need)
        ins = self.E[q].dma_start(out=out, in_=in_)
        slot[1] += 16
        ins.then_inc(self.sems[slot[0]], 16)
        self._mark(slot[0], slot[1], reads, writes)

    def wait_all(self, e):
        self._wait(e, self.snapshot())

    def wait_written(self, e, buf):
        if buf.w:
            self._wait(e, {buf.w[0]: buf.w[1]})


def build_nc(debug=(), stop=None):
    nc = bass.Bass("TRN2", target_bir_lowering=False)

    def din(name, shape):
        return nc.dram_tensor(name, list(shape), F32, kind="ExternalInput").ap()

    x_d = din("x", [S, D])
    w_in_d = din("w_in", [D, 4096])
    w_grp_d = din("w_pool_grp", [4, 128, 128])
    w_ua_d = din("w_up_attn", [512, D])
    w_up_d = din("w_up_pool", [512, D])
    w_out_d = din("w_out", [D, D])
    w_ff1_d = din("w_ff1", [D, 4096])
    w_ff2_d = din("w_ff2", [4096, D])
    c_small_d = din("c_small", [128, 24])
    gmlp_vec_d = din("gmlp_bc", [128, D])
    gmix_vec_d = din("gmix_bc", [128, D])
    c_gv_d = din("c_gv", [64])
    c_ident_d = din("c_ident", [128, 128])
    c_rope_d = din("c_rope", [128, NT * 32])
    c_band_d = din("c_band", [128, 12, 128])
    c_zero_d = din("c_zero", [64, 1024])
    c_ind_d = din("c_ind", [8, S])
    out_d = nc.dram_tensor("out", [S, D], F32, kind="ExternalOutput").ap()
    dbg_d = {}

    es = ExitStack()
    with es:
        RAWB = 103 * 2048 + 256
        raw = nc.alloc_sbuf_tensor("raw", [128, RAWB // 2], BF16)

        def view(off, shape, dt, p0=0):
            esz = 2 if dt == BF16 else 4
            n = 1
            for s_ in shape[1:]:
                n *= s_
            assert off % 4 == 0 and off + n * esz <= RAWB, (off, shape)
            v = raw[p0:p0 + shape[0], off // 2: off // 2 + n * esz // 2]
            if dt != BF16:
                v = v.bitcast(dt)
            if len(shape) == 3:
                v = v.rearrange("p (a b) -> p a b", a=shape[1])
            elif len(shape) == 4:
                v = v.rearrange("p (a b c) -> p a b c", a=shape[1], b=shape[2])
            return v

        KB = 1024
        A0 = 0
        B0 = 7 * KB
        C0 = B0 + 32 * KB
        D0 = C0 + 64 * KB + 256
        E0 = D0 + 32 * KB
        F0 = E0 + 32 * KB
        G0 = F0 + 32 * KB

        ident_f = view(A0 + 0, [128, 128], F32)
        ident_b = view(A0 + 512, [128, 128], BF16)
        tri01 = view(A0 + 768, [128, 128], BF16)
        rope_t = view(A0 + 1024, [128, NT, 32], F32)
        small_c = view(A0 + 3072, [128, 24], F32)
        gmix_c = small_c[:, 0:8]
        gmlp_c = small_c[:, 8:16]
        psc_c = small_c[:, 16:20]
        gcolT_q = small_c[:, 20:21]
        gcolT_k = small_c[:, 21:22]
        gv_qk = view(A0 + 3200, [128, 64], F32)
        gv_q = gv_qk[:, 0:32]
        gv_k = gv_qk[:, 32:64]
        ones_b = view(A0 + 3664, [128, 2], BF16)
        band = view(A0 + 3680, [128, 12, 128], BF16)
        wgrp = view(G0 + 0, [128, 4, 128], BF16)
        ss1 = view(G0 + 1024, [128, NT], F32)
        rstd1 = view(G0 + 1088, [128, NT], F32)
        ss2 = view(G0 + 1152, [128, NT], F32)
        r2sq = view(G0 + 1216, [128, NT], F32)
        kmeanT = view(G0 + 1280, [128, 4, 8], BF16)
        ksum_sb = view(G0 + 1344, [128, 4, 8], F32)
        stat_a = [view(G0 + 5632 + i * 64, [128, 8], F32) for i in range(2)]
        stat_b = [view(G0 + 5760 + i * 64, [128, 8], F32) for i in range(2)]
        junk_h = view(G0 + 5888, [128, 512], BF16)
        ss2h = view(G0 + 6912, [128, 2 * NT], F32)

        hT = view(B0, [128, 8, S], BF16)
        qTz = view(C0, [128, H, S], BF16)
        kT = view(C0 + 32 * KB, [128, 4, S], BF16)
        v_aug = view(C0 + 48 * KB, [128, NT, H, 65], BF16)
        assert C0 + 48 * KB + 16640 <= D0
        x1 = view(C0, [128, NT, D], F32)
        _sb = [F0, C0]
        wga = [view(_sb[s], [128, 8, 128], BF16) for s in range(2)]
        wgp = [view(_sb[s] + 2 * KB, [128, 8, 128], BF16) for s in range(2)]
        wua = [view(_sb[s] + 4 * KB, [128, 4, 128], BF16) for s in range(2)]
        wup = [view(_sb[s] + 5 * KB, [128, 4, 128], BF16) for s in range(2)]
        ga_sb = [view(C0 + 6 * KB + s * 4 * KB, [128, 512], F32) for s in range(2)]
        gp_sb = [view(C0 + 6 * KB + s * 4 * KB + 2 * KB, [128, 512], F32) for s in range(2)]
        xslot = [view(D0 + i * 4 * KB, [128, D], F32) for i in range(8)]
        u_sb = view(D0, [128, NT, 512], BF16)
        pooledT = view(D0 + 16 * KB, [128, 4, S], BF16)
        mT = view(D0, [128, 8, S], BF16)
        a1T = [view(D0 + s * 8 * KB, [128, 8, 512], BF16) for s in range(2)]
        sqz = [view(D0 + 16 * KB + s * 2 * KB, [128, 512], F32) for s in range(2)]
        o_attnT = view(E0, [128, 4, S], BF16)
        junk = view(E0, [128, D], BF16)
        gmix_bc = view(E0 + 2 * KB, [128, D], F32)
        hb0 = [view(E0 + 6 * KB + i * 2 * KB, [128, D], BF16) for i in range(2)]
        E1 = E0 + 16 * KB
        sq_sb = [view(E1 + i * 2 * KB, [128, 512], F32) for i in range(2)]
        qtok = [view(E1 + 4 * KB + i * KB, [128, 512], BF16) for i in range(2)]
        x16 = [view(E1 + 6 * KB + i * 512, [128, 8, 16], F32) for i in range(2)]
        rt1 = [view(E1 + 7 * KB + i * 512, [128, 8, 16], F32) for i in range(2)]
        rt2 = [view(E1 + 8 * KB + i * 512, [128, 8, 16], F32) for i in range(2)]
        rope_q = view(E1 + 9 * KB, [128, NT, 32], F32)
        rope_k = view(E1 + 11 * KB, [128, NT, 32], F32)
        yT = view(E0 + 16 * KB, [128, 4, S], BF16)
        wqkvu = [view(F0 + i * 8 * KB, [128, 8, 512], BF16) for i in range(4)]
        wout = view(F0 + 8 * KB, [128, 8, D], BF16)
        kT2 = view(F0 + 8 * KB, [128, 4, S], BF16)
        xt5 = [view(F0 + 24 * KB + i * 4 * KB, [128, D], F32) for i in range(2)]
        gmlp_bc = view(F0, [128, D], F32)
        hb16 = [view(F0 + 4 * KB + i * 2 * KB, [128, D], BF16) for i in range(2)]
        f2 = F0 + 24 * KB
        PT = [view(f2 + i * 1024, [128, 512], BF16) for i in range(5)]
        o_tok = [view(f2 + 5120 + i * 1024, [128, 512], BF16) for i in range(2)]
        rden = [view(G0 + 7040 + i * 16, [128, 4], F32) for i in range(4)]
        assert f2 + 7168 <= F0 + 31 * KB
        gate_sb = view(F0 + 31 * KB, [128, 8, 8], F32)
        m8_sb = view(F0 + 31 * KB + 256, [128, 8, 8], F32)
        sel_sb = view(F0 + 31 * KB + 512, [128, 8, 8], F32)
        bias8 = [view(F0 + 6 * KB, [128, H, 128], BF16), view(D0 + 16 * KB, [128, H, 128], BF16)]
        rank_sb = view(D0 + 18 * KB, [128, H, 8, 8], F32)
        ffw1 = [view(E0, [128, 8, 1024], BF16), view(F0, [128, 8, 1024], BF16)]
        ffw2 = [view(E0 + 16 * KB, [128, 8, 1024], BF16), view(F0 + 16 * KB, [128, 8, 1024], BF16)]

        banks = [es.enter_context(nc.psum_tensor("bank%d" % i, [128, 512], F32)) for i in range(8)]
        hb = []
        for i in range(8):
            pb_ = Buf("psbank%d" % i, excl=True)
            hb.append([pb_, pb_])

        def bank_f(i):
            return banks[i][:, :]

        def bank_bf(i):
            return banks[i][:, :].bitcast(BF16)

        t = Trk(nc, es)

        def dbg_dump(name, ap, shape, bufs):
            if name not in debug:
                return
            d = nc.dram_tensor("dbg_" + name, list(shape), ap.dtype, kind="ExternalOutput").ap()
            dbg_d[name] = d
            t.dma("sp", d, ap, reads=bufs)

        b_wqkvu = [Buf("wqkvu%d" % i) for i in range(4)]
        w_in_v = w_in_d.rearrange("(c p) n -> p c n", p=128)
        CG_K, CG_Q, CG_V, CG_U = 1, 0, 2, 3
        b_c = {n: Buf(n) for n in ("ident_f", "ident_b", "rope", "gvq", "gvk", "gmix", "gmlp",
                                    "psc", "ones", "band", "wgrp", "gcq", "gck", "ropeq", "ropek",
                                    "qz")}
        t.dma("sp", ident_f, c_ident_d, writes=[b_c["ident_f"]])
        t.dma("sp", small_c, c_small_d, writes=[b_c["gmix"], b_c["gmlp"], b_c["psc"], b_c["gcq"], b_c["gck"]])
        t.dma("sp", gv_qk, c_gv_d.partition_broadcast(128), writes=[b_c["gvq"], b_c["gvk"]])
        t.dma("sp", rope_t, c_rope_d.rearrange("p (t n) -> p t n", t=NT), writes=[b_c["rope"]])
        t.dma("pool", wqkvu[CG_K], w_in_v[:, :, CG_K * 512:(CG_K + 1) * 512], writes=[b_wqkvu[CG_K]])
        t.dma("pool", ident_b, c_ident_d, writes=[b_c["ident_b"]])

        def setup_gains(gv, gc, rp, nm_gv, nm_gc, nm_rp):
            t.op("dve", lambda e: e.memset(gc[0:16, :], 1.0), writes=[b_c[nm_gc]])
            t.op("dve", lambda e: e.memset(gc[64:80, :], 1.0), writes=[b_c[nm_gc]])
            t.op("dve", lambda e: e.tensor_tensor(
                out=rp, in0=rope_t, in1=gv.unsqueeze(1).to_broadcast([128, NT, 32]), op=ALU.mult),
                reads=[b_c["rope"], b_c[nm_gv]], writes=[b_c[nm_rp]])

        b_tri = Buf("tri01")
        b_eps = Buf("eps")
        eps_c = small_c[:, 22:23]

        def small_setup_k():
            t.op("dve", lambda e: e.memset(eps_c, EPS), reads=[b_c["gmix"]], writes=[b_eps])
            t.op("pool", lambda e: e.memset(tri01, 1.0), writes=[b_tri])
            t.op("pool", lambda e: e.affine_select(
                out=tri01, in_=tri01, pattern=[[1, 128]], compare_op=ALU.is_ge, fill=0.0, base=0,
                channel_multiplier=-1), reads=[b_tri], writes=[b_tri])
            t.op("dve", lambda e: e.memset(ones_b, 1.0), writes=[b_c["ones"]])
            setup_gains(gv_k, gcolT_k, rope_k, "gvk", "gck", "ropek")

        def small_setup_rest():
            t.wait_written("pool", b_xslot[7])
            t.dma("pool", wqkvu[CG_Q], w_in_v[:, :, CG_Q * 512:(CG_Q + 1) * 512], writes=[b_wqkvu[CG_Q]])
            setup_gains(gv_q, gcolT_q, rope_q, "gvq", "gcq", "ropeq")
            t.dma("pool", wqkvu[CG_U], w_in_v[:, :, CG_U * 512:(CG_U + 1) * 512], writes=[b_wqkvu[CG_U]])
            t.dma("pool", band, c_band_d, writes=[b_c["band"]])
            t.dma("pool", wgrp, w_grp_d.rearrange("g c d -> c g d"), writes=[b_c["wgrp"]])

        def zero_fill():
            for h in range(H):
                base = 64 if h % 2 == 0 else 0
                t.dma("sp", qTz[base:base + 64, h, :].bitcast(F32), c_zero_d, writes=[b_c["qz"]])

        b_xslot = [Buf("xslot%d" % i) for i in range(8)]
        b_ss1 = Buf("ss1")
        b_rstd1 = Buf("rstd1")
        b_junk = Buf("junk")
        b_hT = [Buf("hT_%d" % i) for i in range(NT)]
        ev_cnt = [0]
        tp_cnt = [0]

        b_gmix_bc = Buf("gmix_bc")
        b_hb0 = [Buf("hb0_%d" % i) for i in range(2)]
        p0_bank = {}

        def p0_load1(tt):
            sl = tt % 8
            t.dma("sp", xslot[sl], x_d[tt * 128:(tt + 1) * 128, :], writes=[b_xslot[sl]])

        def p0_sq(tt):
            sl = tt % 8
            t.op("act", lambda e: e.activation(
                out=junk, in_=xslot[sl], func=AF.Square, accum_out=ss1[:, tt:tt + 1]),
                reads=[b_xslot[sl]], writes=[b_junk, b_ss1])

        def p0_stats(tt):
            c0, c1 = tt, tt + 1
            t.op("act", lambda e: e.activation(out=rstd1[:, c0:c1], in_=ss1[:, c0:c1], func=AF.Sqrt,
                                               scale=1.0 / D, bias=eps_c[:, 0:1]),
                 reads=[b_ss1, b_eps], writes=[b_rstd1])
            t.op("dve", lambda e: e.reciprocal(out=rstd1[:, c0:c1], in_=rstd1[:, c0:c1]),
                 reads=[b_rstd1], writes=[b_rstd1])

        def p0_mul(tt):
            sl = tt % 8
            s2 = tt % 2
            t.op("dve", lambda e: e.scalar_tensor_tensor(
                out=hb0[s2], in0=xslot[sl], scalar=rstd1[:, tt:tt + 1], in1=gmix_bc,
                op0=ALU.mult, op1=ALU.mult),
                reads=[b_xslot[sl], b_rstd1, b_gmix_bc], writes=[b_hb0[s2]])

        def p0_tr1(tt):
            s2 = tt % 2
            bk = tp_cnt[0] % 3
            tp_cnt[0] += 1
            tps = bank_bf(bk)
            t.group("pe", [lambda e, c=c: e.transpose(
                out=tps[:, c * 128:(c + 1) * 128], in_=hb0[s2][:, c * 128:(c + 1) * 128],
                identity=ident_b) for c in range(8)],
                reads=[b_hb0[s2], b_c["ident_b"]], writes=hb[bk])
            t.op("act", lambda e: e.activation(
                out=hT[:, :, tt * 128:(tt + 1) * 128], in_=tps.rearrange("p (c t) -> p c t", c=8),
                func=AF.Copy),
                reads=hb[bk], writes=[b_hT[tt]])

        b_sq = [Buf("sq%d" % i) for i in range(2)]
        b_x16 = [Buf("x16%d" % i) for i in range(2)]
        b_qtok = [Buf("qtok%d" % i) for i in range(2)]
        b_rt1 = [Buf("rt1%d" % i) for i in range(2)]
        b_rt2 = [Buf("rt2%d" % i) for i in range(2)]
        b_sta = [Buf("sta%d" % i) for i in range(2)]
        b_stb = [Buf("stb%d" % i) for i in range(2)]
        b_qT = [Buf("qT%d" % i) for i in range(NT)]
        b_kT = [Buf("kT%d" % i) for i in range(NT)]
        b_v = [Buf("v%d" % i) for i in range(NT)]
        b_vones = Buf("vones")
        b_kmean = Buf("kmeanT")
        b_gate = Buf("gate")
        b_m8 = Buf("m8")
        b_sel = Buf("sel")
        b_rank = Buf("rank")
        b_btok2 = [Buf("bias8_%d" % i) for i in range(2)]
        b_ksum_ps = hb[3][0]
        b_gate_ps = hb[3][0]
        b_bias_ps = hb[3][0]
        ksum_ps = bank_f(3)[:, 0:128].rearrange("p (a b c) -> p a b c", a=4, b=NT)
        gate_ps = bank_f(3)[:, 128:192].rearrange("p (h n) -> p h n", h=H)
        bias_ps = bank_bf(3)[:, 512:640]
        pj_banks = [4, 5, 6, 7]

        jobs = []
        for cga, cgb in ((CG_K, CG_V), (CG_Q, CG_U)):
            for tt in range(NT):
                jobs.append((cga, tt))
                jobs.append((cgb, tt))
        qk_slot = {}
        for ji, (cg, tt) in enumerate(jobs):
            if cg in (CG_K, CG_Q):
                qk_slot[ji] = len(qk_slot) % 2

        def proj_mm(ji):
            cg, tt = jobs[ji]
            bk = pj_banks[ji % 4]
            fns = []
            for c in range(8):
                fns.append(lambda e, c=c: e.matmul(
                    bank_f(bk), lhsT=hT[:, c, tt * 128:(tt + 1) * 128], rhs=wqkvu[cg][:, c, :],
                    start=(c == 0), stop=(c == 7)))
            t.group("pe", fns, reads=[b_hT[tt], b_wqkvu[cg]], writes=hb[bk])

        def post_a2(ji):
            cg, tt = jobs[ji]
            if cg in (CG_K, CG_Q):
                i2 = qk_slot[ji]
                sq3 = sq_sb[i2].rearrange("p (h d) -> p h d", h=H)
                t.op("dve", lambda e: e.tensor_reduce(out=stat_a[i2], in_=sq3, axis=AX.X, op=ALU.add),
                     reads=[b_sq[i2]], writes=[b_sta[i2]])
                t.op("act", lambda e: e.activation(out=stat_b[i2], in_=stat_a[i2], func=AF.Sqrt,
                                                   scale=1.0 / DH, bias=eps_c[:, 0:1]),
                     reads=[b_sta[i2], b_eps], writes=[b_stb[i2]])

        def post_a1(ji):
            cg, tt = jobs[ji]
            bk = pj_banks[ji % 4]
            if cg in (CG_K, CG_Q):
                i2 = qk_slot[ji]
                t.op("act", lambda e: e.activation(out=sq_sb[i2], in_=bank_f(bk), func=AF.Square),
                     reads=hb[bk], writes=[b_sq[i2]])
            elif cg == CG_V:
                t.op("dve", lambda e: e.tensor_copy(
                    out=v_aug[:, tt, :, 0:64], in_=bank_f(bk).rearrange("p (h d) -> p h d", h=H)),
                    reads=hb[bk] + [b_vones], writes=[b_v[tt]])
            else:
                t.op("act", lambda e: e.activation(out=u_sb[:, tt, :], in_=bank_f(bk), func=AF.Copy),
                     reads=hb[bk], writes=[b_u[tt]])

        def post_b(ji):
            cg, tt = jobs[ji]
            if cg not in (CG_K, CG_Q):
                return
            is_k = cg == CG_K
            bk = pj_banks[ji % 4]
            i2 = qk_slot[ji]
            rp = rope_k if is_k else rope_q
            rb = b_c["ropek"] if is_k else b_c["ropeq"]
            ps3 = bank_f(bk).rearrange("p (h d) -> p h d", h=H)
            qt3 = qtok[i2].rearrange("p (h d) -> p h d", h=H)
            t.op("dve", lambda e: e.reciprocal(out=stat_b[i2], in_=stat_b[i2]),
                 reads=[b_stb[i2]], writes=[b_stb[i2]])
            t.op("dve", lambda e: e.tensor_tensor(
                out=qt3, in0=ps3, in1=stat_b[i2].unsqueeze(2).to_broadcast([128, H, DH]), op=ALU.mult),
                reads=hb[bk] + [b_stb[i2]], writes=[b_qtok[i2]])
            t.op("dve", lambda e: e.tensor_tensor(
                out=x16[i2], in0=ps3[:, :, 0:16], in1=stat_b[i2].unsqueeze(2).to_broadcast([128, H, 16]),
                op=ALU.mult),
                reads=hb[bk] + [b_stb[i2]], writes=[b_x16[i2]])
            cs_b = rp[:, tt, 0:16].unsqueeze(1).to_broadcast([128, H, 16])
            sn_lo = rp[:, tt, 16:24].unsqueeze(1).to_broadcast([128, H, 8])
            sn_hi = rp[:, tt, 24:32].unsqueeze(1).to_broadcast([128, H, 8])
            reng = "pool" if is_k else "dve"
            t.op(reng, lambda e: e.tensor_tensor(out=rt1[i2], in0=x16[i2], in1=cs_b, op=ALU.mult),
                 reads=[b_x16[i2], rb], writes=[b_rt1[i2]])
            t.op(reng, lambda e: e.tensor_tensor(out=rt2[i2][:, :, 0:8], in0=x16[i2][:, :, 8:16], in1=sn_lo,
                                                  op=ALU.mult),
                 reads=[b_x16[i2], rb], writes=[b_rt2[i2]])
            t.op(reng, lambda e: e.tensor_tensor(out=rt2[i2][:, :, 8:16], in0=x16[i2][:, :, 0:8], in1=sn_hi,
                                                  op=ALU.mult),
                 reads=[b_x16[i2], rb, b_rt2[i2]], writes=[b_rt2[i2]])
            t.op(reng, lambda e: e.tensor_tensor(out=qt3[:, :, 0:16], in0=rt1[i2], in1=rt2[i2], op=ALU.add),
                 reads=[b_rt1[i2], b_rt2[i2], b_qtok[i2]], writes=[b_qtok[i2]])

        def post_b_pe(ji, on_dve=False):
            cg, tt = jobs[ji]
            if cg not in (CG_K, CG_Q):
                return
            is_k = cg == CG_K
            i2 = qk_slot[ji]
            tbk = tp_cnt[0] % 3
            tp_cnt[0] += 1
            tps = bank_bf(tbk)[:, 0:512]
            fns = []
            for pr in range(4):
                fns.append(lambda e, pr=pr: e.transpose(
                    out=tps[:, pr * 128:(pr + 1) * 128], in_=qtok[i2][:, pr * 128:(pr + 1) * 128],
                    identity=ident_b))
            wr = [hb[tbk][0]]
            if is_k:
                for pr in range(4):
                    fns.append(lambda e, pr=pr: e.matmul(
                        ksum_ps[:, pr, tt, :], lhsT=qtok[i2][:, pr * 128:(pr + 1) * 128], rhs=ones_b,
                        start=True, stop=True))
                wr.append(b_ksum_ps)
            t.group("pe", fns, reads=[b_qtok[i2], b_c["ident_b"], b_c["ones"]], writes=wr)
            tps3 = tps.rearrange("p (a b) -> p a b", a=4)
            if is_k:
                t.op("act", lambda e: e.activation(out=kT[:, :, tt * 128:(tt + 1) * 128], in_=tps3,
                                                   func=AF.Copy, scale=gcolT_k[:, 0:1]),
                     reads=[hb[tbk][0], b_c["gck"]], writes=[b_kT[tt]])
            elif on_dve:
                for p0_, hs_ in ((0, 0), (64, 1)):
                    t.op("dve", lambda e, p0_=p0_, hs_=hs_: e.tensor_scalar(
                        out=qTz[p0_:p0_ + 64, hs_:H:2, tt * 128:(tt + 1) * 128], in0=tps3[p0_:p0_ + 64],
                        scalar1=gcolT_q[p0_:p0_ + 64, 0:1], scalar2=None, op0=ALU.mult),
                        reads=[hb[tbk][0], b_c["gcq"], b_c["qz"], b_qT[tt]], writes=[b_qT[tt]])
            else:
                t.op("act", lambda e: e.activation(
                    out=qTz[0:64, 0:H:2, tt * 128:(tt + 1) * 128], in_=tps3[0:64],
                    func=AF.Copy, scale=gcolT_q[0:64, 0:1]),
                    reads=[hb[tbk][0], b_c["gcq"], b_c["qz"]], writes=[b_qT[tt]])
                t.op("act", lambda e: e.activation(
                    out=qTz[64:128, 1:H:2, tt * 128:(tt + 1) * 128], in_=tps3[64:128],
                    func=AF.Copy, scale=gcolT_q[64:128, 0:1]),
                    reads=[hb[tbk][0], b_c["gcq"], b_c["qz"], b_qT[tt]], writes=[b_qT[tt]])

        def gate_mm(tt):
            fns = []
            for h in range(H):
                fns.append(lambda e, h=h: e.matmul(
                    gate_ps[:, h, :], lhsT=qTz[:, h, tt * 128:(tt + 1) * 128], rhs=kmeanT[:, h // 2, :],
                    start=True, stop=True))
            t.group("pe", fns, reads=[b_qT[tt], b_kmean], writes=[b_gate_ps])

        def gate_sel(tt):
            qb = tt // 2
            t.op("dve", lambda e: e.memset(gate_sb, -1.0e30), writes=[b_gate])
            t.op("dve", lambda e: e.tensor_copy(out=gate_sb[:, :, 0:qb], in_=gate_ps[:, :, 0:qb]),
                 reads=[b_gate_ps, b_gate], writes=[b_gate])
            g_m = gate_sb.unsqueeze(2).to_broadcast([128, H, 8, 8])
            g_n = gate_sb.unsqueeze(3).to_broadcast([128, H, 8, 8])
            t.op("dve", lambda e: e.tensor_tensor(out=rank_sb, in0=g_m, in1=g_n, op=ALU.is_gt),
                 reads=[b_gate, b_rank], writes=[b_rank])

        def gate_sel_b(tt):
            qb = tt // 2
            t.op("dve", lambda e: e.tensor_reduce(out=m8_sb, in_=rank_sb, axis=AX.X, op=ALU.add),
                 reads=[b_rank, b_m8], writes=[b_m8])
            t.op("dve", lambda e: e.tensor_scalar(out=sel_sb, in0=m8_sb, scalar1=2.5, scalar2=None,
                                                  op0=ALU.is_lt),
                 reads=[b_m8, b_sel], writes=[b_sel])
            b8, b_b8 = bias8[tt % 2], b_btok2[tt % 2]
            t.op("dve", lambda e: e.tensor_scalar(
                out=b8[:, 0:H:2, 64:64 + qb], in0=sel_sb[:, 0:H:2, 0:qb], scalar1=-1.0, scalar2=-NEG,
                op0=ALU.add, op1=ALU.mult),
                reads=[b_sel, b_b8], writes=[b_b8])
            t.op("dve", lambda e: e.tensor_scalar(
                out=b8[:, 1:H:2, 0:qb], in0=sel_sb[:, 1:H:2, 0:qb], scalar1=-1.0, scalar2=-NEG,
                op0=ALU.add, op1=ALU.mult),
                reads=[b_sel, b_b8], writes=[b_b8])

        def gate_tr(tt):
            b8, b_b8 = bias8[tt % 2], b_btok2[tt % 2]
            tbk = 3
            tps = bank_bf(tbk)
            t.group("pe", [lambda e, h=h: e.transpose(out=tps[:, h * 128:(h + 1) * 128], in_=b8[:, h, :],
                                                      identity=ident_b) for h in range(H)],
                    reads=[b_b8, b_c["ident_b"]], writes=[hb[tbk][0]])
            tp3 = tps.rearrange("p (h t) -> p h t", h=H)
            t.op("dve", lambda e: e.tensor_copy(out=qTz[64:72, 0:H:2, tt * 128:(tt + 1) * 128],
                                                in_=tp3[64:72, 0:H:2, :]),
                 reads=[hb[tbk][0], b_c["qz"], b_qT[tt]], writes=[b_qT[tt]])
            t.op("dve", lambda e: e.tensor_copy(out=qTz[0:8, 1:H:2, tt * 128:(tt + 1) * 128],
                                                in_=tp3[0:8, 1:H:2, :]),
                 reads=[hb[tbk][0], b_c["qz"], b_qT[tt]], writes=[b_qT[tt]])

        def kmean_fin():
            ks3 = bank_f(3)[:, 0:128].rearrange("p (a n c) -> p a n c", a=4, n=NB)
            ks3 = ks3[:, :, :, 0:4:2]
            t.op("dve", lambda e: e.tensor_reduce(out=ksum_sb, in_=ks3, axis=AX.X, op=ALU.add),
                 reads=[b_ksum_ps], writes=[b_kmean])
            t.op("dve", lambda e: e.tensor_scalar(out=kmeanT, in0=ksum_sb, scalar1=gcolT_k[:, 0:1],
                                                  scalar2=1.0 / 256.0, op0=ALU.mult, op1=ALU.mult),
                 reads=[b_kmean, b_c["gck"]], writes=[b_kmean])

        b_kT2 = Buf("kT2")

        def make_kT2():
            b_kT2.fence = t.snapshot()
            t.dma("sp", kT2, kT, reads=b_kT, writes=[b_kT2])
            for pr in range(4):
                t.dma("pool", kT[64:72, pr, :], c_ind_d, writes=b_kT)
                t.dma("pool", kT2[0:8, pr, :], c_ind_d, writes=[b_kT2])

        later = []

        def run_later(it):
            k = 0
            while k < len(later):
                if later[k][0] <= it:
                    later.pop(k)[1]()
                else:
                    k += 1

        p0_sched = {}

        def p0_at(it_, fn, *a):
            p0_sched.setdefault(max(it_, -1), []).append((fn, a))

        for tt_ in range(2, NT):
            p0_at(2 * tt_ - 6, p0_sq, tt_)
            p0_at(2 * tt_ - 5, p0_stats, tt_)
            p0_at(2 * tt_ - 4, p0_mul, tt_)
            p0_at(2 * tt_ - 2, p0_tr1, tt_)
        for tt_ in range(8, NT):
            p0_at(max(0, 2 * (tt_ - 8) - 1), p0_load1, tt_)

        p0_load1(0)
        t.dma("sp", gmix_bc, gmix_vec_d, writes=[b_gmix_bc])
        for tt_ in range(1, 4):
            p0_load1(tt_)
        t.wait_written("pool", b_xslot[3])
        t.dma("pool", wqkvu[CG_V], w_in_v[:, :, CG_V * 512:(CG_V + 1) * 512], writes=[b_wqkvu[CG_V]])
        t.op("dve", lambda e: e.memset(v_aug[:, :, :, 64:65], 1.0), writes=[b_vones])
        small_setup_k()
        for tt_ in range(4, 8):
            p0_load1(tt_)
        for tt_ in range(2):
            p0_sq(tt_)
            p0_stats(tt_)
            p0_mul(tt_)
            if tt_ == 0:
                p0_tr1(tt_)
        p0_at(1, p0_tr1, 1)
        small_setup_rest()
        for fn_, a_ in p0_sched.get(-1, []):
            fn_(*a_)
        fence_p0 = None
        b_u = None
        NJ = len(jobs)
        DEFER_PE = [NJ - 4, NJ - 2]
        for it in range(NJ + 6):
            if stop and stop.startswith("it") and it >= int(stop[2:]):
                t.wait_all("sp")
                return nc, list(dbg_d.keys())
            if it == 9:
                zero_fill()
            for fn_, a_ in p0_sched.get(it, []):
                fn_(*a_)
            if it < NJ:
                if it == 2 * NT:
                    fence_p0 = t.snapshot()
                    b_u = [Buf("u%d" % i, fence=fence_p0) for i in range(NT)]

                proj_mm(it)
            if 0 <= it - 1 < NJ:
                post_a1(it - 1)
            if 0 <= it - 2 < NJ:
                post_a2(it - 2)
            if 0 <= it - 3 < NJ:
                post_b(it - 3)
            if 0 <= it - 5 < NJ:
                jb = it - 5
                if jb not in DEFER_PE:
                    post_b_pe(jb)
                cg, tt = jobs[jb]
                if cg == CG_K and tt == NT - 1:
                    later.append((it + 2, kmean_fin))
                if cg == CG_V and tt == NT - 1:
                    later.append((it + 2, make_kT2))
            run_later(it)
        run_later(10 ** 9)
        dbg_dump("hT", hT, [128, 8, S], b_hT)
        dbg_dump("qTz", qTz, [128, H, S], b_qT)
        dbg_dump("kT", kT, [128, 4, S], b_kT)
        dbg_dump("v", v_aug, [128, NT, H, 65], b_v)
        dbg_dump("u", u_sb, [128, NT, 512], b_u)
        dbg_dump("kmeanT", kmeanT, [128, 4, 8], [b_kmean])
        fence_p1 = t.snapshot()
        hb[3][0].fence = dict(fence_p1)
        hb[3][1].fence = dict(fence_p1)
        if stop == "p1":
            t.wait_all("sp")
            return nc, list(dbg_d.keys())

        b_wsl = [[Buf("wsl%d_%d" % (s, j), fence=fence_p1) for j in range(4)] for s in range(2)]
        w_ua_v = w_ua_d.rearrange("(c p) n -> p c n", p=128)
        w_up_v = w_up_d.rearrange("(c p) n -> p c n", p=128)

        def load_slices(fc):
            s = fc % 2
            t.dma("pool", wga[s], w_in_v[:, :, 2048 + fc * 128: 2048 + (fc + 1) * 128], writes=[b_wsl[s][0]])
            t.dma("pool", wgp[s], w_in_v[:, :, 3072 + fc * 128: 3072 + (fc + 1) * 128], writes=[b_wsl[s][1]])
            t.dma("pool", wua[s], w_ua_v[:, :, fc * 128:(fc + 1) * 128], writes=[b_wsl[s][2]])
            t.dma("pool", wup[s], w_up_v[:, :, fc * 128:(fc + 1) * 128], writes=[b_wsl[s][3]])

        load_slices(0)

        b_pooled = [Buf("pooled%d" % i, fence=fence_p0) for i in range(4)]
        b_yT = [Buf("yT%d" % i, fence=fence_p1) for i in range(4)]
        def p3_band(tt, bk):
            fns = []
            for g in range(4):
                kind = 2 if tt == 0 else 0
                fns.append(lambda e, g=g, kind=kind: e.matmul(
                    bank_f(bk)[:, g * 128:(g + 1) * 128], lhsT=u_sb[:, tt, g * 128:(g + 1) * 128],
                    rhs=band[:, g * 3 + kind, :], start=True, stop=(tt == 0)))
                if tt > 0:
                    fns.append(lambda e, g=g: e.matmul(
                        bank_f(bk)[:, g * 128:(g + 1) * 128], lhsT=u_sb[:, tt - 1, g * 128:(g + 1) * 128],
                        rhs=band[:, g * 3 + 1, :], start=False, stop=True))
            rd = [b_u[tt], b_c["band"]] + ([b_u[tt - 1]] if tt > 0 else [])
            t.group("pe", fns, reads=rd, writes=hb[bk])
            t.op("dve", lambda e: e.tensor_copy(
                out=pooledT[:, :, tt * 128:(tt + 1) * 128],
                in_=bank_f(bk).rearrange("p (g t) -> p g t", g=4)),
                reads=hb[bk], writes=[b_pooled[tt // 4]])

        def p3_y(tg, g, bk):
            t.group("pe", [lambda e: e.matmul(
                bank_f(bk), lhsT=wgrp[:, g, :], rhs=pooledT[:, g, tg * 512:(tg + 1) * 512],
                start=True, stop=True)],
                reads=[b_pooled[tg], b_c["wgrp"]], writes=hb[bk])
            t.op("dve", lambda e: e.tensor_scalar(
                out=yT[:, g, tg * 512:(tg + 1) * 512], in0=bank_f(bk), scalar1=psc_c[:, g:g + 1],
                scalar2=None, op0=ALU.mult),
                reads=hb[bk] + [b_c["psc"]], writes=[b_yT[tg]])


        NPT = 5
        b_PT = [Buf("PT%d" % i, fence=fence_p1) for i in range(NPT)]
        b_otok = [Buf("otok%d" % i, fence=fence_p1) for i in range(2)]
        b_rden = [Buf("rden%d" % i) for i in range(4)]
        b_oT = [Buf("oT%d" % i, fence=fence_p1) for i in range(NT)]
        st_banks = [0, 1, 2]
        MISC = 3
        accb = [[4, 5], [6, 7]]

        steps = []
        for qb in range(NB):
            for h in range(H):
                for kp in range(qb + 1):
                    steps.append((qb, h, kp))
        LOOK = 3
        pend_post = []

        def emit_qk(si):
            qb, h, kp = steps[si]
            pr = h // 2
            sb_ = st_banks[si % 3]
            q0 = qb * 256
            ksel = kT if h % 2 == 0 else kT2
            fns = []
            for j in range(2):
                kt = 2 * kp + j
                fns.append(lambda e, kt=kt, j=j: e.matmul(
                    bank_f(sb_)[:, j * 256:(j + 1) * 256], lhsT=ksel[:, pr, kt * 128:(kt + 1) * 128],
                    rhs=qTz[:, h, q0:q0 + 256], start=True, stop=True))
            rd = [b_kT[2 * kp], b_kT[2 * kp + 1], b_qT[2 * qb], b_qT[2 * qb + 1]] + ([b_kT2] if h % 2 else [])
            t.group("pe", fns, reads=rd, writes=hb[sb_])
            pt = PT[si % NPT]
            t.op("act", lambda e: e.activation(out=pt, in_=bank_f(sb_), func=AF.Exp, scale=0.125),
                 reads=hb[sb_], writes=[b_PT[si % NPT]])
            if kp == qb and (si < 80 or si in gate_busy):
                dg = pt.rearrange("p (a c) -> p a c", c=128)[:, 0:4:3, :]
                t.op("pool", lambda e: e.affine_select(
                    out=dg, in_=dg, pattern=[[0, 2], [1, 128]],
                    compare_op=ALU.is_ge, fill=0.0, base=0, channel_multiplier=-1),
                    reads=[b_PT[si % NPT]], writes=[b_PT[si % NPT]])
            elif kp == qb:
                dg = pt.rearrange("p (a c) -> p a c", c=128)[:, 0:4:3, :]
                t.op("dve", lambda e: e.tensor_tensor(
                    out=dg, in0=dg, in1=tri01.unsqueeze(1).to_broadcast([128, 2, 128]), op=ALU.mult),
                    reads=[b_PT[si % NPT], b_tri], writes=[b_PT[si % NPT]])

        def emit_pv(si):
            qb, h, kp = steps[si]
            half, hh = h // 4, h % 4
            pt = PT[si % NPT]
            for j in range(2):
                kt = 2 * kp + j
                for ql in range(2):
                    if ql == 0 and kt == 2 * qb + 1:
                        continue
                    last = (2 * qb) if ql == 0 else (2 * qb + 1)
                    bk = accb[ql][half]
                    dst = bank_f(bk)[:, hh * 65:(hh + 1) * 65]
                    c0_ = j * 256 + ql * 128
                    t.group("pe", [lambda e: e.matmul(
                        dst, lhsT=pt[:, c0_:c0_ + 128], rhs=v_aug[:, kt, h, :],
                        start=(kt == 0), stop=(kt == last))],
                        reads=[b_PT[si % NPT], b_v[kt]], writes=hb[bk])
            if kp == qb and hh == 3:
                for ql in range(2):
                    bk = accb[ql][half]
                    acc3 = bank_f(bk)[:, 0:260].rearrange("p (h d) -> p h d", h=4)
                    ri = ql * 2 + half
                    t.op("dve", lambda e, acc3=acc3, ri=ri: e.reciprocal(out=rden[ri], in_=acc3[:, :, 64]),
                         reads=hb[bk], writes=[b_rden[ri]])
                    ot3 = o_tok[ql][:, half * 256:(half + 1) * 256].rearrange("p (h d) -> p h d", h=4)
                    t.op("dve", lambda e, acc3=acc3, ri=ri, ot3=ot3: e.tensor_tensor(
                        out=ot3, in0=acc3[:, :, 0:64],
                        in1=rden[ri].unsqueeze(2).to_broadcast([128, 4, 64]), op=ALU.mult),
                        reads=hb[bk] + [b_rden[ri]], writes=[b_otok[ql]])
                if half == 1:
                    def post(ql, qb=qb):
                        if True:
                            tt = 2 * qb + ql
                            tps = bank_bf(MISC)[:, 0:512]
                            t.group("pe", [lambda e, pr=pr_, ql=ql: e.transpose(
                                out=tps[:, pr * 128:(pr + 1) * 128],
                                in_=o_tok[ql][:, pr * 128:(pr + 1) * 128], identity=ident_b)
                                for pr_ in range(4)],
                                reads=[b_otok[ql], b_c["ident_b"]], writes=hb[MISC])
                            t.op("dve", lambda e, tt=tt: e.tensor_copy(
                                out=o_attnT[:, :, tt * 128:(tt + 1) * 128],
                                in_=tps.rearrange("p (a b) -> p a b", a=4)),
                                reads=hb[MISC], writes=[b_oT[tt]])
                    pend_post.append((si + LOOK + 2, lambda post=post: post(0)))
                    pend_post.append((si + LOOK + 5, lambda post=post: post(1)))

        for b_ in (b_btok2[0], b_btok2[1], b_gate, b_m8, b_sel, b_rank):
            b_.fence = dict(fence_p1)
        for i in range(2):
            t.op("dve", lambda e, i=i: e.memset(bias8[i], 0.0), writes=[b_btok2[i]])
        gate_at = {}
        gate_busy = set()
        P3_START = 130
        misc_used = set()
        for i_, (qb_, h_, kp_) in enumerate(steps):
            if h_ == 0 and kp_ == 0 and i_ > 0:
                misc_used.update((i_ + LOOK + 1, i_ + LOOK + 4))

        def misc_free(x_):
            return all(abs(x_ - u_) > 2 for u_ in misc_used)

        s_ = 1
        tr_prev = []
        for k in range(8):
            if k == 4:
                s_ = max(s_, 82)
            while True:
                if misc_free(s_) and (k < 2 or s_ > tr_prev[k - 2]):
                    tr_ = next((s_ + d_ for d_ in range(10, 18) if misc_free(s_ + d_)), None)
                    if tr_ is not None:
                        break
                s_ += 1
            assert tr_ < (80 if k < 4 else P3_START - 2), (k, s_, tr_)
            gate_at.setdefault(s_, []).append(("sel", 8 + k))
            gate_at.setdefault(s_ + 3, []).append(("selb", 8 + k))
            gate_busy.update(range(s_ - 1, s_ + 7))
            gate_at.setdefault(tr_, []).append(("tr", 8 + k))
            misc_used.update((s_, tr_))
            tr_prev.append(tr_)
            s_ += 3
        p3_at = {}
        bounds_ = [len(steps)]
        for i_, (qb_, h_, kp_) in enumerate(steps):
            if h_ == 0 and kp_ == 0:
                bounds_.append(i_)
        slots_ = []
        s_ = P3_START + 2
        while len(slots_) < 32:
            if all(not (b_ + 2 <= s_ <= b_ + 10) for b_ in bounds_):
                slots_.append(s_)
                s_ += 4
            else:
                s_ += 1
        assert slots_[-1] < len(steps) - 4, slots_
        for k in range(NT):
            p3_at[slots_[k]] = ("band", k)
        for k in range(16):
            p3_at[slots_[NT + k]] = ("y", k // 4, k % 4)
        for si in range(len(steps) + LOOK):
            if si == P3_START:
                f80 = t.snapshot()
                for b_ in b_pooled + b_yT:
                    b_.fence = dict(f80)
            if si in (3, 6):
                post_b_pe(DEFER_PE[0] if si == 3 else DEFER_PE[1], on_dve=True)
            if si < len(steps):
                emit_qk(si)
            if si in p3_at:
                a_ = p3_at[si]
                if a_[0] == "band":
                    p3_band(a_[1], MISC)
                else:
                    p3_y(a_[1], a_[2], MISC)
            for kind, gt in gate_at.get(si, []):
                if kind == "sel":
                    gate_mm(gt)
                    gate_sel(gt)
                elif kind == "selb":
                    gate_sel_b(gt)
                else:
                    gate_tr(gt)
            if si - LOOK >= 0:
                emit_pv(si - LOOK)
            while pend_post and pend_post[0][0] <= si:
                pend_post.pop(0)[1]()
        while pend_post:
            pend_post.pop(0)[1]()
        dbg_dump("o_attnT", o_attnT, [128, 4, S], b_oT)
        dbg_dump("yT", yT, [128, 4, S], b_yT)
        fence_p2 = t.snapshot()
        if stop == "p2":
            t.wait_all("sp")
            return nc, list(dbg_d.keys())
        b_wout = [Buf("wout%d" % i, fence=fence_p2) for i in range(2)]
        w_out_v = w_out_d.rearrange("(c p) n -> p c n", p=128)
        for hf in range(2):
            t.dma("pool", wout[:, :, hf * 512:(hf + 1) * 512], w_out_v[:, :, hf * 512:(hf + 1) * 512],
                  writes=[b_wout[hf]])

        b_ga = [Buf("ga%d" % s, fence=fence_p2) for s in range(2)]
        b_gp = [Buf("gp%d" % s, fence=fence_p2) for s in range(2)]
        b_mT = [Buf("mT%d" % i, fence=fence_p2) for i in range(4)]
        for j in range(4):
            b_wsl[1][j].fence = dict(fence_p2)
        it = 0
        p4_mid = {}
        for fc in range(8):
            if fc == 3:
                p4_mid = {"dve": t.cnt["dve"]}
            if fc + 1 < 8:
                load_slices(fc + 1)
            s = fc % 2
            for tg in range(4):
                T0 = tg * 512
                pb = (it % 2) * 4
                i2 = it % 2
                it += 1
                t.group("pe", [lambda e, c=c: e.matmul(bank_f(pb + 0), lhsT=wga[s][:, c, :],
                                                        rhs=hT[:, c, T0:T0 + 512], start=(c == 0), stop=(c == 7))
                               for c in range(8)],
                        reads=[b_wsl[s][0]] + b_hT[tg * 4:(tg + 1) * 4], writes=hb[pb + 0])
                t.group("pe", [lambda e, c=c: e.matmul(bank_f(pb + 1), lhsT=wgp[s][:, c, :],
                                                        rhs=hT[:, c, T0:T0 + 512], start=(c == 0), stop=(c == 7))
                               for c in range(8)],
                        reads=[b_wsl[s][1]] + b_hT[tg * 4:(tg + 1) * 4], writes=hb[pb + 1])
                t.group("pe", [lambda e, c=c: e.matmul(bank_f(pb + 2), lhsT=wua[s][:, c, :],
                                                        rhs=o_attnT[:, c, T0:T0 + 512], start=(c == 0), stop=(c == 3))
                               for c in range(4)],
                        reads=[b_wsl[s][2]] + b_oT[tg * 4:(tg + 1) * 4], writes=hb[pb + 2])
                t.group("pe", [lambda e, c=c: e.matmul(bank_f(pb + 3), lhsT=wup[s][:, c, :],
                                                        rhs=yT[:, c, T0:T0 + 512], start=(c == 0), stop=(c == 3))
                               for c in range(4)],
                        reads=[b_wsl[s][3], b_yT[tg]], writes=hb[pb + 3])
                t.op("act", lambda e: e.activation(out=ga_sb[i2], in_=bank_f(pb + 0), func=AF.Sigmoid),
                     reads=hb[pb + 0], writes=[b_ga[i2]])
                t.op("act", lambda e: e.activation(out=gp_sb[i2], in_=bank_f(pb + 1), func=AF.Sigmoid),
                     reads=hb[pb + 1], writes=[b_gp[i2]])
                t.op("dve", lambda e: e.tensor_tensor(out=ga_sb[i2], in0=bank_f(pb + 2), in1=ga_sb[i2],
                                                      op=ALU.mult),
                     reads=hb[pb + 2] + [b_ga[i2]], writes=[b_ga[i2]])
                t.op("dve", lambda e: e.tensor_tensor(out=gp_sb[i2], in0=bank_f(pb + 3), in1=gp_sb[i2],
                                                      op=ALU.mult),
                     reads=hb[pb + 3] + [b_gp[i2]], writes=[b_gp[i2]])
                t.op("dve", lambda e: e.tensor_tensor(out=mT[:, fc, T0:T0 + 512], in0=ga_sb[i2],
                                                      in1=gp_sb[i2], op=ALU.add),
                     reads=[b_ga[i2], b_gp[i2]], writes=[b_mT[tg]])
        dbg_dump("mT", mT, [128, 8, S], b_mT)
        fence_p4 = t.snapshot()

        w_ff1_v = w_ff1_d.rearrange("(c p) n -> p c n", p=128)
        w_ff2_v = w_ff2_d.rearrange("(j p) n -> p j n", p=128)
        b_ffw1 = [[Buf("ffw1_%d_%d" % (s, i)) for i in range(2)] for s in range(2)]
        b_ffw2 = [[Buf("ffw2_%d_%d" % (s, i)) for i in range(2)] for s in range(2)]

        def load_ff(fq):
            s = fq % 2
            for hf in range(2):
                t.dma("pool", ffw1[s][:, :, hf * 512:(hf + 1) * 512],
                      w_ff1_v[:, :, fq * 1024 + hf * 512: fq * 1024 + (hf + 1) * 512],
                      writes=[b_ffw1[s][hf]])
            for hf in range(2):
                t.dma("pool", ffw2[s][:, :, hf * 512:(hf + 1) * 512],
                      w_ff2_v[:, fq * 8:(fq + 1) * 8, hf * 512:(hf + 1) * 512],
                      writes=[b_ffw2[s][hf]])

        for hf in range(2):
            b_ffw1[0][hf].fence = dict(fence_p4)
            b_ffw2[0][hf].fence = dict(fence_p4)
        load_ff(0)

        b_x1 = [Buf("x1_%d" % i, fence=(fence_p4 if i < 4 else fence_p2)) for i in range(NT)]
        b_xt5 = [Buf("xt5_%d" % i, fence=fence_p2) for i in range(2)]
        b_ss2 = Buf("ss2")
        b_r2 = Buf("r2sq")
        b_junk2 = Buf("junk2")
        b_ss2h = Buf("ss2h")
        b_h2T = [Buf("h2T%d" % i, fence=fence_p4) for i in range(4)]
        p5_banks = [0, 1, 2, 3]
        p5_i = 0

        p5_bk = {}

        def p5_mm(tt):
            nonlocal p5_i
            for hf in range(2):
                bk = p5_banks[p5_i % 4]
                p5_i += 1
                p5_bk[(tt, hf)] = bk
                t.group("pe", [lambda e, c=c, bk=bk, hf=hf: e.matmul(
                    bank_f(bk), lhsT=mT[:, c, tt * 128:(tt + 1) * 128],
                    rhs=wout[:, c, hf * 512:(hf + 1) * 512], start=(c == 0), stop=(c == 7))
                    for c in range(8)],
                    reads=[b_mT[tt // 4], b_wout[hf]], writes=hb[bk])

        def p5_add(tt):
            for hf in range(2):
                bk = p5_bk[(tt, hf)]
                dst = x1[:, tt, hf * 512:(hf + 1) * 512]
                t.op("dve", lambda e, bk=bk, dst=dst: e.tensor_tensor(
                    out=dst, in0=bank_f(bk), in1=dst, op=ALU.add),
                    reads=hb[bk] + [b_x1[tt]], writes=[b_x1[tt]])

        p6_banks = [4, 5, 6, 7]
        p6_i = 0

        b_gbc = Buf("gmlp_bc", fence=fence_p4)
        b_hb16 = [Buf("hb16_%d" % i, fence=fence_p4) for i in range(2)]

        p6_slot = {}

        def p6_mul(tt):
            nonlocal p6_i
            s2 = p6_i % 2
            p6_slot[tt] = (s2, p6_banks[p6_i % 4])
            p6_i += 1
            t.op("dve", lambda e: e.tensor_tensor(out=hb16[s2], in0=x1[:, tt, :], in1=gmlp_bc, op=ALU.mult),
                 reads=[b_x1[tt], b_gbc], writes=[b_hb16[s2]])

        def p6_tr(tt):
            s2, bk = p6_slot[tt]
            tps = bank_bf(bk)
            t.group("pe", [lambda e, c=c: e.transpose(
                out=tps[:, c * 128:(c + 1) * 128], in_=hb16[s2][:, c * 128:(c + 1) * 128],
                identity=ident_b) for c in range(8)],
                reads=[b_hb16[s2], b_c["ident_b"]], writes=hb[bk])
            t.op("act", lambda e: e.activation(
                out=hT[:, :, tt * 128:(tt + 1) * 128], in_=tps.rearrange("p (c t) -> p c t", c=8),
                func=AF.Copy),
                reads=hb[bk], writes=[b_h2T[tt // 4]])

        def p6_sq(tg, k):
            tt, hf = tg * 4 + k // 2, k % 2
            t.op("act", lambda e: e.activation(
                out=junk_h, in_=x1[:, tt, hf * 512:(hf + 1) * 512], func=AF.Square,
                accum_out=ss2h[:, hf * NT + tt: hf * NT + tt + 1]),
                reads=[b_x1[tt]], writes=[b_junk2, b_ss2h])

        def p6_stats(tg):
            c0, c1 = tg * 4, tg * 4 + 4
            t.op("dve", lambda e: e.tensor_tensor(out=ss2[:, c0:c1], in0=ss2h[:, c0:c1],
                                                  in1=ss2h[:, NT + c0:NT + c1], op=ALU.add),
                 reads=[b_ss2h], writes=[b_ss2])
            t.op("dve", lambda e: e.tensor_scalar(out=r2sq[:, c0:c1], in0=ss2[:, c0:c1], scalar1=1.0 / D,
                                                  scalar2=EPS, op0=ALU.mult, op1=ALU.add),
                 reads=[b_ss2], writes=[b_r2])
            t.op("dve", lambda e: e.reciprocal(out=r2sq[:, c0:c1], in_=r2sq[:, c0:c1]),
                 reads=[b_r2], writes=[b_r2])

        P5_ORDER = list(range(4, NT)) + list(range(4))
        t._wait("sp", p4_mid)
        for tt in P5_ORDER:
            if tt == 0:
                t.dma("sp", gmlp_bc, gmlp_vec_d, writes=[b_gbc])
            t.dma("sp", x1[:, tt, :], x_d[tt * 128:(tt + 1) * 128, :], writes=[b_x1[tt]])
        for k in range(NT + 1):
            if k < NT:
                p5_mm(P5_ORDER[k])
            if k >= 1:
                p6_mul(P5_ORDER[k - 1])
            if k < NT:
                p5_add(P5_ORDER[k])
            if k >= 1:
                p6_tr(P5_ORDER[k - 1])
        fence_p5 = t.snapshot()
        dbg_dump("x1", x1, [128, NT, D], b_x1)
        fence_p6 = t.snapshot()
        for hf in range(2):
            b_ffw1[1][hf].fence = dict(fence_p6)
            b_ffw2[1][hf].fence = dict(fence_p6)
        load_ff(1)

        b_a1T = [Buf("a1T%d" % s, fence=fence_p5) for s in range(2)]
        b_sqz = [Buf("sqz%d" % s, fence=fence_p5) for s in range(2)]
        f1_banks = [0, 1, 2, 3]
        f2_banks = [4, 5, 6, 7]
        f1_i = 0
        f2_i = 0
        sq_i = 0

        def ff1(fq, tg, ab):
            nonlocal f1_i, sq_i
            s = fq % 2
            for j in range(8):
                bk = f1_banks[f1_i % 4]
                f1_i += 1
                hf = j // 4
                t.group("pe", [lambda e, c=c, j=j, bk=bk: e.matmul(
                    bank_f(bk), lhsT=ffw1[s][:, c, j * 128:(j + 1) * 128],
                    rhs=hT[:, c, tg * 512:(tg + 1) * 512], start=(c == 0), stop=(c == 7))
                    for c in range(8)],
                    reads=[b_ffw1[s][hf], b_h2T[tg]], writes=hb[bk])
                si_ = sq_i % 2
                sq_i += 1
                t.op("act", lambda e, bk=bk, si_=si_: e.activation(out=sqz[si_], in_=bank_f(bk), func=AF.Square),
                     reads=hb[bk], writes=[b_sqz[si_]])
                t.op("dve", lambda e, bk=bk, si_=si_, j=j: e.scalar_tensor_tensor(
                    out=a1T[ab][:, j, :], in0=bank_f(bk), scalar=0.0, in1=sqz[si_],
                    op0=ALU.is_gt, op1=ALU.mult),
                    reads=hb[bk] + [b_sqz[si_]], writes=[b_a1T[ab]])
                if fq == 0:
                    p6_sq(tg, j)

        out_bufs = []

        def ff2(fq, tg, ab):
            nonlocal f2_i
            s = fq % 2
            for i in range(4):
                tt = tg * 4 + i
                for hf in range(2):
                    bk = f2_banks[f2_i % 4]
                    f2_i += 1
                    t.group("pe", [lambda e, j=j, bk=bk, hf=hf, i=i: e.matmul(
                        bank_f(bk), lhsT=a1T[ab][:, j, i * 128:(i + 1) * 128],
                        rhs=ffw2[s][:, j, hf * 512:(hf + 1) * 512], start=(j == 0), stop=(j == 7))
                        for j in range(8)],
                        reads=[b_a1T[ab], b_ffw2[s][hf]], writes=hb[bk])
                    dst = x1[:, tt, hf * 512:(hf + 1) * 512]
                    t.op("dve", lambda e, bk=bk, dst=dst, tt=tt: e.scalar_tensor_tensor(
                        out=dst, in0=bank_f(bk), scalar=r2sq[:, tt:tt + 1], in1=dst,
                        op0=ALU.mult, op1=ALU.add),
                        reads=hb[bk] + [b_r2, b_x1[tt]], writes=[b_x1[tt]])
                if fq == 3:
                    t.dma("sp", out_d[tt * 128:(tt + 1) * 128, :], x1[:, tt, :], reads=[b_x1[tt]])

        ab_i = 0
        pending = None
        for fq in range(4):
            if fq >= 1 and fq + 1 < 4:
                pass
            for tg in (1, 2, 3, 0):
                ab = ab_i % 2
                ab_i += 1
                ff1(fq, tg, ab)
                if fq == 0:
                    p6_stats(tg)
                if pending is not None:
                    ff2(*pending)
                    if pending[0] == fq - 1 and fq + 1 < 4:
                        load_ff(fq + 1)
                pending = (fq, tg, ab)
        ff2(*pending)

        t.wait_all("sp")
    return nc, list(dbg_d.keys())


def _host_consts():
    ident = np.eye(128, dtype=np.float32)
    half = 8
    inv = (500000.0 ** (-np.arange(half, dtype=np.float32) / half)).astype(np.float32)
    ang = np.arange(S, dtype=np.float32)[:, None] * inv[None, :]
    cos = np.cos(ang).astype(np.float32)
    sin = np.sin(ang).astype(np.float32)
    rope = np.concatenate([cos, cos, -sin, sin], axis=1).astype(np.float32)
    rope = np.ascontiguousarray(rope.reshape(NT, 128, 32).transpose(1, 0, 2).reshape(128, NT * 32))
    band = np.zeros((128, 12, 128), dtype=np.float32)
    tp = np.arange(128)[:, None]
    tq = np.arange(128)[None, :]
    for g, w in enumerate((2, 4, 8, 16)):
        inwin = (tp <= tq) & (tp > tq - w)
        main = np.where(inwin, 1.0 / w, 0.0) - np.eye(128)
        prev = np.where((tp - 128) > (tq - w), 1.0 / w, 0.0)
        cnt = np.minimum(tq + 1, w).astype(np.float64)
        main0 = np.where(inwin, 1.0 / cnt, 0.0) - np.eye(128)
        band[:, g * 3 + 0, :] = main
        band[:, g * 3 + 1, :] = prev
        band[:, g * 3 + 2, :] = main0
    return ident, rope, band


_CACHE = {}


def kernel(x, norm_mix, w_in, q_norm, k_norm, w_pool_grp, pool_scale, w_up_attn, w_up_pool,
           w_out, norm_mlp, w_ff1, w_ff2, _debug=(), _cores=N_CORES, _stop=None):
    f = lambda a: np.ascontiguousarray(np.asarray(a, dtype=np.float32))
    x = f(x)
    ident, rope, band = _host_consts()
    gq, gk = f(q_norm)[0], f(k_norm)[0]
    small = np.zeros((128, 24), dtype=np.float32)
    small[:, 0:8] = f(norm_mix)[0].reshape(8, 128).T
    small[:, 8:16] = f(norm_mlp)[0].reshape(8, 128).T
    small[:, 16:20] = f(pool_scale)[0].reshape(4, 128).T
    small[:, 20] = np.concatenate([gq, gq])
    small[:, 21] = np.concatenate([gk, gk])
    gv = np.concatenate([gq[0:16], gq[8:16], gq[0:8], gk[0:16], gk[8:16], gk[0:8]]).astype(np.float32)
    shared = {
        "w_in": f(w_in)[0], "w_pool_grp": f(w_pool_grp)[0], "w_up_attn": f(w_up_attn)[0],
        "w_up_pool": f(w_up_pool)[0], "w_out": f(w_out)[0], "w_ff1": f(w_ff1)[0], "w_ff2": f(w_ff2)[0],
        "c_small": small, "c_gv": gv, "gmlp_bc": np.ascontiguousarray(np.broadcast_to(f(norm_mlp)[0], (128, D))),
        "gmix_bc": np.ascontiguousarray(np.broadcast_to(f(norm_mix)[0], (128, D))),
        "c_ident": ident, "c_rope": rope, "c_band": band,
        "c_zero": np.zeros((64, 1024), dtype=np.float32),
        "c_ind": (np.arange(S)[None, :] // 256 == np.arange(8)[:, None]).astype(np.float32),
    }
    key = (tuple(_debug), _stop)
    if key not in _CACHE:
        _CACHE[key] = build_nc(debug=_debug, stop=_stop)
    nc, dbg_names = _CACHE[key]
    in_maps = []
    for b in range(_cores):
        m = dict(shared)
        m["x"] = x[b]
        in_maps.append(m)
    res = run_bass_kernel_spmd(nc, in_maps, core_ids=list(range(_cores)))
    out = np.stack([res.results[b]["out"] for b in range(_cores)], axis=0)
    if _debug:
        return out, {n: res.results[0]["dbg_" + n] for n in dbg_names}
    return out
```

```python
import numpy as np
from contextlib import ExitStack

import concourse.bass as bass
import concourse.mybir as mybir
from concourse.bass_utils import run_bass_kernel_spmd

F32 = mybir.dt.float32
BF16 = mybir.dt.bfloat16
AF = mybir.ActivationFunctionType
ALU = mybir.AluOpType
AX = mybir.AxisListType

S = 2048
D = 1024
NT = 16
H = 8
DH = 64
NB = 8
NEG = -240000.0
EPS = 1e-6
N_CORES = 8


class Buf:
    __slots__ = ("name", "w", "r", "fence", "excl")

    def __init__(self, name, fence=None, excl=False):
        self.name = name
        self.excl = excl
        self.w = None
        self.r = {}
        self.fence = dict(fence) if fence else None


class Trk:
    def __init__(self, nc, es, n_ring=12):
        self.nc = nc
        self.E = {"pe": nc.tensor, "act": nc.scalar, "dve": nc.vector,
                  "pool": nc.gpsimd, "sp": nc.sync}
        self.sems = {}
        for k in self.E:
            self.sems[k] = es.enter_context(nc.semaphore("s_" + k))
        self.cnt = {k: 0 for k in self.E}
        self.seen = {k: {} for k in self.E}
        self.rings = {}
        for q in ("sp", "pool"):
            slots = []
            for j in range(n_ring):
                key = "d_%s%d" % (q, j)
                self.sems[key] = es.enter_context(nc.semaphore(key))
                slots.append([key, 0])
            self.rings[q] = {"i": 0, "slots": slots}

    def snapshot(self):
        snap = {k: v for k, v in self.cnt.items() if v > 0}
        for q in self.rings.values():
            for key, tot in q["slots"]:
                if tot > 0:
                    snap[key] = tot
        return snap

    @staticmethod
    def _add(need, k, v):
        if need.get(k, 0) < v:
            need[k] = v

    def _need(self, reads, writes, e=None):
        need = {}
        for b in reads:
            if b.w:
                self._add(need, *b.w)
            if b.excl:
                for k, v in b.r.items():
                    if k != e:
                        self._add(need, k, v)
            if b.fence:
                for k, v in b.fence.items():
                    self._add(need, k, v)
        for b in writes:
            if b.w:
                self._add(need, *b.w)
            for k, v in b.r.items():
                self._add(need, k, v)
            if b.fence:
                for k, v in b.fence.items():
                    self._add(need, k, v)
        return need

    def _wait(self, e, need):
        eng = self.E[e]
        seen = self.seen[e]
        for k, v in need.items():
            if e == "pe" and k == "pe":
                continue
            if seen.get(k, 0) < v:
                eng.wait_ge(self.sems[k], v)
                seen[k] = v

    def _mark(self, key, val, reads, writes):
        for b in reads:
            if b.r.get(key, 0) < val:
                b.r[key] = val
        for b in writes:
            b.w = (key, val)
            b.r = {}

    def op(self, e, fn, reads=(), writes=()):
        self._wait(e, self._need(reads, writes, e))
        ins = fn(self.E[e])
        self.cnt[e] += 1
        ins.then_inc(self.sems[e], 1)
        self._mark(e, self.cnt[e], reads, writes)

    def group(self, e, fns, reads=(), writes=()):
        self._wait(e, self._need(reads, writes, e))
        ins = None
        for fn in fns:
            ins = fn(self.E[e])
        self.cnt[e] += 1
        ins.then_inc(self.sems[e], 1)
        self._mark(e, self.cnt[e], reads, writes)

    def dma(self, q, out, in_, reads=(), writes=()):
        need = self._need(reads, writes)
        ring = self.rings[q]
        slot = ring["slots"][ring["i"] % len(ring["slots"])]
        ring["i"] += 1
        if slot[1] > 0:
            self._add(need, slot[0], slot[1])
        self._wait(q, need)
        ins = self.E[q].dma_start(out=out, in_=in_)
        slot[1] += 16
        ins.then_inc(self.sems[slot[0]], 16)
        self._mark(slot[0], slot[1], reads, writes)

    def wait_all(self, e):
        self._wait(e, self.snapshot())

    def wait_written(self, e, buf):
        if buf.w:
            self._wait(e, {buf.w[0]: buf.w[1]})


def build_nc(debug=(), stop=None):
    nc = bass.Bass("TRN2", target_bir_lowering=False)

    def din(name, shape):
        return nc.dram_tensor(name, list(shape), F32, kind="ExternalInput").ap()

    x_d = din("x", [S, D])
    w_in_d = din("w_in", [D, 4096])
    w_grp_d = din("w_pool_grp", [4, 128, 128])
    w_ua_d = din("w_up_attn", [512, D])
    w_up_d = din("w_up_pool", [512, D])
    w_out_d = din("w_out", [D, D])
    w_ff1_d = din("w_ff1", [D, 4096])
    w_ff2_d = din("w_ff2", [4096, D])
    c_small_d = din("c_small", [128, 24])
    gmlp_vec_d = din("gmlp_bc", [128, D])
    gmix_vec_d = din("gmix_bc", [128, D])
    c_gv_d = din("c_gv", [64])
    c_ident_d = din("c_ident", [128, 128])
    c_rope_d = din("c_rope", [128, NT * 32])
    c_band_d = din("c_band", [128, 12, 128])
    c_zero_d = din("c_zero", [64, 1024])
    c_ind_d = din("c_ind", [8, S])
    out_d = nc.dram_tensor("out", [S, D], F32, kind="ExternalOutput").ap()
    dbg_d = {}

    es = ExitStack()
    with es:
        RAWB = 103 * 2048 + 256
        raw = nc.alloc_sbuf_tensor("raw", [128, RAWB // 2], BF16)

        def view(off, shape, dt, p0=0):
            esz = 2 if dt == BF16 else 4
            n = 1
            for s_ in shape[1:]:
                n *= s_
            assert off % 4 == 0 and off + n * esz <= RAWB, (off, shape)
            v = raw[p0:p0 + shape[0], off // 2: off // 2 + n * esz // 2]
            if dt != BF16:
                v = v.bitcast(dt)
            if len(shape) == 3:
                v = v.rearrange("p (a b) -> p a b", a=shape[1])
            elif len(shape) == 4:
                v = v.rearrange("p (a b c) -> p a b c", a=shape[1], b=shape[2])
            return v

        KB = 1024
        A0 = 0
        B0 = 7 * KB
        C0 = B0 + 32 * KB
        D0 = C0 + 64 * KB + 256
        E0 = D0 + 32 * KB
        F0 = E0 + 32 * KB
        G0 = F0 + 32 * KB

        ident_f = view(A0 + 0, [128, 128], F32)
        ident_b = view(A0 + 512, [128, 128], BF16)
        tri01 = view(A0 + 768, [128, 128], BF16)
        rope_t = view(A0 + 1024, [128, NT, 32], F32)
        small_c = view(A0 + 3072, [128, 24], F32)
        gmix_c = small_c[:, 0:8]
        gmlp_c = small_c[:, 8:16]
        psc_c = small_c[:, 16:20]
        gcolT_q = small_c[:, 20:21]
        gcolT_k = small_c[:, 21:22]
        gv_qk = view(A0 + 3200, [128, 64], F32)
        gv_q = gv_qk[:, 0:32]
        gv_k = gv_qk[:, 32:64]
        ones_b = view(A0 + 3664, [128, 2], BF16)
        band = view(A0 + 3680, [128, 12, 128], BF16)
        wgrp = view(G0 + 0, [128, 4, 128], BF16)
        ss1 = view(G0 + 1024, [128, NT], F32)
        rstd1 = view(G0 + 1088, [128, NT], F32)
        ss2 = view(G0 + 1152, [128, NT], F32)
        r2sq = view(G0 + 1216, [128, NT], F32)
        kmeanT = view(G0 + 1280, [128, 4, 8], BF16)
        ksum_sb = view(G0 + 1344, [128, 4, 8], F32)
        stat_a = [view(G0 + 5632 + i * 64, [128, 8], F32) for i in range(2)]
        stat_b = [view(G0 + 5760 + i * 64, [128, 8], F32) for i in range(2)]
        junk_h = view(G0 + 5888, [128, 512], BF16)
        ss2h = view(G0 + 6912, [128, 2 * NT], F32)

        hT = view(B0, [128, 8, S], BF16)
        qTz = view(C0, [128, H, S], BF16)
        kT = view(C0 + 32 * KB, [128, 4, S], BF16)
        v_aug = view(C0 + 48 * KB, [128, NT, H, 65], BF16)
        assert C0 + 48 * KB + 16640 <= D0
        x1 = view(C0, [128, NT, D], F32)
        _sb = [F0, C0]
        wga = [view(_sb[s], [128, 8, 128], BF16) for s in range(2)]
        wgp = [view(_sb[s] + 2 * KB, [128, 8, 128], BF16) for s in range(2)]
        wua = [view(_sb[s] + 4 * KB, [128, 4, 128], BF16) for s in range(2)]
        wup = [view(_sb[s] + 5 * KB, [128, 4, 128], BF16) for s in range(2)]
        ga_sb = [view(C0 + 6 * KB + s * 4 * KB, [128, 512], F32) for s in range(2)]
        gp_sb = [view(C0 + 6 * KB + s * 4 * KB + 2 * KB, [128, 512], F32) for s in range(2)]
        xslot = [view(D0 + i * 4 * KB, [128, D], F32) for i in range(8)]
        u_sb = view(D0, [128, NT, 512], BF16)
        pooledT = view(D0 + 16 * KB, [128, 4, S], BF16)
        mT = view(D0, [128, 8, S], BF16)
        a1T = [view(D0 + s * 8 * KB, [128, 8, 512], BF16) for s in range(2)]
        sqz = [view(D0 + 16 * KB + s * 2 * KB, [128, 512], F32) for s in range(2)]
        o_attnT = view(E0, [128, 4, S], BF16)
        junk = view(E0, [128, D], BF16)
        gmix_bc = view(E0 + 2 * KB, [128, D], F32)
        hb0 = [view(E0 + 6 * KB + i * 2 * KB, [128, D], BF16) for i in range(2)]
        E1 = E0 + 16 * KB
        sq_sb = [view(E1 + i * 2 * KB, [128, 512], F32) for i in range(2)]
        qtok = [view(E1 + 4 * KB + i * KB, [128, 512], BF16) for i in range(2)]
        x16 = [view(E1 + 6 * KB + i * 512, [128, 8, 16], F32) for i in range(2)]
        rt1 = [view(E1 + 7 * KB + i * 512, [128, 8, 16], F32) for i in range(2)]
        rt2 = [view(E1 + 8 * KB + i * 512, [128, 8, 16], F32) for i in range(2)]
        rope_q = view(E1 + 9 * KB, [128, NT, 32], F32)
        rope_k = view(E1 + 11 * KB, [128, NT, 32], F32)
        yT = view(E0 + 16 * KB, [128, 4, S], BF16)
        wqkvu = [view(F0 + i * 8 * KB, [128, 8, 512], BF16) for i in range(4)]
        wout = view(F0 + 8 * KB, [128, 8, D], BF16)
        kT2 = view(F0 + 8 * KB, [128, 4, S], BF16)
        xt5 = [view(F0 + 24 * KB + i * 4 * KB, [128, D], F32) for i in range(2)]
        gmlp_bc = view(F0, [128, D], F32)
        hb16 = [view(F0 + 4 * KB + i * 2 * KB, [128, D], BF16) for i in range(2)]
        f2 = F0 + 24 * KB
        PT = [view(f2 + i * 1024, [128, 512], BF16) for i in range(5)]
        o_tok = [view(f2 + 5120 + i * 1024, [128, 512], BF16) for i in range(2)]
        rden = [view(G0 + 7040 + i * 16, [128, 4], F32) for i in range(4)]
        assert f2 + 7168 <= F0 + 31 * KB
        gate_sb = view(F0 + 31 * KB, [128, 8, 8], F32)
        m8_sb = view(F0 + 31 * KB + 256, [128, 8, 8], F32)
        sel_sb = view(F0 + 31 * KB + 512, [128, 8, 8], F32)
        bias8 = [view(F0 + 6 * KB, [128, H, 128], BF16), view(D0 + 16 * KB, [128, H, 128], BF16)]
        rank_sb = view(D0 + 18 * KB, [128, H, 8, 8], F32)
        ffw1 = [view(E0, [128, 8, 1024], BF16), view(F0, [128, 8, 1024], BF16)]
        ffw2 = [view(E0 + 16 * KB, [128, 8, 1024], BF16), view(F0 + 16 * KB, [128, 8, 1024], BF16)]

        banks = [es.enter_context(nc.psum_tensor("bank%d" % i, [128, 512], F32)) for i in range(8)]
        hb = []
        for i in range(8):
            pb_ = Buf("psbank%d" % i, excl=True)
            hb.append([pb_, pb_])

        def bank_f(i):
            return banks[i][:, :]

        def bank_bf(i):
            return banks[i][:, :].bitcast(BF16)

        t = Trk(nc, es)

        def dbg_dump(name, ap, shape, bufs):
            if name not in debug:
                return
            d = nc.dram_tensor("dbg_" + name, list(shape), ap.dtype, kind="ExternalOutput").ap()
            dbg_d[name] = d
            t.dma("sp", d, ap, reads=bufs)

        b_wqkvu = [Buf("wqkvu%d" % i) for i in range(4)]
        w_in_v = w_in_d.rearrange("(c p) n -> p c n", p=128)
        CG_K, CG_Q, CG_V, CG_U = 1, 0, 2, 3
        b_c = {n: Buf(n) for n in ("ident_f", "ident_b", "rope", "gvq", "gvk", "gmix", "gmlp",
                                    "psc", "ones", "band", "wgrp", "gcq", "gck", "ropeq", "ropek",
                                    "qz")}
        t.dma("sp", ident_f, c_ident_d, writes=[b_c["ident_f"]])
        t.dma("sp", small_c, c_small_d, writes=[b_c["gmix"], b_c["gmlp"], b_c["psc"], b_c["gcq"], b_c["gck"]])
        t.dma("sp", gv_qk, c_gv_d.partition_broadcast(128), writes=[b_c["gvq"], b_c["gvk"]])
        t.dma("sp", rope_t, c_rope_d.rearrange("p (t n) -> p t n", t=NT), writes=[b_c["rope"]])
        t.dma("pool", wqkvu[CG_K], w_in_v[:, :, CG_K * 512:(CG_K + 1) * 512], writes=[b_wqkvu[CG_K]])
        t.dma("pool", ident_b, c_ident_d, writes=[b_c["ident_b"]])

        def setup_gains(gv, gc, rp, nm_gv, nm_gc, nm_rp):
            t.op("dve", lambda e: e.memset(gc[0:16, :], 1.0), writes=[b_c[nm_gc]])
            t.op("dve", lambda e: e.memset(gc[64:80, :], 1.0), writes=[b_c[nm_gc]])
            t.op("dve", lambda e: e.tensor_tensor(
                out=rp, in0=rope_t, in1=gv.unsqueeze(1).to_broadcast([128, NT, 32]), op=ALU.mult),
                reads=[b_c["rope"], b_c[nm_gv]], writes=[b_c[nm_rp]])

        b_tri = Buf("tri01")
        b_eps = Buf("eps")
        eps_c = small_c[:, 22:23]

        def small_setup_k():
            t.op("dve", lambda e: e.memset(eps_c, EPS), reads=[b_c["gmix"]], writes=[b_eps])
            t.op("pool", lambda e: e.memset(tri01, 1.0), writes=[b_tri])
            t.op("pool", lambda e: e.affine_select(
                out=tri01, in_=tri01, pattern=[[1, 128]], compare_op=ALU.is_ge, fill=0.0, base=0,
                channel_multiplier=-1), reads=[b_tri], writes=[b_tri])
            t.op("dve", lambda e: e.memset(ones_b, 1.0), writes=[b_c["ones"]])
            setup_gains(gv_k, gcolT_k, rope_k, "gvk", "gck", "ropek")

        def small_setup_rest():
            t.wait_written("pool", b_xslot[7])
            t.dma("pool", wqkvu[CG_Q], w_in_v[:, :, CG_Q * 512:(CG_Q + 1) * 512], writes=[b_wqkvu[CG_Q]])
            setup_gains(gv_q, gcolT_q, rope_q, "gvq", "gcq", "ropeq")
            t.dma("pool", wqkvu[CG_U], w_in_v[:, :, CG_U * 512:(CG_U + 1) * 512], writes=[b_wqkvu[CG_U]])
            t.dma("pool", band, c_band_d, writes=[b_c["band"]])
            t.dma("pool", wgrp, w_grp_d.rearrange("g c d -> c g d"), writes=[b_c["wgrp"]])

        def zero_fill():
            for h in range(H):
                base = 64 if h % 2 == 0 else 0
                t.dma("sp", qTz[base:base + 64, h, :].bitcast(F32), c_zero_d, writes=[b_c["qz"]])

        b_xslot = [Buf("xslot%d" % i) for i in range(8)]
        b_ss1 = Buf("ss1")
        b_rstd1 = Buf("rstd1")
        b_junk = Buf("junk")
        b_hT = [Buf("hT_%d" % i) for i in range(NT)]
        ev_cnt = [0]
        tp_cnt = [0]

        b_gmix_bc = Buf("gmix_bc")
        b_hb0 = [Buf("hb0_%d" % i) for i in range(2)]
        p0_bank = {}

        def p0_load1(tt):
            sl = tt % 8
            t.dma("sp", xslot[sl], x_d[tt * 128:(tt + 1) * 128, :], writes=[b_xslot[sl]])

        def p0_sq(tt):
            sl = tt % 8
            t.op("act", lambda e: e.activation(
                out=junk, in_=xslot[sl], func=AF.Square, accum_out=ss1[:, tt:tt + 1]),
                reads=[b_xslot[sl]], writes=[b_junk, b_ss1])

        def p0_stats(tt):
            c0, c1 = tt, tt + 1
            t.op("act", lambda e: e.activation(out=rstd1[:, c0:c1], in_=ss1[:, c0:c1], func=AF.Sqrt,
                                               scale=1.0 / D, bias=eps_c[:, 0:1]),
                 reads=[b_ss1, b_eps], writes=[b_rstd1])
            t.op("dve", lambda e: e.reciprocal(out=rstd1[:, c0:c1], in_=rstd1[:, c0:c1]),
                 reads=[b_rstd1], writes=[b_rstd1])

        def p0_mul(tt):
            sl = tt % 8
            s2 = tt % 2
            t.op("dve", lambda e: e.scalar_tensor_tensor(
                out=hb0[s2], in0=xslot[sl], scalar=rstd1[:, tt:tt + 1], in1=gmix_bc,
                op0=ALU.mult, op1=ALU.mult),
                reads=[b_xslot[sl], b_rstd1, b_gmix_bc], writes=[b_hb0[s2]])

        def p0_tr1(tt):
            s2 = tt % 2
            bk = tp_cnt[0] % 3
            tp_cnt[0] += 1
            tps = bank_bf(bk)
            t.group("pe", [lambda e, c=c: e.transpose(
                out=tps[:, c * 128:(c + 1) * 128], in_=hb0[s2][:, c * 128:(c + 1) * 128],
                identity=ident_b) for c in range(8)],
                reads=[b_hb0[s2], b_c["ident_b"]], writes=hb[bk])
            t.op("act", lambda e: e.activation(
                out=hT[:, :, tt * 128:(tt + 1) * 128], in_=tps.rearrange("p (c t) -> p c t", c=8),
                func=AF.Copy),
                reads=hb[bk], writes=[b_hT[tt]])

        b_sq = [Buf("sq%d" % i) for i in range(2)]
        b_x16 = [Buf("x16%d" % i) for i in range(2)]
        b_qtok = [Buf("qtok%d" % i) for i in range(2)]
        b_rt1 = [Buf("rt1%d" % i) for i in range(2)]
        b_rt2 = [Buf("rt2%d" % i) for i in range(2)]
        b_sta = [Buf("sta%d" % i) for i in range(2)]
        b_stb = [Buf("stb%d" % i) for i in range(2)]
        b_qT = [Buf("qT%d" % i) for i in range(NT)]
        b_kT = [Buf("kT%d" % i) for i in range(NT)]
        b_v = [Buf("v%d" % i) for i in range(NT)]
        b_vones = Buf("vones")
        b_kmean = Buf("kmeanT")
        b_gate = Buf("gate")
        b_m8 = Buf("m8")
        b_sel = Buf("sel")
        b_rank = Buf("rank")
        b_btok2 = [Buf("bias8_%d" % i) for i in range(2)]
        b_ksum_ps = hb[3][0]
        b_gate_ps = hb[3][0]
        b_bias_ps = hb[3][0]
        ksum_ps = bank_f(3)[:, 0:128].rearrange("p (a b c) -> p a b c", a=4, b=NT)
        gate_ps = bank_f(3)[:, 128:192].rearrange("p (h n) -> p h n", h=H)
        bias_ps = bank_bf(3)[:, 512:640]
        pj_banks = [4, 5, 6, 7]

        jobs = []
        for cga, cgb in ((CG_K, CG_V), (CG_Q, CG_U)):
            for tt in range(NT):
                jobs.append((cga, tt))
                jobs.append((cgb, tt))
        qk_slot = {}
        for ji, (cg, tt) in enumerate(jobs):
            if cg in (CG_K, CG_Q):
                qk_slot[ji] = len(qk_slot) % 2

        def proj_mm(ji):
            cg, tt = jobs[ji]
            bk = pj_banks[ji % 4]
            fns = []
            for c in range(8):
                fns.append(lambda e, c=c: e.matmul(
                    bank_f(bk), lhsT=hT[:, c, tt * 128:(tt + 1) * 128], rhs=wqkvu[cg][:, c, :],
                    start=(c == 0), stop=(c == 7)))
            t.group("pe", fns, reads=[b_hT[tt], b_wqkvu[cg]], writes=hb[bk])

        def post_a2(ji):
            cg, tt = jobs[ji]
            if cg in (CG_K, CG_Q):
                i2 = qk_slot[ji]
                sq3 = sq_sb[i2].rearrange("p (h d) -> p h d", h=H)
                t.op("dve", lambda e: e.tensor_reduce(out=stat_a[i2], in_=sq3, axis=AX.X, op=ALU.add),
                     reads=[b_sq[i2]], writes=[b_sta[i2]])
                t.op("act", lambda e: e.activation(out=stat_b[i2], in_=stat_a[i2], func=AF.Sqrt,
                                                   scale=1.0 / DH, bias=eps_c[:, 0:1]),
                     reads=[b_sta[i2], b_eps], writes=[b_stb[i2]])

        def post_a1(ji):
            cg, tt = jobs[ji]
            bk = pj_banks[ji % 4]
            if cg in (CG_K, CG_Q):
                i2 = qk_slot[ji]
                t.op("act", lambda e: e.activation(out=sq_sb[i2], in_=bank_f(bk), func=AF.Square),
                     reads=hb[bk], writes=[b_sq[i2]])
            elif cg == CG_V:
                t.op("dve", lambda e: e.tensor_copy(
                    out=v_aug[:, tt, :, 0:64], in_=bank_f(bk).rearrange("p (h d) -> p h d", h=H)),
                    reads=hb[bk] + [b_vones], writes=[b_v[tt]])
            else:
                t.op("act", lambda e: e.activation(out=u_sb[:, tt, :], in_=bank_f(bk), func=AF.Copy),
                     reads=hb[bk], writes=[b_u[tt]])

        def post_b(ji):
            cg, tt = jobs[ji]
            if cg not in (CG_K, CG_Q):
                return
            is_k = cg == CG_K
            bk = pj_banks[ji % 4]
            i2 = qk_slot[ji]
            rp = rope_k if is_k else rope_q
            rb = b_c["ropek"] if is_k else b_c["ropeq"]
            ps3 = bank_f(bk).rearrange("p (h d) -> p h d", h=H)
            qt3 = qtok[i2].rearrange("p (h d) -> p h d", h=H)
            t.op("dve", lambda e: e.reciprocal(out=stat_b[i2], in_=stat_b[i2]),
                 reads=[b_stb[i2]], writes=[b_stb[i2]])
            t.op("dve", lambda e: e.tensor_tensor(
                out=qt3, in0=ps3, in1=stat_b[i2].unsqueeze(2).to_broadcast([128, H, DH]), op=ALU.mult),
                reads=hb[bk] + [b_stb[i2]], writes=[b_qtok[i2]])
            t.op("dve", lambda e: e.tensor_tensor(
                out=x16[i2], in0=ps3[:, :, 0:16], in1=stat_b[i2].unsqueeze(2).to_broadcast([128, H, 16]),
                op=ALU.mult),
                reads=hb[bk] + [b_stb[i2]], writes=[b_x16[i2]])
            cs_b = rp[:, tt, 0:16].unsqueeze(1).to_broadcast([128, H, 16])
            sn_lo = rp[:, tt, 16:24].unsqueeze(1).to_broadcast([128, H, 8])
            sn_hi = rp[:, tt, 24:32].unsqueeze(1).to_broadcast([128, H, 8])
            reng = "pool" if is_k else "dve"
            t.op(reng, lambda e: e.tensor_tensor(out=rt1[i2], in0=x16[i2], in1=cs_b, op=ALU.mult),
                 reads=[b_x16[i2], rb], writes=[b_rt1[i2]])
            t.op(reng, lambda e: e.tensor_tensor(out=rt2[i2][:, :, 0:8], in0=x16[i2][:, :, 8:16], in1=sn_lo,
                                                  op=ALU.mult),
                 reads=[b_x16[i2], rb], writes=[b_rt2[i2]])
            t.op(reng, lambda e: e.tensor_tensor(out=rt2[i2][:, :, 8:16], in0=x16[i2][:, :, 0:8], in1=sn_hi,
                                                  op=ALU.mult),
                 reads=[b_x16[i2], rb, b_rt2[i2]], writes=[b_rt2[i2]])
            t.op(reng, lambda e: e.tensor_tensor(out=qt3[:, :, 0:16], in0=rt1[i2], in1=rt2[i2], op=ALU.add),
                 reads=[b_rt1[i2], b_rt2[i2], b_qtok[i2]], writes=[b_qtok[i2]])

        def post_b_pe(ji):
            cg, tt = jobs[ji]
            if cg not in (CG_K, CG_Q):
                return
            is_k = cg == CG_K
            i2 = qk_slot[ji]
            tbk = tp_cnt[0] % 3
            tp_cnt[0] += 1
            tps = bank_bf(tbk)[:, 0:512]
            fns = []
            for pr in range(4):
                fns.append(lambda e, pr=pr: e.transpose(
                    out=tps[:, pr * 128:(pr + 1) * 128], in_=qtok[i2][:, pr * 128:(pr + 1) * 128],
                    identity=ident_b))
            wr = [hb[tbk][0]]
            if is_k:
                for pr in range(4):
                    fns.append(lambda e, pr=pr: e.matmul(
                        ksum_ps[:, pr, tt, :], lhsT=qtok[i2][:, pr * 128:(pr + 1) * 128], rhs=ones_b,
                        start=True, stop=True))
                wr.append(b_ksum_ps)
            t.group("pe", fns, reads=[b_qtok[i2], b_c["ident_b"], b_c["ones"]], writes=wr)
            tps3 = tps.rearrange("p (a b) -> p a b", a=4)
            if is_k:
                t.op("act", lambda e: e.activation(out=kT[:, :, tt * 128:(tt + 1) * 128], in_=tps3,
                                                   func=AF.Copy, scale=gcolT_k[:, 0:1]),
                     reads=[hb[tbk][0], b_c["gck"]], writes=[b_kT[tt]])
            else:
                t.op("act", lambda e: e.activation(
                    out=qTz[0:64, 0:H:2, tt * 128:(tt + 1) * 128], in_=tps3[0:64],
                    func=AF.Copy, scale=gcolT_q[0:64, 0:1]),
                    reads=[hb[tbk][0], b_c["gcq"], b_c["qz"]], writes=[b_qT[tt]])
                t.op("act", lambda e: e.activation(
                    out=qTz[64:128, 1:H:2, tt * 128:(tt + 1) * 128], in_=tps3[64:128],
                    func=AF.Copy, scale=gcolT_q[64:128, 0:1]),
                    reads=[hb[tbk][0], b_c["gcq"], b_c["qz"], b_qT[tt]], writes=[b_qT[tt]])

        def gate_mm(tt):
            fns = []
            for h in range(H):
                fns.append(lambda e, h=h: e.matmul(
                    gate_ps[:, h, :], lhsT=qTz[:, h, tt * 128:(tt + 1) * 128], rhs=kmeanT[:, h // 2, :],
                    start=True, stop=True))
            t.group("pe", fns, reads=[b_qT[tt], b_kmean], writes=[b_gate_ps])

        def gate_sel(tt):
            qb = tt // 2
            t.op("dve", lambda e: e.memset(gate_sb, -1.0e30), writes=[b_gate])
            t.op("dve", lambda e: e.tensor_copy(out=gate_sb[:, :, 0:qb], in_=gate_ps[:, :, 0:qb]),
                 reads=[b_gate_ps, b_gate], writes=[b_gate])
            g_m = gate_sb.unsqueeze(2).to_broadcast([128, H, 8, 8])
            g_n = gate_sb.unsqueeze(3).to_broadcast([128, H, 8, 8])
            t.op("dve", lambda e: e.tensor_tensor(out=rank_sb, in0=g_m, in1=g_n, op=ALU.is_gt),
                 reads=[b_gate, b_rank], writes=[b_rank])

        def gate_sel_b(tt):
            qb = tt // 2
            t.op("dve", lambda e: e.tensor_reduce(out=m8_sb, in_=rank_sb, axis=AX.X, op=ALU.add),
                 reads=[b_rank, b_m8], writes=[b_m8])
            t.op("dve", lambda e: e.tensor_scalar(out=sel_sb, in0=m8_sb, scalar1=2.5, scalar2=None,
                                                  op0=ALU.is_lt),
                 reads=[b_m8, b_sel], writes=[b_sel])
            b8, b_b8 = bias8[tt % 2], b_btok2[tt % 2]
            t.op("dve", lambda e: e.tensor_scalar(
                out=b8[:, 0:H:2, 64:64 + qb], in0=sel_sb[:, 0:H:2, 0:qb], scalar1=-1.0, scalar2=-NEG,
                op0=ALU.add, op1=ALU.mult),
                reads=[b_sel, b_b8], writes=[b_b8])
            t.op("dve", lambda e: e.tensor_scalar(
                out=b8[:, 1:H:2, 0:qb], in0=sel_sb[:, 1:H:2, 0:qb], scalar1=-1.0, scalar2=-NEG,
                op0=ALU.add, op1=ALU.mult),
                reads=[b_sel, b_b8], writes=[b_b8])

        def gate_tr(tt):
            b8, b_b8 = bias8[tt % 2], b_btok2[tt % 2]
            tbk = 3
            tps = bank_bf(tbk)
            t.group("pe", [lambda e, h=h: e.transpose(out=tps[:, h * 128:(h + 1) * 128], in_=b8[:, h, :],
                                                      identity=ident_b) for h in range(H)],
                    reads=[b_b8, b_c["ident_b"]], writes=[hb[tbk][0]])
            tp3 = tps.rearrange("p (h t) -> p h t", h=H)
            t.op("dve", lambda e: e.tensor_copy(out=qTz[64:72, 0:H:2, tt * 128:(tt + 1) * 128],
                                                in_=tp3[64:72, 0:H:2, :]),
                 reads=[hb[tbk][0], b_c["qz"], b_qT[tt]], writes=[b_qT[tt]])
            t.op("dve", lambda e: e.tensor_copy(out=qTz[0:8, 1:H:2, tt * 128:(tt + 1) * 128],
                                                in_=tp3[0:8, 1:H:2, :]),
                 reads=[hb[tbk][0], b_c["qz"], b_qT[tt]], writes=[b_qT[tt]])

        def kmean_fin():
            ks3 = bank_f(3)[:, 0:128].rearrange("p (a n c) -> p a n c", a=4, n=NB)
            ks3 = ks3[:, :, :, 0:4:2]
            t.op("dve", lambda e: e.tensor_reduce(out=ksum_sb, in_=ks3, axis=AX.X, op=ALU.add),
                 reads=[b_ksum_ps], writes=[b_kmean])
            t.op("dve", lambda e: e.tensor_scalar(out=kmeanT, in0=ksum_sb, scalar1=gcolT_k[:, 0:1],
                                                  scalar2=1.0 / 256.0, op0=ALU.mult, op1=ALU.mult),
                 reads=[b_kmean, b_c["gck"]], writes=[b_kmean])

        b_kT2 = Buf("kT2")

        def make_kT2():
            b_kT2.fence = t.snapshot()
            t.dma("sp", kT2, kT, reads=b_kT, writes=[b_kT2])
            for pr in range(4):
                t.dma("pool", kT[64:72, pr, :], c_ind_d, writes=b_kT)
                t.dma("pool", kT2[0:8, pr, :], c_ind_d, writes=[b_kT2])

        later = []

        def run_later(it):
            k = 0
            while k < len(later):
                if later[k][0] <= it:
                    later.pop(k)[1]()
                else:
                    k += 1

        p0_sched = {}

        def p0_at(it_, fn, *a):
            p0_sched.setdefault(max(it_, -1), []).append((fn, a))

        for tt_ in range(2, NT):
            p0_at(2 * tt_ - 6, p0_sq, tt_)
            p0_at(2 * tt_ - 5, p0_stats, tt_)
            p0_at(2 * tt_ - 4, p0_mul, tt_)
            p0_at(2 * tt_ - 2, p0_tr1, tt_)
        for tt_ in range(8, NT):
            p0_at(max(0, 2 * (tt_ - 8) - 1), p0_load1, tt_)

        p0_load1(0)
        t.dma("sp", gmix_bc, gmix_vec_d, writes=[b_gmix_bc])
        for tt_ in range(1, 4):
            p0_load1(tt_)
        t.wait_written("pool", b_xslot[3])
        t.dma("pool", wqkvu[CG_V], w_in_v[:, :, CG_V * 512:(CG_V + 1) * 512], writes=[b_wqkvu[CG_V]])
        t.op("dve", lambda e: e.memset(v_aug[:, :, :, 64:65], 1.0), writes=[b_vones])
        small_setup_k()
        for tt_ in range(4, 8):
            p0_load1(tt_)
        for tt_ in range(2):
            p0_sq(tt_)
            p0_stats(tt_)
            p0_mul(tt_)
            p0_tr1(tt_)
        small_setup_rest()
        for fn_, a_ in p0_sched.get(-1, []):
            fn_(*a_)
        fence_p0 = None
        b_u = None
        NJ = len(jobs)
        DEFER_PE = [NJ - 4, NJ - 2]
        for it in range(NJ + 6):
            if stop and stop.startswith("it") and it >= int(stop[2:]):
                t.wait_all("sp")
                return nc, list(dbg_d.keys())
            if it == 9:
                zero_fill()
            for fn_, a_ in p0_sched.get(it, []):
                fn_(*a_)
            if it < NJ:
                if it == 2 * NT:
                    fence_p0 = t.snapshot()
                    b_u = [Buf("u%d" % i, fence=fence_p0) for i in range(NT)]

                proj_mm(it)
            if 0 <= it - 1 < NJ:
                post_a1(it - 1)
            if 0 <= it - 2 < NJ:
                post_a2(it - 2)
            if 0 <= it - 3 < NJ:
                post_b(it - 3)
            if 0 <= it - 5 < NJ:
                jb = it - 5
                if jb not in DEFER_PE:
                    post_b_pe(jb)
                cg, tt = jobs[jb]
                if cg == CG_K and tt == NT - 1:
                    later.append((it + 2, kmean_fin))
                if cg == CG_V and tt == NT - 1:
                    later.append((it + 2, make_kT2))
            run_later(it)
        run_later(10 ** 9)
        dbg_dump("hT", hT, [128, 8, S], b_hT)
        dbg_dump("qTz", qTz, [128, H, S], b_qT)
        dbg_dump("kT", kT, [128, 4, S], b_kT)
        dbg_dump("v", v_aug, [128, NT, H, 65], b_v)
        dbg_dump("u", u_sb, [128, NT, 512], b_u)
        dbg_dump("kmeanT", kmeanT, [128, 4, 8], [b_kmean])
        fence_p1 = t.snapshot()
        hb[3][0].fence = dict(fence_p1)
        hb[3][1].fence = dict(fence_p1)
        if stop == "p1":
            t.wait_all("sp")
            return nc, list(dbg_d.keys())

        b_wsl = [[Buf("wsl%d_%d" % (s, j), fence=fence_p1) for j in range(4)] for s in range(2)]
        w_ua_v = w_ua_d.rearrange("(c p) n -> p c n", p=128)
        w_up_v = w_up_d.rearrange("(c p) n -> p c n", p=128)

        def load_slices(fc):
            s = fc % 2
            t.dma("pool", wga[s], w_in_v[:, :, 2048 + fc * 128: 2048 + (fc + 1) * 128], writes=[b_wsl[s][0]])
            t.dma("pool", wgp[s], w_in_v[:, :, 3072 + fc * 128: 3072 + (fc + 1) * 128], writes=[b_wsl[s][1]])
            t.dma("pool", wua[s], w_ua_v[:, :, fc * 128:(fc + 1) * 128], writes=[b_wsl[s][2]])
            t.dma("pool", wup[s], w_up_v[:, :, fc * 128:(fc + 1) * 128], writes=[b_wsl[s][3]])

        load_slices(0)

        b_pooled = [Buf("pooled%d" % i, fence=fence_p0) for i in range(4)]
        b_yT = [Buf("yT%d" % i, fence=fence_p1) for i in range(4)]
        def p3_band(tt, bk):
            fns = []
            for g in range(4):
                kind = 2 if tt == 0 else 0
                fns.append(lambda e, g=g, kind=kind: e.matmul(
                    bank_f(bk)[:, g * 128:(g + 1) * 128], lhsT=u_sb[:, tt, g * 128:(g + 1) * 128],
                    rhs=band[:, g * 3 + kind, :], start=True, stop=(tt == 0)))
                if tt > 0:
                    fns.append(lambda e, g=g: e.matmul(
                        bank_f(bk)[:, g * 128:(g + 1) * 128], lhsT=u_sb[:, tt - 1, g * 128:(g + 1) * 128],
                        rhs=band[:, g * 3 + 1, :], start=False, stop=True))
            rd = [b_u[tt], b_c["band"]] + ([b_u[tt - 1]] if tt > 0 else [])
            t.group("pe", fns, reads=rd, writes=hb[bk])
            t.op("dve", lambda e: e.tensor_copy(
                out=pooledT[:, :, tt * 128:(tt + 1) * 128],
                in_=bank_f(bk).rearrange("p (g t) -> p g t", g=4)),
                reads=hb[bk], writes=[b_pooled[tt // 4]])

        def p3_y(tg, g, bk):
            t.group("pe", [lambda e: e.matmul(
                bank_f(bk), lhsT=wgrp[:, g, :], rhs=pooledT[:, g, tg * 512:(tg + 1) * 512],
                start=True, stop=True)],
                reads=[b_pooled[tg], b_c["wgrp"]], writes=hb[bk])
            t.op("dve", lambda e: e.tensor_scalar(
                out=yT[:, g, tg * 512:(tg + 1) * 512], in0=bank_f(bk), scalar1=psc_c[:, g:g + 1],
                scalar2=None, op0=ALU.mult),
                reads=hb[bk] + [b_c["psc"]], writes=[b_yT[tg]])


        NPT = 5
        b_PT = [Buf("PT%d" % i, fence=fence_p1) for i in range(NPT)]
        b_otok = [Buf("otok%d" % i, fence=fence_p1) for i in range(2)]
        b_rden = [Buf("rden%d" % i) for i in range(4)]
        b_oT = [Buf("oT%d" % i, fence=fence_p1) for i in range(NT)]
        st_banks = [0, 1, 2]
        MISC = 3
        accb = [[4, 5], [6, 7]]

        steps = []
        for qb in range(NB):
            for h in range(H):
                for kp in range(qb + 1):
                    steps.append((qb, h, kp))
        LOOK = 4
        pend_post = []

        def emit_qk(si):
            qb, h, kp = steps[si]
            pr = h // 2
            sb_ = st_banks[si % 3]
            q0 = qb * 256
            ksel = kT if h % 2 == 0 else kT2
            fns = []
            for j in range(2):
                kt = 2 * kp + j
                fns.append(lambda e, kt=kt, j=j: e.matmul(
                    bank_f(sb_)[:, j * 256:(j + 1) * 256], lhsT=ksel[:, pr, kt * 128:(kt + 1) * 128],
                    rhs=qTz[:, h, q0:q0 + 256], start=True, stop=True))
            rd = [b_kT[2 * kp], b_kT[2 * kp + 1], b_qT[2 * qb], b_qT[2 * qb + 1]] + ([b_kT2] if h % 2 else [])
            t.group("pe", fns, reads=rd, writes=hb[sb_])
            pt = PT[si % NPT]
            t.op("act", lambda e: e.activation(out=pt, in_=bank_f(sb_), func=AF.Exp, scale=0.125),
                 reads=hb[sb_], writes=[b_PT[si % NPT]])
            if kp == qb and (si < 80 or si in gate_busy):
                dg = pt.rearrange("p (a c) -> p a c", c=128)[:, 0:4:3, :]
                t.op("pool", lambda e: e.affine_select(
                    out=dg, in_=dg, pattern=[[0, 2], [1, 128]],
                    compare_op=ALU.is_ge, fill=0.0, base=0, channel_multiplier=-1),
                    reads=[b_PT[si % NPT]], writes=[b_PT[si % NPT]])
            elif kp == qb:
                dg = pt.rearrange("p (a c) -> p a c", c=128)[:, 0:4:3, :]
                t.op("dve", lambda e: e.tensor_tensor(
                    out=dg, in0=dg, in1=tri01.unsqueeze(1).to_broadcast([128, 2, 128]), op=ALU.mult),
                    reads=[b_PT[si % NPT], b_tri], writes=[b_PT[si % NPT]])

        def emit_pv(si):
            qb, h, kp = steps[si]
            half, hh = h // 4, h % 4
            pt = PT[si % NPT]
            for j in range(2):
                kt = 2 * kp + j
                for ql in range(2):
                    if ql == 0 and kt == 2 * qb + 1:
                        continue
                    last = (2 * qb) if ql == 0 else (2 * qb + 1)
                    bk = accb[ql][half]
                    dst = bank_f(bk)[:, hh * 65:(hh + 1) * 65]
                    c0_ = j * 256 + ql * 128
                    t.group("pe", [lambda e: e.matmul(
                        dst, lhsT=pt[:, c0_:c0_ + 128], rhs=v_aug[:, kt, h, :],
                        start=(kt == 0), stop=(kt == last))],
                        reads=[b_PT[si % NPT], b_v[kt]], writes=hb[bk])
            if kp == qb and hh == 3:
                for ql in range(2):
                    bk = accb[ql][half]
                    acc3 = bank_f(bk)[:, 0:260].rearrange("p (h d) -> p h d", h=4)
                    ri = ql * 2 + half
                    t.op("dve", lambda e, acc3=acc3, ri=ri: e.reciprocal(out=rden[ri], in_=acc3[:, :, 64]),
                         reads=hb[bk], writes=[b_rden[ri]])
                    ot3 = o_tok[ql][:, half * 256:(half + 1) * 256].rearrange("p (h d) -> p h d", h=4)
                    t.op("dve", lambda e, acc3=acc3, ri=ri, ot3=ot3: e.tensor_tensor(
                        out=ot3, in0=acc3[:, :, 0:64],
                        in1=rden[ri].unsqueeze(2).to_broadcast([128, 4, 64]), op=ALU.mult),
                        reads=hb[bk] + [b_rden[ri]], writes=[b_otok[ql]])
                if half == 1:
                    def post(ql, qb=qb):
                        if True:
                            tt = 2 * qb + ql
                            tps = bank_bf(MISC)[:, 0:512]
                            t.group("pe", [lambda e, pr=pr_, ql=ql: e.transpose(
                                out=tps[:, pr * 128:(pr + 1) * 128],
                                in_=o_tok[ql][:, pr * 128:(pr + 1) * 128], identity=ident_b)
                                for pr_ in range(4)],
                                reads=[b_otok[ql], b_c["ident_b"]], writes=hb[MISC])
                            t.op("dve", lambda e, tt=tt: e.tensor_copy(
                                out=o_attnT[:, :, tt * 128:(tt + 1) * 128],
                                in_=tps.rearrange("p (a b) -> p a b", a=4)),
                                reads=hb[MISC], writes=[b_oT[tt]])
                    pend_post.append((si + LOOK + 2, lambda post=post: post(0)))
                    pend_post.append((si + LOOK + 5, lambda post=post: post(1)))

        for b_ in (b_btok2[0], b_btok2[1], b_gate, b_m8, b_sel, b_rank):
            b_.fence = dict(fence_p1)
        for i in range(2):
            t.op("dve", lambda e, i=i: e.memset(bias8[i], 0.0), writes=[b_btok2[i]])
        gate_at = {}
        gate_busy = set()
        P3_START = 130
        misc_used = set()
        for i_, (qb_, h_, kp_) in enumerate(steps):
            if h_ == 0 and kp_ == 0 and i_ > 0:
                misc_used.update((i_ + LOOK + 1, i_ + LOOK + 4))

        def misc_free(x_):
            return all(abs(x_ - u_) > 2 for u_ in misc_used)

        s_ = 1
        tr_prev = []
        for k in range(8):
            if k == 4:
                s_ = max(s_, 82)
            while True:
                if misc_free(s_) and (k < 2 or s_ > tr_prev[k - 2]):
                    tr_ = next((s_ + d_ for d_ in range(10, 18) if misc_free(s_ + d_)), None)
                    if tr_ is not None:
                        break
                s_ += 1
            assert tr_ < (80 if k < 4 else P3_START - 2), (k, s_, tr_)
            gate_at.setdefault(s_, []).append(("sel", 8 + k))
            gate_at.setdefault(s_ + 3, []).append(("selb", 8 + k))
            gate_busy.update(range(s_ - 1, s_ + 7))
            gate_at.setdefault(tr_, []).append(("tr", 8 + k))
            misc_used.update((s_, tr_))
            tr_prev.append(tr_)
            s_ += 3
        p3_at = {}
        bounds_ = [len(steps)]
        for i_, (qb_, h_, kp_) in enumerate(steps):
            if h_ == 0 and kp_ == 0:
                bounds_.append(i_)
        slots_ = []
        s_ = P3_START + 2
        while len(slots_) < 32:
            if all(not (b_ + 2 <= s_ <= b_ + 10) for b_ in bounds_):
                slots_.append(s_)
                s_ += 4
            else:
                s_ += 1
        assert slots_[-1] < len(steps) - 4, slots_
        for k in range(NT):
            p3_at[slots_[k]] = ("band", k)
        for k in range(16):
            p3_at[slots_[NT + k]] = ("y", k // 4, k % 4)
        for si in range(len(steps) + LOOK):
            if si == P3_START:
                f80 = t.snapshot()
                for b_ in b_pooled + b_yT:
                    b_.fence = dict(f80)
            if si in (3, 6):
                post_b_pe(DEFER_PE[0] if si == 3 else DEFER_PE[1])
            if si < len(steps):
                emit_qk(si)
            if si in p3_at:
                a_ = p3_at[si]
                if a_[0] == "band":
                    p3_band(a_[1], MISC)
                else:
                    p3_y(a_[1], a_[2], MISC)
            for kind, gt in gate_at.get(si, []):
                if kind == "sel":
                    gate_mm(gt)
                    gate_sel(gt)
                elif kind == "selb":
                    gate_sel_b(gt)
                else:
                    gate_tr(gt)
            if si - LOOK >= 0:
                emit_pv(si - LOOK)
            while pend_post and pend_post[0][0] <= si:
                pend_post.pop(0)[1]()
        while pend_post:
            pend_post.pop(0)[1]()
        dbg_dump("o_attnT", o_attnT, [128, 4, S], b_oT)
        dbg_dump("yT", yT, [128, 4, S], b_yT)
        fence_p2 = t.snapshot()
        if stop == "p2":
            t.wait_all("sp")
            return nc, list(dbg_d.keys())
        b_wout = [Buf("wout%d" % i, fence=fence_p2) for i in range(2)]
        w_out_v = w_out_d.rearrange("(c p) n -> p c n", p=128)
        for hf in range(2):
            t.dma("pool", wout[:, :, hf * 512:(hf + 1) * 512], w_out_v[:, :, hf * 512:(hf + 1) * 512],
                  writes=[b_wout[hf]])

        b_ga = [Buf("ga%d" % s, fence=fence_p2) for s in range(2)]
        b_gp = [Buf("gp%d" % s, fence=fence_p2) for s in range(2)]
        b_mT = [Buf("mT%d" % i, fence=fence_p2) for i in range(4)]
        for j in range(4):
            b_wsl[1][j].fence = dict(fence_p2)
        it = 0
        p4_mid = {}
        for fc in range(8):
            if fc == 3:
                p4_mid = {"dve": t.cnt["dve"]}
            if fc + 1 < 8:
                load_slices(fc + 1)
            s = fc % 2
            for tg in range(4):
                T0 = tg * 512
                pb = (it % 2) * 4
                i2 = it % 2
                it += 1
                t.group("pe", [lambda e, c=c: e.matmul(bank_f(pb + 0), lhsT=wga[s][:, c, :],
                                                        rhs=hT[:, c, T0:T0 + 512], start=(c == 0), stop=(c == 7))
                               for c in range(8)],
                        reads=[b_wsl[s][0]] + b_hT[tg * 4:(tg + 1) * 4], writes=hb[pb + 0])
                t.group("pe", [lambda e, c=c: e.matmul(bank_f(pb + 1), lhsT=wgp[s][:, c, :],
                                                        rhs=hT[:, c, T0:T0 + 512], start=(c == 0), stop=(c == 7))
                               for c in range(8)],
                        reads=[b_wsl[s][1]] + b_hT[tg * 4:(tg + 1) * 4], writes=hb[pb + 1])
                t.group("pe", [lambda e, c=c: e.matmul(bank_f(pb + 2), lhsT=wua[s][:, c, :],
                                                        rhs=o_attnT[:, c, T0:T0 + 512], start=(c == 0), stop=(c == 3))
                               for c in range(4)],
                        reads=[b_wsl[s][2]] + b_oT[tg * 4:(tg + 1) * 4], writes=hb[pb + 2])
                t.group("pe", [lambda e, c=c: e.matmul(bank_f(pb + 3), lhsT=wup[s][:, c, :],
                                                        rhs=yT[:, c, T0:T0 + 512], start=(c == 0), stop=(c == 3))
                               for c in range(4)],
                        reads=[b_wsl[s][3], b_yT[tg]], writes=hb[pb + 3])
                t.op("act", lambda e: e.activation(out=ga_sb[i2], in_=bank_f(pb + 0), func=AF.Sigmoid),
                     reads=hb[pb + 0], writes=[b_ga[i2]])
                t.op("act", lambda e: e.activation(out=gp_sb[i2], in_=bank_f(pb + 1), func=AF.Sigmoid),
                     reads=hb[pb + 1], writes=[b_gp[i2]])
                t.op("dve", lambda e: e.tensor_tensor(out=ga_sb[i2], in0=bank_f(pb + 2), in1=ga_sb[i2],
                                                      op=ALU.mult),
                     reads=hb[pb + 2] + [b_ga[i2]], writes=[b_ga[i2]])
                t.op("dve", lambda e: e.tensor_tensor(out=gp_sb[i2], in0=bank_f(pb + 3), in1=gp_sb[i2],
                                                      op=ALU.mult),
                     reads=hb[pb + 3] + [b_gp[i2]], writes=[b_gp[i2]])
                t.op("dve", lambda e: e.tensor_tensor(out=mT[:, fc, T0:T0 + 512], in0=ga_sb[i2],
                                                      in1=gp_sb[i2], op=ALU.add),
                     reads=[b_ga[i2], b_gp[i2]], writes=[b_mT[tg]])
        dbg_dump("mT", mT, [128, 8, S], b_mT)
        fence_p4 = t.snapshot()

        w_ff1_v = w_ff1_d.rearrange("(c p) n -> p c n", p=128)
        w_ff2_v = w_ff2_d.rearrange("(j p) n -> p j n", p=128)
        b_ffw1 = [[Buf("ffw1_%d_%d" % (s, i)) for i in range(2)] for s in range(2)]
        b_ffw2 = [[Buf("ffw2_%d_%d" % (s, i)) for i in range(2)] for s in range(2)]

        def load_ff(fq):
            s = fq % 2
            for hf in range(2):
                t.dma("pool", ffw1[s][:, :, hf * 512:(hf + 1) * 512],
                      w_ff1_v[:, :, fq * 1024 + hf * 512: fq * 1024 + (hf + 1) * 512],
                      writes=[b_ffw1[s][hf]])
            for hf in range(2):
                t.dma("pool", ffw2[s][:, :, hf * 512:(hf + 1) * 512],
                      w_ff2_v[:, fq * 8:(fq + 1) * 8, hf * 512:(hf + 1) * 512],
                      writes=[b_ffw2[s][hf]])

        for hf in range(2):
            b_ffw1[0][hf].fence = dict(fence_p4)
            b_ffw2[0][hf].fence = dict(fence_p4)
        load_ff(0)

        b_x1 = [Buf("x1_%d" % i, fence=(fence_p4 if i < 4 else fence_p2)) for i in range(NT)]
        b_xt5 = [Buf("xt5_%d" % i, fence=fence_p2) for i in range(2)]
        b_ss2 = Buf("ss2")
        b_r2 = Buf("r2sq")
        b_junk2 = Buf("junk2")
        b_ss2h = Buf("ss2h")
        b_h2T = [Buf("h2T%d" % i, fence=fence_p4) for i in range(4)]
        p5_banks = [0, 1, 2, 3]
        p5_i = 0

        p5_bk = {}

        def p5_mm(tt):
            nonlocal p5_i
            for hf in range(2):
                bk = p5_banks[p5_i % 4]
                p5_i += 1
                p5_bk[(tt, hf)] = bk
                t.group("pe", [lambda e, c=c, bk=bk, hf=hf: e.matmul(
                    bank_f(bk), lhsT=mT[:, c, tt * 128:(tt + 1) * 128],
                    rhs=wout[:, c, hf * 512:(hf + 1) * 512], start=(c == 0), stop=(c == 7))
                    for c in range(8)],
                    reads=[b_mT[tt // 4], b_wout[hf]], writes=hb[bk])

        def p5_add(tt):
            for hf in range(2):
                bk = p5_bk[(tt, hf)]
                dst = x1[:, tt, hf * 512:(hf + 1) * 512]
                t.op("dve", lambda e, bk=bk, dst=dst: e.tensor_tensor(
                    out=dst, in0=bank_f(bk), in1=dst, op=ALU.add),
                    reads=hb[bk] + [b_x1[tt]], writes=[b_x1[tt]])

        p6_banks = [4, 5, 6, 7]
        p6_i = 0

        b_gbc = Buf("gmlp_bc", fence=fence_p4)
        b_hb16 = [Buf("hb16_%d" % i, fence=fence_p4) for i in range(2)]

        p6_slot = {}

        def p6_mul(tt):
            nonlocal p6_i
            s2 = p6_i % 2
            p6_slot[tt] = (s2, p6_banks[p6_i % 4])
            p6_i += 1
            t.op("dve", lambda e: e.tensor_tensor(out=hb16[s2], in0=x1[:, tt, :], in1=gmlp_bc, op=ALU.mult),
                 reads=[b_x1[tt], b_gbc], writes=[b_hb16[s2]])

        def p6_tr(tt):
            s2, bk = p6_slot[tt]
            tps = bank_bf(bk)
            t.group("pe", [lambda e, c=c: e.transpose(
                out=tps[:, c * 128:(c + 1) * 128], in_=hb16[s2][:, c * 128:(c + 1) * 128],
                identity=ident_b) for c in range(8)],
                reads=[b_hb16[s2], b_c["ident_b"]], writes=hb[bk])
            t.op("act", lambda e: e.activation(
                out=hT[:, :, tt * 128:(tt + 1) * 128], in_=tps.rearrange("p (c t) -> p c t", c=8),
                func=AF.Copy),
                reads=hb[bk], writes=[b_h2T[tt // 4]])

        def p6_sq(tg, k):
            tt, hf = tg * 4 + k // 2, k % 2
            t.op("act", lambda e: e.activation(
                out=junk_h, in_=x1[:, tt, hf * 512:(hf + 1) * 512], func=AF.Square,
                accum_out=ss2h[:, hf * NT + tt: hf * NT + tt + 1]),
                reads=[b_x1[tt]], writes=[b_junk2, b_ss2h])

        def p6_stats(tg):
            c0, c1 = tg * 4, tg * 4 + 4
            t.op("dve", lambda e: e.tensor_tensor(out=ss2[:, c0:c1], in0=ss2h[:, c0:c1],
                                                  in1=ss2h[:, NT + c0:NT + c1], op=ALU.add),
                 reads=[b_ss2h], writes=[b_ss2])
            t.op("dve", lambda e: e.tensor_scalar(out=r2sq[:, c0:c1], in0=ss2[:, c0:c1], scalar1=1.0 / D,
                                                  scalar2=EPS, op0=ALU.mult, op1=ALU.add),
                 reads=[b_ss2], writes=[b_r2])
            t.op("dve", lambda e: e.reciprocal(out=r2sq[:, c0:c1], in_=r2sq[:, c0:c1]),
                 reads=[b_r2], writes=[b_r2])

        P5_ORDER = list(range(4, NT)) + list(range(4))
        t._wait("sp", p4_mid)
        for tt in P5_ORDER:
            if tt == 0:
                t.dma("sp", gmlp_bc, gmlp_vec_d, writes=[b_gbc])
            t.dma("sp", x1[:, tt, :], x_d[tt * 128:(tt + 1) * 128, :], writes=[b_x1[tt]])
        for k in range(NT + 1):
            if k < NT:
                p5_mm(P5_ORDER[k])
            if k >= 1:
                p6_mul(P5_ORDER[k - 1])
            if k < NT:
                p5_add(P5_ORDER[k])
            if k >= 1:
                p6_tr(P5_ORDER[k - 1])
        fence_p5 = t.snapshot()
        dbg_dump("x1", x1, [128, NT, D], b_x1)
        fence_p6 = t.snapshot()
        for hf in range(2):
            b_ffw1[1][hf].fence = dict(fence_p6)
            b_ffw2[1][hf].fence = dict(fence_p6)
        load_ff(1)

        b_a1T = [Buf("a1T%d" % s, fence=fence_p5) for s in range(2)]
        b_sqz = [Buf("sqz%d" % s, fence=fence_p5) for s in range(2)]
        f1_banks = [0, 1, 2, 3]
        f2_banks = [4, 5, 6, 7]
        f1_i = 0
        f2_i = 0
        sq_i = 0

        def ff1(fq, tg, ab):
            nonlocal f1_i, sq_i
            s = fq % 2
            for j in range(8):
                bk = f1_banks[f1_i % 4]
                f1_i += 1
                hf = j // 4
                t.group("pe", [lambda e, c=c, j=j, bk=bk: e.matmul(
                    bank_f(bk), lhsT=ffw1[s][:, c, j * 128:(j + 1) * 128],
                    rhs=hT[:, c, tg * 512:(tg + 1) * 512], start=(c == 0), stop=(c == 7))
                    for c in range(8)],
                    reads=[b_ffw1[s][hf], b_h2T[tg]], writes=hb[bk])
                si_ = sq_i % 2
                sq_i += 1
                t.op("act", lambda e, bk=bk, si_=si_: e.activation(out=sqz[si_], in_=bank_f(bk), func=AF.Square),
                     reads=hb[bk], writes=[b_sqz[si_]])
                t.op("dve", lambda e, bk=bk, si_=si_, j=j: e.scalar_tensor_tensor(
                    out=a1T[ab][:, j, :], in0=bank_f(bk), scalar=0.0, in1=sqz[si_],
                    op0=ALU.is_gt, op1=ALU.mult),
                    reads=hb[bk] + [b_sqz[si_]], writes=[b_a1T[ab]])
                if fq == 0:
                    p6_sq(tg, j)

        out_bufs = []

        def ff2(fq, tg, ab):
            nonlocal f2_i
            s = fq % 2
            for i in range(4):
                tt = tg * 4 + i
                for hf in range(2):
                    bk = f2_banks[f2_i % 4]
                    f2_i += 1
                    t.group("pe", [lambda e, j=j, bk=bk, hf=hf, i=i: e.matmul(
                        bank_f(bk), lhsT=a1T[ab][:, j, i * 128:(i + 1) * 128],
                        rhs=ffw2[s][:, j, hf * 512:(hf + 1) * 512], start=(j == 0), stop=(j == 7))
                        for j in range(8)],
                        reads=[b_a1T[ab], b_ffw2[s][hf]], writes=hb[bk])
                    dst = x1[:, tt, hf * 512:(hf + 1) * 512]
                    t.op("dve", lambda e, bk=bk, dst=dst, tt=tt: e.scalar_tensor_tensor(
                        out=dst, in0=bank_f(bk), scalar=r2sq[:, tt:tt + 1], in1=dst,
                        op0=ALU.mult, op1=ALU.add),
                        reads=hb[bk] + [b_r2, b_x1[tt]], writes=[b_x1[tt]])
                if fq == 3:
                    t.dma("sp", out_d[tt * 128:(tt + 1) * 128, :], x1[:, tt, :], reads=[b_x1[tt]])

        ab_i = 0
        pending = None
        for fq in range(4):
            if fq >= 1 and fq + 1 < 4:
                pass
            for tg in (1, 2, 3, 0):
                ab = ab_i % 2
                ab_i += 1
                ff1(fq, tg, ab)
                if fq == 0:
                    p6_stats(tg)
                if pending is not None:
                    ff2(*pending)
                pending = (fq, tg, ab)
            if fq + 2 < 4:
                ff2(*pending)
                pending = None
                load_ff(fq + 2)
        ff2(*pending)

        t.wait_all("sp")
    return nc, list(dbg_d.keys())


def _host_consts():
    ident = np.eye(128, dtype=np.float32)
    half = 8
    inv = (500000.0 ** (-np.arange(half, dtype=np.float32) / half)).astype(np.float32)
    ang = np.arange(S, dtype=np.float32)[:, None] * inv[None, :]
    cos = np.cos(ang).astype(np.float32)
    sin = np.sin(ang).astype(np.float32)
    rope = np.concatenate([cos, cos, -sin, sin], axis=1).astype(np.float32)
    rope = np.ascontiguousarray(rope.reshape(NT, 128, 32).transpose(1, 0, 2).reshape(128, NT * 32))
    band = np.zeros((128, 12, 128), dtype=np.float32)
    tp = np.arange(128)[:, None]
    tq = np.arange(128)[None, :]
    for g, w in enumerate((2, 4, 8, 16)):
        inwin = (tp <= tq) & (tp > tq - w)
        main = np.where(inwin, 1.0 / w, 0.0) - np.eye(128)
        prev = np.where((tp - 128) > (tq - w), 1.0 / w, 0.0)
        cnt = np.minimum(tq + 1, w).astype(np.float64)
        main0 = np.where(inwin, 1.0 / cnt, 0.0) - np.eye(128)
        band[:, g * 3 + 0, :] = main
        band[:, g * 3 + 1, :] = prev
        band[:, g * 3 + 2, :] = main0
    return ident, rope, band


_CACHE = {}


def kernel(x, norm_mix, w_in, q_norm, k_norm, w_pool_grp, pool_scale, w_up_attn, w_up_pool,
           w_out, norm_mlp, w_ff1, w_ff2, _debug=(), _cores=N_CORES, _stop=None):
    f = lambda a: np.ascontiguousarray(np.asarray(a, dtype=np.float32))
    x = f(x)
    ident, rope, band = _host_consts()
    gq, gk = f(q_norm)[0], f(k_norm)[0]
    small = np.zeros((128, 24), dtype=np.float32)
    small[:, 0:8] = f(norm_mix)[0].reshape(8, 128).T
    small[:, 8:16] = f(norm_mlp)[0].reshape(8, 128).T
    small[:, 16:20] = f(pool_scale)[0].reshape(4, 128).T
    small[:, 20] = np.concatenate([gq, gq])
    small[:, 21] = np.concatenate([gk, gk])
    gv = np.concatenate([gq[0:16], gq[8:16], gq[0:8], gk[0:16], gk[8:16], gk[0:8]]).astype(np.float32)
    shared = {
        "w_in": f(w_in)[0], "w_pool_grp": f(w_pool_grp)[0], "w_up_attn": f(w_up_attn)[0],
        "w_up_pool": f(w_up_pool)[0], "w_out": f(w_out)[0], "w_ff1": f(w_ff1)[0], "w_ff2": f(w_ff2)[0],
        "c_small": small, "c_gv": gv, "gmlp_bc": np.ascontiguousarray(np.broadcast_to(f(norm_mlp)[0], (128, D))),
        "gmix_bc": np.ascontiguousarray(np.broadcast_to(f(norm_mix)[0], (128, D))),
        "c_ident": ident, "c_rope": rope, "c_band": band,
        "c_zero": np.zeros((64, 1024), dtype=np.float32),
        "c_ind": (np.arange(S)[None, :] // 256 == np.arange(8)[:, None]).astype(np.float32),
    }
    key = (tuple(_debug), _stop)
    if key not in _CACHE:
        _CACHE[key] = build_nc(debug=_debug, stop=_stop)
    nc, dbg_names = _CACHE[key]
    in_maps = []
    for b in range(_cores):
        m = dict(shared)
        m["x"] = x[b]
        in_maps.append(m)
    res = run_bass_kernel_spmd(nc, in_maps, core_ids=list(range(_cores)))
    out = np.stack([res.results[b]["out"] for b in range(_cores)], axis=0)
    if _debug:
        return out, {n: res.results[0]["dbg_" + n] for n in dbg_names}
    return out
```
